# Optimizing a Trainium2 kernel written in Bass

```python
import math
import jax, jax.numpy as jnp
from jax import lax
import numpy as np

D_MODEL = 1024
BATCH = 8
SEQ = 2048
DEPTH = 1

N_ATTN_HEADS = 8
ATTN_HEAD_DIM = 64
ATTN_V_DIM = 2 * ATTN_HEAD_DIM
ATTN_QK_WIDTH = N_ATTN_HEADS * 2 * ATTN_HEAD_DIM
ATTN_WIDTH = N_ATTN_HEADS * ATTN_V_DIM
Q_BLOCK = 128
CONV_WIDTH = D_MODEL
CONV_KERNEL = 3
PEER_HEADS = 8
PEER_N_KEYS = 128
PEER_N_EXPERTS = PEER_N_KEYS * PEER_N_KEYS
PEER_KEY_DIM = 256
PEER_HALF_DIM = PEER_KEY_DIM // 2
PEER_TOPK = 16
TOKEN_CHUNK = 128
RMS_EPS = 1e-6

IN_WIDTHS = (ATTN_QK_WIDTH, ATTN_QK_WIDTH, ATTN_WIDTH,
             CONV_WIDTH, CONV_WIDTH, CONV_WIDTH, D_MODEL, D_MODEL)
IN_WIDTH = sum(IN_WIDTHS)
IN_SPLITS = tuple(int(v) for v in np.cumsum(IN_WIDTHS)[:-1])

kernel_name = "hybrid_diffattn_shortconv_peer"


def rmsnorm(x, g):
    xf = x.astype(jnp.float32)
    r = lax.rsqrt(jnp.mean(xf * xf, axis=-1, keepdims=True) + RMS_EPS)
    return (xf * r).astype(x.dtype) * g


def diff_attention(q, k, v, lam):
    seq = q.shape[1]
    scale = ATTN_HEAD_DIM ** -0.5
    outs = []
    for blk in range(seq // Q_BLOCK):
        start = blk * Q_BLOCK
        end = start + Q_BLOCK
        qb, kb, vb = q[:, start:end], k[:, :end], v[:, :end]
        s = jnp.einsum('bqhmd,bkhmd->bhmqk', qb, kb).astype(jnp.float32) * scale
        q_pos = start + jnp.arange(Q_BLOCK)
        k_pos = jnp.arange(end)
        s = jnp.where(k_pos[None, :] <= q_pos[:, None], s, -jnp.inf)
        p = jax.nn.softmax(s, axis=-1)
        a = p[:, :, 0] - lam * p[:, :, 1]
        outs.append(jnp.einsum('bhqk,bkhe->bqhe', a.astype(vb.dtype), vb))
    return jnp.concatenate(outs, axis=1)


def causal_depthwise_conv(u, w):
    return lax.conv_general_dilated(
        u, w[:, None, :].astype(u.dtype), window_strides=(1,),
        padding=[(CONV_KERNEL - 1, 0)],
        dimension_numbers=('NWC', 'WIO', 'NWC'),
        feature_group_count=u.shape[-1])


def peer_ffn(xn, w_query, sub_keys, expert_u, expert_v):
    b, s, d = xn.shape
    t = b * s
    xf = xn.reshape(t, d)
    q = (xf @ w_query).reshape(t, PEER_HEADS, 2, PEER_HALF_DIM)
    half_scores = jnp.einsum('thmd,hmnd->thmn', q, sub_keys)
    top_s, top_i = lax.top_k(half_scores, PEER_TOPK)
    cand_s = (top_s[:, :, 0, :, None] + top_s[:, :, 1, None, :]).reshape(t, PEER_HEADS, PEER_TOPK * PEER_TOPK)
    cand_i = (top_i[:, :, 0, :, None] * PEER_N_KEYS + top_i[:, :, 1, None, :]).reshape(t, PEER_HEADS, PEER_TOPK * PEER_TOPK)
    sel_s, pos = lax.top_k(cand_s, PEER_TOPK)
    sel_i = jnp.take_along_axis(cand_i, pos, axis=-1)
    gates = jax.nn.softmax(sel_s.astype(jnp.float32), axis=-1).astype(xn.dtype)
    hk = PEER_HEADS * PEER_TOPK
    n_chunks = t // TOKEN_CHUNK

    def chunk_fn(args):
        xc, ic, gc = args
        u = expert_u[ic]
        a = jnp.einsum('cd,ced->ce', xc, u)
        hact = jax.nn.gelu(a, approximate=False) * gc
        return jnp.einsum('ce,ced->cd', hact, expert_v[ic])

    y = lax.map(chunk_fn, (xf.reshape(n_chunks, TOKEN_CHUNK, d),
                           sel_i.reshape(n_chunks, TOKEN_CHUNK, hk),
                           gates.reshape(n_chunks, TOKEN_CHUNK, hk)))
    return y.reshape(b, s, d)


def setup_inputs(seed: int = 0) -> dict:
    key = jax.random.key(seed)
    ks = jax.random.split(key, 16)
    f32 = jnp.float32

    def nrm(k, shape, scale):
        return jax.random.normal(k, shape, f32) * scale

    def gain(k, shape):
        return 1.0 + 0.01 * jax.random.normal(k, shape, f32)

    return {
        "x": nrm(ks[0], (BATCH, SEQ, D_MODEL), 1.0),
        "norm1_g": gain(ks[1], (DEPTH, D_MODEL)),
        "w_in": nrm(ks[2], (DEPTH, D_MODEL, IN_WIDTH), D_MODEL ** -0.5),
        "lambda_qk": nrm(ks[3], (DEPTH, 4, ATTN_HEAD_DIM), 0.1),
        "subln_g": gain(ks[4], (DEPTH, ATTN_V_DIM)),
        "conv_w": nrm(ks[5], (DEPTH, CONV_KERNEL, CONV_WIDTH), CONV_KERNEL ** -0.5),
        "w_attn_o": nrm(ks[6], (DEPTH, ATTN_WIDTH, D_MODEL), ATTN_WIDTH ** -0.5),
        "w_conv_o": nrm(ks[7], (DEPTH, CONV_WIDTH, D_MODEL), CONV_WIDTH ** -0.5),
        "w_out": nrm(ks[8], (DEPTH, D_MODEL, D_MODEL), D_MODEL ** -0.5),
        "norm2_g": gain(ks[9], (DEPTH, D_MODEL)),
        "w_query": nrm(ks[10], (DEPTH, D_MODEL, PEER_HEADS * PEER_KEY_DIM), D_MODEL ** -0.5),
        "sub_keys": nrm(ks[11], (DEPTH, PEER_HEADS, 2, PEER_N_KEYS, PEER_HALF_DIM), PEER_HALF_DIM ** -0.5),
        "expert_u": nrm(ks[12], (DEPTH, PEER_N_EXPERTS, D_MODEL), D_MODEL ** -0.5),
        "expert_v": nrm(ks[13], (DEPTH, PEER_N_EXPERTS, D_MODEL), D_MODEL ** -0.5),
        "final_g": gain(ks[14], (D_MODEL,)),
    }


def reference(x, norm1_g, w_in, lambda_qk, subln_g, conv_w, w_attn_o, w_conv_o,
              w_out, norm2_g, w_query, sub_keys, expert_u, expert_v, final_g):
    b, s, _ = x.shape
    h = x
    for l in range(DEPTH):
        lam_init = 0.8 - 0.6 * math.exp(-0.3 * l)
        lq = lambda_qk[l].astype(jnp.float32)
        lam = jnp.exp(jnp.sum(lq[0] * lq[1])) - jnp.exp(jnp.sum(lq[2] * lq[3])) + lam_init

        n = rmsnorm(h, norm1_g[l])
        proj = n @ w_in[l]
        q, k, v, cb, cc, cx, g_attn, g_conv = jnp.split(proj, IN_SPLITS, axis=-1)

        q = q.reshape(b, s, N_ATTN_HEADS, 2, ATTN_HEAD_DIM)
        k = k.reshape(b, s, N_ATTN_HEADS, 2, ATTN_HEAD_DIM)
        v = v.reshape(b, s, N_ATTN_HEADS, ATTN_V_DIM)
        o = diff_attention(q, k, v, lam)
        o = rmsnorm(o, subln_g[l]) * (1.0 - lam_init)
        y_attn = o.reshape(b, s, ATTN_WIDTH) @ w_attn_o[l]

        y_conv = (cb * causal_depthwise_conv(cc * cx, conv_w[l])) @ w_conv_o[l]

        merged = jax.nn.sigmoid(g_attn) * y_attn + jax.nn.sigmoid(g_conv) * y_conv
        h = h + merged @ w_out[l]

        h = h + peer_ffn(rmsnorm(h, norm2_g[l]), w_query[l], sub_keys[l], expert_u[l], expert_v[l])
    return rmsnorm(h, final_g)
```

```python
import os
from contextlib import ExitStack

import numpy as np
import concourse.bass as bass
import concourse.mybir as mybir
from concourse.bass_utils import run_bass_kernel_spmd

F32 = mybir.dt.float32
BF16 = mybir.dt.bfloat16
U8 = mybir.dt.uint8
U32 = mybir.dt.uint32
I32 = mybir.dt.int32
AF = mybir.ActivationFunctionType
ALU = mybir.AluOpType
AX = mybir.AxisListType

S = 2048
D = 1024
NT = 16
EPS = 1e-6
LAM_INIT = 0.2
NB = 128
TGE = 256
NGE = S // TGE


class _Op:
    __slots__ = ("eng", "fn", "deps", "key", "dma_cnt", "need_inc", "cnt", "barrier", "snap")

    def __init__(self, eng, fn, deps, key):
        self.eng = eng
        self.fn = fn
        self.deps = deps
        self.key = key
        self.dma_cnt = 0
        self.need_inc = False
        self.cnt = 0
        self.barrier = False
        self.snap = None


class Prog:
    ENGS = ("pe", "act", "dve", "pool", "sp")

    def __init__(self):
        self.ops = []
        self.last_w = {}
        self.readers = {}
        self.dma_count = {}

    def op(self, eng, fn, r=(), w=(), key=None):
        idx = len(self.ops)
        deps = set()
        for c in r:
            if c in self.last_w:
                deps.add(self.last_w[c])
        for c in w:
            if c in self.last_w:
                deps.add(self.last_w[c])
            for x in self.readers.get(c, ()):
                deps.add(x)
        o = _Op(eng, fn, deps, key)
        if key is not None:
            self.dma_count[key] = self.dma_count.get(key, 0) + 16
            o.dma_cnt = self.dma_count[key]
        self.ops.append(o)
        for c in w:
            self.last_w[c] = idx
            self.readers[c] = []
        for c in r:
            if c not in w:
                self.readers.setdefault(c, []).append(idx)
        return idx

    def barrier(self):
        o = _Op(None, None, set(), None)
        o.barrier = True
        self.ops.append(o)
        self.last_w = {}
        self.readers = {}

    def finalize(self):
        ops = self.ops
        for o in ops:
            if o.barrier:
                continue
            for d in o.deps:
                dep = ops[d]
                if dep.key is None and not (o.eng == "pe" and dep.eng == "pe"):
                    dep.need_inc = True
        last_on = {e: None for e in self.ENGS}
        for i, o in enumerate(ops):
            if o.barrier:
                for e in self.ENGS:
                    if last_on[e] is not None:
                        ops[last_on[e]].need_inc = True
            elif o.key is None:
                last_on[o.eng] = i
        cnt = {e: 0 for e in self.ENGS}
        dcnt = {}
        for o in ops:
            if o.barrier:
                o.snap = (dict(cnt), dict(dcnt))
                continue
            if o.key is not None:
                dcnt[o.key] = o.dma_cnt
            elif o.need_inc:
                cnt[o.eng] += 1
                o.cnt = cnt[o.eng]

    def emit(self, eng_name, e, sems):
        ops = self.ops
        seen = {}

        def wait(s, v):
            if v > 0 and seen.get(s, 0) < v:
                e.wait_ge(sems[s], v)
                seen[s] = v

        for o in ops:
            if o.barrier:
                cnt, dcnt = o.snap
                for b, v in cnt.items():
                    if b != eng_name:
                        wait(("eng", b), v)
                for k, v in dcnt.items():
                    wait(("dma", k), v)
                continue
            if o.eng != eng_name:
                continue
            waits = {}
            for d in o.deps:
                dep = ops[d]
                if dep.key is not None:
                    s, v = ("dma", dep.key), dep.dma_cnt
                else:
                    if eng_name == "pe" and dep.eng == "pe":
                        continue
                    s, v = ("eng", dep.eng), dep.cnt
                if waits.get(s, 0) < v:
                    waits[s] = v
            for s, v in waits.items():
                wait(s, v)
            ins = o.fn(e)
            if o.key is not None:
                ins.then_inc(sems[("dma", o.key)], 16)
            elif o.need_inc:
                ins.then_inc(sems[("eng", eng_name)], 1)


class Arena:
    def __init__(self, ap, size):
        self.ap = ap
        self.size = size
        self.off = 0

    def alloc(self, nbytes, dtype):
        o = (self.off + 63) // 64 * 64
        assert o + nbytes <= self.size, f"SBUF arena overflow {o + nbytes} > {self.size}"
        self.off = o + nbytes
        return self.ap[:, o:o + nbytes].bitcast(dtype)


def build_nc(debug=False, stop_after=None):
    nc = bass.Bass("TRN2", target_bir_lowering=False)

    def din(name, shape, dtype=F32):
        return nc.dram_tensor(name, shape, dtype, kind="ExternalInput").ap()

    x = din("x", [S, D])
    norm1_g = din("norm1_g", [D])
    w_in = din("w_in", [D, 8192])
    lambda_qk = din("lambda_qk", [256])
    subln_g = din("subln_g", [128])
    conv_w = din("conv_w", [3, D])
    w_attn_o = din("w_attn_o", [D, D])
    w_conv_o = din("w_conv_o", [D, D])
    w_out = din("w_out", [D, D])
    norm2_g = din("norm2_g", [D])
    w_query = din("w_query", [D, 2048])
    sub_keys = din("sub_keys", [8, 2, 128, 128])
    expert_u = din("expert_u", [16384, D])
    expert_v = din("expert_v", [16384, D])
    final_g = din("final_g", [D])
    out = nc.dram_tensor("out", [S, D], F32, kind="ExternalOutput").ap()
    skind = "ExternalOutput" if debug else "Internal"
    uT_scr = nc.dram_tensor("uT_scr", [NB, 128, 1024], BF16, kind=skind).ap()
    v_scr = nc.dram_tensor("v_scr", [16384, D], BF16, kind=skind).ap()
    h_scr = nc.dram_tensor("h_scr", [S, D], F32, kind=skind).ap()
    wq_scr = nc.dram_tensor("wq_scr", [D, 2048], BF16, kind="Internal").ap()

    if debug:
        dbg_nT = nc.dram_tensor("dbg_nT", [128, 8 * S], BF16, kind="ExternalOutput").ap()
        dbg_ycT = nc.dram_tensor("dbg_ycT", [128, 8 * S], BF16, kind="ExternalOutput").ap()
        dbg_attnT = nc.dram_tensor("dbg_attnT", [128, 8 * S], BF16, kind="ExternalOutput").ap()
        dbg_mg = nc.dram_tensor("dbg_mg", [128, 8 * S], BF16, kind="ExternalOutput").ap()
        dbg_small = nc.dram_tensor("dbg_small", [128, 256], F32, kind="ExternalOutput").ap()

    P = Prog()
    ARENA_BYTES = 204 * 1024

    with ExitStack() as es:
        arena_t = es.enter_context(nc.sbuf_tensor("arena", [128, ARENA_BYTES], U8))
        ps = es.enter_context(nc.psum_tensor("ps", [128, 4096], F32))
        A = Arena(arena_t, ARENA_BYTES)

        def bank(b):
            return ps[:, b * 512:(b + 1) * 512]

        def bank_bf(b):
            return bank(b).bitcast(BF16)

        ident_bf = A.alloc(256, BF16)
        ident_f = A.alloc(512, F32)
        mask_tri = A.alloc(256, BF16)
        iota_f = A.alloc(512, F32)
        iota_i = A.alloc(512, I32)
        diff_i = A.alloc(512, I32)
        diff_f = A.alloc(512, F32)
        g1b = A.alloc(4096, F32)
        g2b = A.alloc(4096, F32)
        gfb = A.alloc(4096, F32)
        subgb = A.alloc(512, F32)
        cw = A.alloc(96, F32).rearrange("p (c j) -> p c j", j=3)
        lq = A.alloc(1024, F32)
        lam_s = A.alloc(64, F32)
        skT = A.alloc(4096, BF16).rearrange("p (g n) -> p g n", n=128)
        small = A.alloc(64 * 4 * 4, F32)
        junk = A.alloc(2048, BF16)
        junk_f = A.alloc(512, F32)
        smp = A.alloc(6 * 16 * 4, F32)
        iota_b = A.alloc(256, BF16)
        c_mhalf = A.alloc(4, F32)
        c_e = A.alloc(512, F32)
        const_mark = A.off

        sm_ctr = [0]

        def sm():
            k = sm_ctr[0] % 256
            sm_ctr[0] += 1
            return small[:, k:k + 1], f"sm{k}"

        P.op("sp", lambda e: e.dma_start(out=g1b, in_=norm1_g.partition_broadcast(128)), w=["g1b"], key="c0_0")
        P.op("sp", lambda e: e.dma_start(out=g2b, in_=norm2_g.partition_broadcast(128)), w=["g2b"], key="c0_1")
        P.op("sp", lambda e: e.dma_start(out=gfb, in_=final_g.partition_broadcast(128)), w=["gfb"], key="c0_2")
        P.op("sp", lambda e: e.dma_start(out=subgb, in_=subln_g.partition_broadcast(128)), w=["subgb"], key="c0_3")
        P.op("sp", lambda e: e.dma_start(out=lq, in_=lambda_qk.partition_broadcast(128)), w=["lq"], key="c0_4")
        for j in range(3):
            for c in range(8):
                P.op("sp", lambda e, j=j, c=c: e.dma_start(out=cw[:, c, j:j + 1],
                                                           in_=conv_w[j, c * 128:(c + 1) * 128].rearrange("(p o) -> p o", o=1)),
                     w=["cw"], key="c0_5")
        P.op("dve", lambda e: e.memset(c_mhalf, -0.5), w=["c_mhalf"])
        P.op("dve", lambda e: e.memset(c_e, float(np.float32(np.e))), w=["c_e"])
        P.op("pool", lambda e: e.iota(iota_i, pattern=[[1, 128]], base=0, channel_multiplier=0), w=["iota_i"])
        P.op("pool", lambda e: e.iota(diff_i, pattern=[[1, 128]], base=0, channel_multiplier=-1), w=["diff_i"])
        P.op("dve", lambda e: e.tensor_copy(iota_f, iota_i), r=["iota_i"], w=["iota_f"])
        P.op("dve", lambda e: e.tensor_copy(diff_f, diff_i), r=["diff_i"], w=["diff_f"])
        P.op("dve", lambda e: e.tensor_copy(iota_b, iota_i), r=["iota_i"], w=["iota_b"])
        P.op("dve", lambda e: e.tensor_single_scalar(ident_f, diff_f, 0.0, ALU.is_equal), r=["diff_f"], w=["ident_f"])
        P.op("dve", lambda e: e.tensor_single_scalar(ident_bf, diff_f, 0.0, ALU.is_equal), r=["diff_f"], w=["ident_bf"])
        P.op("dve", lambda e: e.tensor_single_scalar(mask_tri, diff_f, 0.0, ALU.is_ge), r=["diff_f"], w=["mask_tri"])
        P.op("dve", lambda e: e.tensor_scalar(subgb, subgb, 1.0 - LAM_INIT, None, ALU.mult), r=["subgb"], w=["subgb"])
        P.op("dve", lambda e: e.tensor_tensor_reduce(out=junk_f[:, 0:64], in0=lq[:, 0:64], in1=lq[:, 64:128],
                                                     scale=1.0, scalar=0.0, op0=ALU.mult, op1=ALU.add,
                                                     accum_out=lam_s[:, 0:1]), r=["lq"], w=["lam0"])
        P.op("dve", lambda e: e.tensor_tensor_reduce(out=junk_f[:, 64:128], in0=lq[:, 128:192], in1=lq[:, 192:256],
                                                     scale=1.0, scalar=0.0, op0=ALU.mult, op1=ALU.add,
                                                     accum_out=lam_s[:, 1:2]), r=["lq"], w=["lam1"])
        P.op("act", lambda e: e.activation(lam_s[:, 2:4], lam_s[:, 0:2], AF.Exp), r=["lam0", "lam1"], w=["lam2"])
        P.op("dve", lambda e: e.tensor_tensor(lam_s[:, 4:5], lam_s[:, 3:4], lam_s[:, 2:3], ALU.subtract),
             r=["lam2"], w=["lam4"])
        neglam = lam_s[:, 5:6]
        P.op("dve", lambda e: e.tensor_scalar(neglam, lam_s[:, 4:5], -LAM_INIT, None, ALU.add),
             r=["lam4"], w=["neglam"])

        m0 = A.off
        sk_nat = A.alloc(4096, BF16).rearrange("p (g k) -> p g k", k=128)
        P.op("pool", lambda e: e.dma_start(out=sk_nat, in_=sub_keys.rearrange("h m n k -> n (h m) k")),
             w=["sk_nat"], key="c1")
        for g4 in range(4):
            for q in range(4):
                g = g4 * 4 + q
                P.op("pe", lambda e, g=g, q=q, g4=g4: e.transpose(bank_bf(g4)[:, q * 128:(q + 1) * 128], sk_nat[:, g, :], ident_bf),
                     r=["sk_nat", "ident_bf"], w=[f"ps{g4}"])
            P.op("act", lambda e, g4=g4: e.activation(skT[:, g4 * 4:(g4 + 1) * 4, :],
                                                      bank_bf(g4)[:, 0:512].rearrange("p (g n) -> p g n", n=128), AF.Copy),
                 r=[f"ps{g4}"], w=["skT"])
        P.barrier()
        A.off = m0

        def rstd_from(ssq_ap, ssq_cell, n):
            lnv, c1 = sm()
            rs, c2 = sm()
            P.op("act", lambda e: e.activation(lnv, ssq_ap, AF.Ln, bias=EPS, scale=1.0 / n), r=[ssq_cell], w=[c1])
            P.op("act", lambda e: e.activation(rs, lnv, AF.Exp, scale=-0.5), r=[c1], w=[c2])
            return rs, c2

        smp_ctr = [0]

        def smp_slot():
            k = smp_ctr[0] % 16
            smp_ctr[0] += 1
            return smp[:, k * 6:k * 6 + 2], smp[:, k * 6 + 2:k * 6 + 4], smp[:, k * 6 + 4:k * 6 + 6], f"smp{k}"

        def rstd_pool(ssq_ap, ssq_cell, n):
            mse, c1 = sm()
            rs, c2 = sm()
            P.op("dve", lambda e: e.tensor_scalar(mse, ssq_ap, 1.0 / n, EPS, ALU.mult, ALU.add), r=[ssq_cell], w=[c1])
            P.op("pool", lambda e: e.tensor_tensor(rs, mse, c_mhalf, ALU.pow), r=[c1, "c_mhalf"], w=[c2])
            return rs, c2

        mP = A.off
        ub = [A.alloc(2048, BF16) for _ in range(3)]
        uTo = [A.alloc(2048, BF16) for _ in range(3)]
        conv_jobs = []
        if stop_after != "M":
            for r in range(8):
                conv_jobs.append(lambda r=r: P.op("pool", lambda e: e.dma_start(out=wq_scr[r * 128:(r + 1) * 128, :], in_=w_query[r * 128:(r + 1) * 128, :]),
                                                  w=["wq_scr"], key="wqconv"))
            for r in range(32):
                conv_jobs.append(lambda r=r: P.op("pool", lambda e: e.dma_start(out=v_scr[r * 512:(r + 1) * 512, :], in_=expert_v[r * 512:(r + 1) * 512, :]),
                                                  w=["v_scr"], key="vconv"))

        def p_block(b):
            nonlocal sb_ctr
            s = b % 3
            pb = sb_ctr % 4
            sb_ctr += 1
            P.op("pool", lambda e: e.dma_start(out=ub[s], in_=expert_u[b * 128:(b + 1) * 128, :]), w=[f"ub{s}"], key=f"ub{s}")
            for c in range(8):
                P.op("pe", lambda e, c=c: e.transpose(bank_bf(pb)[:, c * 128:(c + 1) * 128], ub[s][:, c * 128:(c + 1) * 128], ident_bf),
                     r=[f"ub{s}"], w=[f"ps{pb}"])
            if b % 2 == 0:
                P.op("act", lambda e: e.activation(uTo[s], bank_bf(pb), AF.Copy), r=[f"ps{pb}"], w=[f"uTo{s}"])
            else:
                P.op("dve", lambda e: e.tensor_copy(uTo[s], bank_bf(pb)), r=[f"ps{pb}"], w=[f"uTo{s}"])
            P.op("sp", lambda e: e.dma_start(out=uT_scr[b], in_=uTo[s]), r=[f"uTo{s}"], w=["uT_scr"], key=f"uTo{s}")

        p_next = [0]

        nT = A.alloc(32768, BF16).rearrange("p (c t) -> p c t", t=S)
        ycT = A.alloc(32768, BF16).rearrange("p (c t) -> p c t", t=S)
        attnT = A.alloc(32768, BF16).rearrange("p (c t) -> p c t", t=S)
        NW = 8
        wring = [A.alloc(2048, BF16).rearrange("p (c n) -> p c n", n=128) for _ in range(NW)]
        w_ctr = [0]

        def load_panel(src_ap):
            s = w_ctr[0] % NW
            w_ctr[0] += 1
            P.op("pool", lambda e: e.dma_start(out=wring[s], in_=src_ap), w=[f"w{s}"], key=f"w{s}")
            return wring[s], f"w{s}"

        w_in_v = w_in.rearrange("(c p) n -> p c n", p=128)
        ps_ctr = [0]

        mS = A.off
        xt = [A.alloc(4096, F32) for _ in range(2)]
        nb = [A.alloc(2048, BF16) for _ in range(2)]
        for tt in range(NT):
            s = tt % 2
            pb = tt % 4
            P.op("sp", lambda e, tt=tt, s=s: e.dma_start(out=xt[s], in_=x[tt * 128:(tt + 1) * 128, :]), w=[f"xt{s}"], key=f"xt{s}")
            ssq, cq = sm()
            P.op("dve", lambda e, s=s, ssq=ssq: e.tensor_tensor_reduce(out=junk, in0=xt[s], in1=xt[s], scale=1.0, scalar=0.0,
                                                                       op0=ALU.mult, op1=ALU.add, accum_out=ssq),
                 r=[f"xt{s}"], w=[cq])
            rs, cr = rstd_from(ssq, cq, D)
            P.op("dve", lambda e, s=s, rs=rs: e.scalar_tensor_tensor(out=nb[s], in0=xt[s], scalar=rs, in1=g1b, op0=ALU.mult, op1=ALU.mult),
                 r=[f"xt{s}", cr, "g1b"], w=[f"nb{s}"])
            for c in range(8):
                P.op("pe", lambda e, s=s, c=c, pb=pb: e.transpose(bank_bf(pb)[:, c * 128:(c + 1) * 128],
                                                                nb[s][:, c * 128:(c + 1) * 128], ident_bf),
                     r=[f"nb{s}"], w=[f"ps{pb}"])
            P.op("act", lambda e, tt=tt, pb=pb: e.activation(nT[:, :, tt * 128:(tt + 1) * 128],
                                                            bank_bf(pb).rearrange("p (c t) -> p c t", t=128), AF.Copy),
                 r=[f"ps{pb}"], w=[f"nT{tt // 4}"])
        A.off = mS

        uconv = [A.alloc(2050 * 4, F32) for _ in range(2)]
        tmp1 = [A.alloc(2048, F32) for _ in range(2)]
        zt = [A.alloc(2048, F32) for _ in range(2)]
        for k in range(2):
            P.op("dve", lambda e, k=k: e.memset(uconv[k][:, 0:2], 0.0), w=[f"uconv{k}"])
        step = 0
        def conv_panels(cch):
            return (load_panel(w_in_v[:, :, 3072 + cch * 128:3072 + (cch + 1) * 128]),
                    load_panel(w_in_v[:, :, 4096 + cch * 128:4096 + (cch + 1) * 128]),
                    load_panel(w_in_v[:, :, 5120 + cch * 128:5120 + (cch + 1) * 128]))

        cpan = {0: conv_panels(0)}
        for cch in range(8):
            if cch + 1 < 8:
                cpan[cch + 1] = conv_panels(cch + 1)
            (wcb, kcb), (wcc, kcc), (wcx, kcx) = cpan.pop(cch)
            uc = uconv[cch % 2]
            ucc = f"uconv{cch % 2}"
            for tg in range(4):
                bset = (step % 2) * 3
                step += 1
                bA, bB, bC = bset, bset + 1, bset + 2
                for (bk, wp, wk) in ((bA, wcx, kcx), (bB, wcc, kcc), (bC, wcb, kcb)):
                    for c in range(8):
                        P.op("pe", lambda e, bk=bk, wp=wp, c=c, tg=tg: e.matmul(bank(bk), lhsT=wp[:, c, :],
                                                                            rhs=nT[:, c, tg * 512:(tg + 1) * 512],
                                                                            start=(c == 0), stop=(c == 7)),
                             r=[wk, f"nT{tg}"], w=[f"ps{bk}"])
                s = tg % 2
                o0 = tg * 512
                P.op("act", lambda e, s=s, bA=bA: e.activation(tmp1[s], bank(bA), AF.Copy), r=[f"ps{bA}"], w=[f"tmp1{s}"])
                P.op("dve", lambda e, s=s, bB=bB, uc=uc, o0=o0: e.tensor_tensor(uc[:, 2 + o0:2 + o0 + 512], bank(bB), tmp1[s], ALU.mult),
                     r=[f"ps{bB}", f"tmp1{s}"], w=[ucc])
                P.op("dve", lambda e, s=s, uc=uc, o0=o0, cch=cch: e.tensor_scalar(zt[s], uc[:, 2 + o0:2 + o0 + 512], cw[:, cch, 2:3], None, ALU.mult),
                     r=[ucc, "cw"], w=[f"zt{s}"])
                P.op("dve", lambda e, s=s, uc=uc, o0=o0, cch=cch: e.scalar_tensor_tensor(out=zt[s], in0=uc[:, 1 + o0:1 + o0 + 512], scalar=cw[:, cch, 1:2],
                                                                                       in1=zt[s], op0=ALU.mult, op1=ALU.add),
                     r=[ucc, "cw", f"zt{s}"], w=[f"zt{s}"])
                P.op("dve", lambda e, s=s, uc=uc, o0=o0, cch=cch: e.scalar_tensor_tensor(out=zt[s], in0=uc[:, o0:o0 + 512], scalar=cw[:, cch, 0:1],
                                                                                       in1=zt[s], op0=ALU.mult, op1=ALU.add),
                     r=[ucc, "cw", f"zt{s}"], w=[f"zt{s}"])
                P.op("dve", lambda e, s=s, bC=bC, cch=cch, o0=o0: e.tensor_tensor(ycT[:, cch, o0:o0 + 512], bank(bC), zt[s], ALU.mult),
                     r=[f"ps{bC}", f"zt{s}"], w=[f"ycT{tg}"])
        A.off = mS

        qT = A.alloc(4096, BF16)
        kT = A.alloc(4096, BF16)
        vsb = A.alloc(16 * 130 * 2, BF16).rearrange("p (t e) -> p t e", e=130)
        NPT = 6
        pt = [A.alloc(1024, BF16) for _ in range(NPT)]
        of1 = [A.alloc(512, F32) for _ in range(2)]
        of2 = [A.alloc(512, F32) for _ in range(2)]
        onb = [A.alloc(256, BF16) for _ in range(2)]
        P.op("dve", lambda e: e.memset(vsb[:, :, 128:129], 1.0), w=["vsb1"])
        pt_ctr = 0
        sb_ctr = 0
        fin_ctr = 0

        def oacc(a):
            bk = 4 + a // 2
            o = (a % 2) * 130
            return bank(bk)[:, o:o + 129], f"ps{bk}"

        def head_panels(h):
            return (load_panel(w_in_v[:, :, h * 128:(h + 1) * 128]),
                    load_panel(w_in_v[:, :, 1024 + h * 128:1024 + (h + 1) * 128]),
                    load_panel(w_in_v[:, :, 2048 + h * 128:2048 + (h + 1) * 128]))

        hpan = {0: head_panels(0)}
        for h in range(8):
            if h + 1 < 8:
                hpan[h + 1] = head_panels(h + 1)
            for _ in range(5):
                if conv_jobs:
                    conv_jobs.pop(0)()
            (wq, kq), (wk, kk), (wv, kv) = hpan.pop(h)
            for (dst, dname, wp, wkey) in ((qT, "qT", wq, kq), (kT, "kT", wk, kk)):
                for tg in range(4):
                    bk = sb_ctr % 4
                    sb_ctr += 1
                    for c in range(8):
                        P.op("pe", lambda e, bk=bk, wp=wp, c=c, tg=tg: e.matmul(bank(bk), lhsT=wp[:, c, :],
                                                                            rhs=nT[:, c, tg * 512:(tg + 1) * 512],
                                                                            start=(c == 0), stop=(c == 7)),
                             r=[wkey, f"nT{tg}"], w=[f"ps{bk}"])
                    P.op("act", lambda e, bk=bk, dst=dst, tg=tg: e.activation(dst[:, tg * 512:(tg + 1) * 512], bank(bk), AF.Copy),
                         r=[f"ps{bk}"], w=[f"{dname}{tg}"])
            for t4 in range(4):
                bk = sb_ctr % 4
                sb_ctr += 1
                for tq in range(4):
                    tt = t4 * 4 + tq
                    for c in range(8):
                        P.op("pe", lambda e, bk=bk, tq=tq, tt=tt, c=c, wv=wv: e.matmul(bank(bk)[:, tq * 128:(tq + 1) * 128],
                                                                                   lhsT=nT[:, c, tt * 128:(tt + 1) * 128], rhs=wv[:, c, :],
                                                                                   start=(c == 0), stop=(c == 7)),
                             r=[kv, f"nT{t4}"], w=[f"ps{bk}"])
                P.op("dve", lambda e, bk=bk, t4=t4: e.tensor_copy(vsb[:, t4 * 4:(t4 + 1) * 4, 0:128],
                                                                bank(bk).rearrange("p (t e) -> p t e", e=128)),
                     r=[f"ps{bk}"], w=["vsb"])
            for qg in range(4):
                steps = [(j, m) for j in range(4 * qg + 4) for m in range(2)]

                def emit_S(j, m, qg=qg):
                    nonlocal sb_ctr, pt_ctr
                    col0 = max(qg * 512, j * 128)
                    ncols = (qg + 1) * 512 - col0
                    diag = j >= 4 * qg
                    bk = sb_ctr % 4
                    sb_ctr += 1
                    sl = pt_ctr % NPT
                    pt_ctr += 1
                    P.op("pe", lambda e, bk=bk, m=m, j=j, col0=col0, ncols=ncols: e.matmul(
                        bank(bk)[:, 0:ncols], lhsT=kT[64 * m:64 * m + 64, j * 128:(j + 1) * 128],
                        rhs=qT[64 * m:64 * m + 64, col0:col0 + ncols], start=True, stop=True),
                         r=[f"kT{j // 4}", f"qT{qg}"], w=[f"ps{bk}"])
                    P.op("act", lambda e, bk=bk, sl=sl, ncols=ncols: e.activation(pt[sl][:, 0:ncols], bank(bk)[:, 0:ncols], AF.Exp, scale=0.125),
                         r=[f"ps{bk}"], w=[f"pt{sl}"])
                    if diag:
                        P.op("dve", lambda e, sl=sl: e.tensor_tensor(pt[sl][:, 0:128], pt[sl][:, 0:128], mask_tri, ALU.mult),
                             r=[f"pt{sl}"], w=[f"pt{sl}"])
                    return (sl, col0)

                def emit_PV(j, m, st, qg=qg):
                    sl, col0 = st
                    for i in range(max(4 * qg, j), 4 * qg + 4):
                        off = i * 128 - col0
                        aidx = m * 4 + (i - 4 * qg)
                        oa, oc = oacc(aidx)
                        P.op("pe", lambda e, oa=oa, sl=sl, off=off, j=j, i=i, aidx=aidx: e.matmul(
                            oa, lhsT=pt[sl][:, off:off + 128], rhs=vsb[:, j, 0:129],
                            start=(j == 0 and aidx % 2 == 0), stop=(j == i), skip_group_check=True),
                             r=[f"pt{sl}", "vsb", "vsb1"], w=[oc])

                DEPTH = 2
                sts = {}
                for k in range(min(DEPTH, len(steps))):
                    sts[k] = emit_S(*steps[k])
                for k in range(len(steps)):
                    if k + DEPTH < len(steps):
                        sts[k + DEPTH] = emit_S(*steps[k + DEPTH])
                    emit_PV(steps[k][0], steps[k][1], sts[k])
                for il in range(4):
                    i = 4 * qg + il
                    fs = fin_ctr % 2
                    fin_ctr += 1
                    o0a, c0 = oacc(il)
                    o1a, c1 = oacc(4 + il)
                    r0, cr0 = sm()
                    r1, cr1 = sm()
                    r1n, cr1n = sm()
                    P.op("dve", lambda e, r0=r0, o0a=o0a: e.reciprocal(r0, o0a[:, 128:129]), r=[c0], w=[cr0])
                    P.op("dve", lambda e, r1=r1, o1a=o1a: e.reciprocal(r1, o1a[:, 128:129]), r=[c1], w=[cr1])
                    P.op("dve", lambda e, r1=r1, r1n=r1n: e.tensor_tensor(r1n, r1, neglam, ALU.mult), r=[cr1, "neglam"], w=[cr1n])
                    P.op("dve", lambda e, fs=fs, o0a=o0a, r0=r0: e.tensor_scalar(of1[fs], o0a[:, 0:128], r0, None, ALU.mult),
                         r=[c0, cr0], w=[f"of1{fs}"])
                    P.op("dve", lambda e, fs=fs, o1a=o1a, r1n=r1n: e.scalar_tensor_tensor(out=of2[fs], in0=o1a[:, 0:128], scalar=r1n, in1=of1[fs],
                                                                                       op0=ALU.mult, op1=ALU.add),
                         r=[c1, cr1n, f"of1{fs}"], w=[f"of2{fs}"])
                    ssq, cq = sm()
                    P.op("dve", lambda e, fs=fs, ssq=ssq: e.tensor_tensor_reduce(out=junk_f, in0=of2[fs], in1=of2[fs], scale=1.0, scalar=0.0,
                                                                                 op0=ALU.mult, op1=ALU.add, accum_out=ssq),
                         r=[f"of2{fs}"], w=[cq])
                    rs, cr = rstd_from(ssq, cq, 128)
                    P.op("dve", lambda e, fs=fs, rs=rs: e.scalar_tensor_tensor(out=onb[fs], in0=of2[fs], scalar=rs, in1=subgb,
                                                                             op0=ALU.mult, op1=ALU.mult),
                         r=[f"of2{fs}", cr, "subgb"], w=[f"onb{fs}"])
                    bk = sb_ctr % 4
                    sb_ctr += 1
                    P.op("pe", lambda e, bk=bk, fs=fs: e.transpose(bank_bf(bk)[:, 0:128], onb[fs], ident_bf), r=[f"onb{fs}"], w=[f"ps{bk}"])
                    P.op("act", lambda e, bk=bk, h=h, i=i: e.activation(attnT[:, h, i * 128:(i + 1) * 128], bank_bf(bk)[:, 0:128], AF.Copy),
                         r=[f"ps{bk}"], w=[f"attnT{i // 4}"])
                if stop_after != "M":
                    for _ in range(4):
                        if p_next[0] < NB:
                            p_block(p_next[0])
                            p_next[0] += 1
        A.off = mS

        mergedT = A.alloc(32768, BF16).rearrange("p (c t) -> p c t", t=S)
        sga = [A.alloc(2048, F32) for _ in range(2)]
        sgc = [A.alloc(2048, F32) for _ in range(2)]
        t1 = [A.alloc(2048, F32) for _ in range(2)]
        t2 = [A.alloc(2048, F32) for _ in range(2)]
        wao_v = w_attn_o.rearrange("(c p) n -> p c n", p=128)
        wco_v = w_conv_o.rearrange("(c p) n -> p c n", p=128)
        step = 0
        def merge_panels(cch):
            cs = slice(cch * 128, (cch + 1) * 128)
            return (load_panel(wao_v[:, :, cs]), load_panel(wco_v[:, :, cs]),
                    load_panel(w_in_v[:, :, 6144 + cch * 128:6144 + (cch + 1) * 128]),
                    load_panel(w_in_v[:, :, 7168 + cch * 128:7168 + (cch + 1) * 128]))

        mpan = {0: merge_panels(0)}
        for cch in range(8):
            if cch + 1 < 8:
                mpan[cch + 1] = merge_panels(cch + 1)
            (wao, kao), (wco, kco), (wga, kga), (wgc, kgc) = mpan.pop(cch)
            for tg in range(4):
                bset = (step % 2) * 4
                s = step % 2
                step += 1
                bA, bB, bC, bD = bset, bset + 1, bset + 2, bset + 3
                ts = slice(tg * 512, (tg + 1) * 512)
                for (bk, wp, wk_, src, sname) in ((bA, wao, kao, attnT, "attnT"), (bB, wco, kco, ycT, "ycT"),
                                                 (bC, wga, kga, nT, "nT"), (bD, wgc, kgc, nT, "nT")):
                    for c in range(8):
                        P.op("pe", lambda e, bk=bk, wp=wp, c=c, src=src, ts=ts: e.matmul(bank(bk), lhsT=wp[:, c, :], rhs=src[:, c, ts],
                                                                                     start=(c == 0), stop=(c == 7)),
                             r=[wk_, f"{sname}{tg}"], w=[f"ps{bk}"])
                P.op("act", lambda e, s=s, bC=bC: e.activation(sga[s], bank(bC), AF.Sigmoid), r=[f"ps{bC}"], w=[f"sga{s}"])
                P.op("act", lambda e, s=s, bD=bD: e.activation(sgc[s], bank(bD), AF.Sigmoid), r=[f"ps{bD}"], w=[f"sgc{s}"])
                P.op("dve", lambda e, s=s, bA=bA: e.tensor_tensor(t1[s], bank(bA), sga[s], ALU.mult), r=[f"ps{bA}", f"sga{s}"], w=[f"t1{s}"])
                P.op("dve", lambda e, s=s, bB=bB: e.tensor_tensor(t2[s], bank(bB), sgc[s], ALU.mult), r=[f"ps{bB}", f"sgc{s}"], w=[f"t2{s}"])
                P.op("dve", lambda e, s=s, cch=cch, ts=ts: e.tensor_tensor(mergedT[:, cch, ts], t1[s], t2[s], ALU.add),
                     r=[f"t1{s}", f"t2{s}"], w=[f"mg{tg}"])
        P.barrier()
        if debug:
            P.op("sp", lambda e: e.dma_start(out=dbg_nT, in_=nT.rearrange("p c t -> p (c t)")), w=["dbg1"], key="dbg1")
            P.op("sp", lambda e: e.dma_start(out=dbg_ycT, in_=ycT.rearrange("p c t -> p (c t)")), w=["dbg2"], key="dbg2")
            P.op("sp", lambda e: e.dma_start(out=dbg_attnT, in_=attnT.rearrange("p c t -> p (c t)")), w=["dbg3"], key="dbg3")
            P.op("sp", lambda e: e.dma_start(out=dbg_mg, in_=mergedT.rearrange("p c t -> p (c t)")), w=["dbg4"], key="dbg4")
            P.op("sp", lambda e: e.dma_start(out=dbg_small, in_=small), w=["dbg5"], key="dbg5")
            P.barrier()
        A.off = mP
        wout = A.alloc(16384, BF16).rearrange("p (c n) -> p c n", n=1024)
        xt2 = [A.alloc(4096, F32) for _ in range(2)]
        ht = [A.alloc(4096, F32) for _ in range(2)]
        assert A.off <= mP + 65536
        for c in range(8):
            P.op("pool", lambda e, c=c: e.dma_start(out=wout[:, c, :], in_=w_out[c * 128:(c + 1) * 128, :]), w=["wout"], key="wout")
        for tt in range(NT):
            s = tt % 2
            P.op("sp", lambda e, tt=tt, s=s: e.dma_start(out=xt2[s], in_=x[tt * 128:(tt + 1) * 128, :]), w=[f"xt2{s}"], key=f"xt2{s}")
            for half in range(2):
                bk = (tt * 2 + half) % 8
                hs_ = slice(half * 512, (half + 1) * 512)
                for c in range(8):
                    P.op("pe", lambda e, bk=bk, c=c, tt=tt, hs_=hs_: e.matmul(bank(bk), lhsT=mergedT[:, c, tt * 128:(tt + 1) * 128],
                                                                          rhs=wout[:, c, hs_], start=(c == 0), stop=(c == 7)),
                         r=["wout", f"mg{tt // 4}"], w=[f"ps{bk}"])
                P.op("dve", lambda e, bk=bk, s=s, hs_=hs_: e.tensor_tensor(ht[s][:, hs_], bank(bk), xt2[s][:, hs_], ALU.add),
                     r=[f"ps{bk}", f"xt2{s}"], w=[f"ht{s}"])
            P.op("sp", lambda e, tt=tt, s=s: e.dma_start(out=h_scr[tt * 128:(tt + 1) * 128, :], in_=ht[s]), r=[f"ht{s}"], w=["h_scr"], key=f"ht{s}")
        P.barrier()
        A.off = mP

        if stop_after != "M":
            HB = NB // 2
            GTh = [A.alloc(HB * TGE * 2, BF16).rearrange("p (i t) -> p i t", t=TGE) for _ in range(2)]
            xn2T = [A.alloc(8 * TGE * 2, BF16).rearrange("p (c t) -> p c t", t=TGE) for _ in range(2)]
            hsb = [[A.alloc(4096, F32) for _ in range(2)] for _ in range(2)]
            IJGT = [A.alloc(3 * TGE * 4, F32).rearrange("p (q t) -> p q t", t=TGE) for _ in range(2)]
            IJb = [A.alloc(2 * TGE * 2, BF16).rearrange("p (q t) -> p q t", t=TGE) for _ in range(2)]
            qTa = A.alloc(16 * TGE * 2, BF16).rearrange("p (g t) -> p g t", t=TGE)
            S2 = A.alloc(8192, F32).rearrange("p (g n) -> p g n", n=128)
            cand2 = S2.rearrange("p g n -> p (g n)").rearrange("p (h c) -> p h c", c=256)
            Eo = cand2.rearrange("p h (k a) -> p h k a", a=16)
            xnb = [A.alloc(2048, BF16) for _ in range(2)]
            tv = A.alloc(1024, F32).rearrange("p (g k) -> p g k", k=16)
            ti = A.alloc(1024, U32).rearrange("p (g k) -> p g k", k=16)
            tif = A.alloc(1024, F32).rearrange("p (g k) -> p g k", k=16)
            cand = A.alloc(8192, F32).rearrange("p (h c) -> p h c", c=256)
            cv = A.alloc(512, F32).rearrange("p (h k) -> p h k", k=16)
            ci = A.alloc(512, U32).rearrange("p (h k) -> p h k", k=16)
            cia = A.alloc(512, U32).rearrange("p (h k) -> p h k", k=16)
            cib = A.alloc(512, U32).rearrange("p (h k) -> p h k", k=16)
            af_ = A.alloc(512, F32).rearrange("p (h k) -> p h k", k=16)
            bf_ = A.alloc(512, F32).rearrange("p (h k) -> p h k", k=16)
            IJG = A.alloc(3 * 512, F32).rearrange("p (q h k) -> p q h k", q=3, k=16)
            dg = A.alloc(512, F32).rearrange("p (h k) -> p h k", k=16)
            eg = A.alloc(512, F32).rearrange("p (h k) -> p h k", k=16)
            zs = A.alloc(32, F32)
            rz = A.alloc(32, F32)
            CH = 8
            NOH = 3
            oh1 = [A.alloc(CH * 64 * 2, BF16).rearrange("p (t i) -> p t i", i=64) for _ in range(NOH)]
            oh2 = [A.alloc(CH * 128 * 2, BF16).rearrange("p (t i) -> p t i", i=128) for _ in range(NOH)]
            oh2g = [A.alloc(CH * 128 * 2, BF16).rearrange("p (t i) -> p t i", i=128) for _ in range(NOH)]
            NU = 6
            uTb = [A.alloc(2048, BF16).rearrange("p (c e) -> p c e", e=128) for _ in range(NU)]
            vb = [A.alloc(2048, BF16) for _ in range(NU)]
            geb = [A.alloc(TGE * 2, BF16) for _ in range(3)]
            hab = [A.alloc(TGE * 2, BF16) for _ in range(3)]
            NWQ = 2
            wqr = [A.alloc(2048, BF16).rearrange("p (c n) -> p c n", n=128) for _ in range(NWQ)]
            wq_v = wq_scr.rearrange("(c p) n -> p c n", p=128)
            iota16 = iota_f[:, 0:16]
            st = {"wq": 0, "oh": 0, "blk": 0, "pp": 0}

            class RPool:
                def __init__(self, items):
                    self.free = list(items)

                def acquire(self):
                    return self.free.pop(0) if self.free else None

                def release(self, x):
                    self.free.append(x)

            bank_pool = RPool([6, 7])
            oh_pool = RPool(list(range(NOH)))
            wq_pool = RPool(list(range(NWQ)))

            def prep_topk(g):
                gb = g % 2
                th = []

                def e1(_unused):
                    ssq2, ln2, rs2, pc = smp_slot()
                    hss = [hsb[gb][tl] for tl in range(2)]
                    hcs = [f"hsb{gb}{tl}" for tl in range(2)]
                    for tl in range(2):
                        tt = g * 2 + tl
                        P.op("sp", lambda e, tl=tl, tt=tt: e.dma_start(out=hss[tl], in_=h_scr[tt * 128:(tt + 1) * 128, :]),
                             r=["h_scr"], w=[hcs[tl]], key=hcs[tl])
                    yield
                    yield
                    for tl in range(2):
                        P.op("dve", lambda e, tl=tl: e.tensor_tensor_reduce(out=junk, in0=hss[tl], in1=hss[tl], scale=1.0, scalar=0.0,
                                                                            op0=ALU.mult, op1=ALU.add, accum_out=ssq2[:, tl:tl + 1]),
                             r=[hcs[tl]], w=[pc + f"s{tl}"])
                    yield
                    P.op("act", lambda e: e.activation(ln2, ssq2, AF.Ln, bias=EPS, scale=1.0 / D), r=[pc + "s0", pc + "s1"], w=[pc + "l"])
                    P.op("act", lambda e: e.activation(rs2, ln2, AF.Exp, scale=-0.5), r=[pc + "l"], w=[pc + "r"])
                    yield
                    yield
                    for tl in range(2):
                        P.op("dve", lambda e, tl=tl: e.scalar_tensor_tensor(out=xnb[tl], in0=hss[tl], scalar=rs2[:, tl:tl + 1], in1=g2b,
                                                                            op0=ALU.mult, op1=ALU.mult),
                             r=[hcs[tl], pc + "r", "g2b"], w=[f"xnb{tl}"])
                    yield
                    for tl in range(2):
                        while True:
                            pb = bank_pool.acquire()
                            if pb is not None:
                                break
                            yield
                        for c in range(8):
                            P.op("pe", lambda e, c=c, tl=tl, pb=pb: e.transpose(bank_bf(pb)[:, c * 128:(c + 1) * 128],
                                                                             xnb[tl][:, c * 128:(c + 1) * 128], ident_bf),
                                 r=[f"xnb{tl}"], w=[f"ps{pb}"])
                        yield
                        P.op("act", lambda e, tl=tl, pb=pb: e.activation(xn2T[gb][:, :, tl * 128:(tl + 1) * 128],
                                                                        bank_bf(pb).rearrange("p (c t) -> p c t", t=128), AF.Copy),
                             r=[f"ps{pb}"], w=[f"xn2T{gb}"])
                        bank_pool.release(pb)

                def e2(grp):
                    while True:
                        sl = wq_pool.acquire()
                        if sl is not None:
                            break
                        yield
                    P.op("sp", lambda e: e.dma_start(out=wqr[sl], in_=wq_v[:, :, grp * 128:(grp + 1) * 128]), w=[f"wq{sl}"], key=f"wq{sl}")
                    yield
                    yield
                    yield
                    while True:
                        bk = bank_pool.acquire()
                        if bk is not None:
                            break
                        yield
                    for c in range(8):
                        P.op("pe", lambda e, c=c: e.matmul(bank(bk)[:, 0:TGE], lhsT=wqr[sl][:, c, :], rhs=xn2T[gb][:, c, :],
                                                           start=(c == 0), stop=(c == 7)),
                             r=[f"wq{sl}", f"xn2T{gb}"], w=[f"ps{bk}"])
                    wq_pool.release(sl)
                    yield
                    P.op("act", lambda e: e.activation(qTa[:, grp, :], bank(bk)[:, 0:TGE], AF.Copy), r=[f"ps{bk}"], w=[f"qTa{grp}"])
                    bank_pool.release(bk)

                def e3(tl, quad):
                    while True:
                        bk = bank_pool.acquire()
                        if bk is not None:
                            break
                        yield
                    grps = [quad * 4 + q for q in range(4)]
                    scs = {grp: bank(bk)[:, (grp % 4) * 128:(grp % 4 + 1) * 128] for grp in grps}
                    for grp in grps:
                        P.op("pe", lambda e, grp=grp: e.matmul(scs[grp], lhsT=qTa[:, grp, tl * 128:(tl + 1) * 128], rhs=skT[:, grp, :],
                                                               start=True, stop=True),
                             r=[f"qTa{grp}", "skT"], w=[f"ps{bk}"])
                    yield
                    for grp in grps:
                        P.op("dve", lambda e, grp=grp: e.max(out=tv[:, grp, 0:8], in_=scs[grp]), r=[f"ps{bk}"], w=[f"tva{grp}"])
                    for grp in grps:
                        P.op("dve", lambda e, grp=grp: e.max_index(out=ti[:, grp, 0:8], in_max=tv[:, grp, 0:8], in_values=scs[grp]),
                             r=[f"ps{bk}", f"tva{grp}"], w=[f"tia{grp}"])
                    for grp in grps:
                        P.op("dve", lambda e, grp=grp: e.match_replace(out=S2[:, grp, :], in_to_replace=tv[:, grp, 0:8], in_values=scs[grp],
                                                                     imm_value=-1e30),
                             r=[f"ps{bk}", f"tva{grp}"], w=[f"S2{grp}"])
                    for grp in grps:
                        P.op("dve", lambda e, grp=grp: e.max(out=tv[:, grp, 8:16], in_=S2[:, grp, :]), r=[f"S2{grp}"], w=[f"tvb{grp}"])
                    for grp in grps:
                        P.op("dve", lambda e, grp=grp: e.max_index(out=ti[:, grp, 8:16], in_max=tv[:, grp, 8:16], in_values=S2[:, grp, :]),
                             r=[f"S2{grp}", f"tvb{grp}"], w=[f"tib{grp}"])
                    bank_pool.release(bk)

                TVC = [f"tva{g_}" for g_ in range(16)] + [f"tvb{g_}" for g_ in range(16)]
                TIC = [f"tia{g_}" for g_ in range(16)] + [f"tib{g_}" for g_ in range(16)]
                S2C = [f"S2{g_}" for g_ in range(16)]
                CVC = [f"cva{h_}" for h_ in range(8)] + [f"cvb{h_}" for h_ in range(8)]
                CIC = [f"cia{h_}" for h_ in range(8)] + [f"cib{h_}" for h_ in range(8)]
                C2C = [f"cand2_{h_}" for h_ in range(8)]
                tvv = tv.rearrange("p (h m) k -> p h m k", m=2)
                tifv = tif.rearrange("p (h m) k -> p h m k", m=2)
                candv = cand.rearrange("p h (a b) -> p h a b", b=16)

                def e4a(tl):
                    yield
                    P.op("dve", lambda e: e.tensor_copy(tif, ti), r=TIC, w=["tif"])
                    P.op("dve", lambda e: e.tensor_tensor(candv, tvv[:, :, 0, :].unsqueeze(3).broadcast_to([128, 8, 16, 16]),
                                                          tvv[:, :, 1, :].unsqueeze(2).broadcast_to([128, 8, 16, 16]), ALU.add),
                         r=TVC, w=[f"cand{h_}" for h_ in range(8)])
                    for h in range(8):
                        P.op("dve", lambda e, h=h: e.max(out=cv[:, h, 0:8], in_=cand[:, h, :]), r=[f"cand{h}"], w=[f"cva{h}"])
                    for h in range(8):
                        P.op("dve", lambda e, h=h: e.max_index(out=ci[:, h, 0:8], in_max=cv[:, h, 0:8], in_values=cand[:, h, :]),
                             r=[f"cand{h}", f"cva{h}"], w=[f"cia{h}"])
                    for h in range(8):
                        P.op("dve", lambda e, h=h: e.match_replace(out=cand2[:, h, :], in_to_replace=cv[:, h, 0:8], in_values=cand[:, h, :],
                                                                 imm_value=-1e30), r=[f"cand{h}", f"cva{h}"], w=[f"cand2_{h}"] + S2C[2 * h:2 * h + 2])

                def e4b(tl):
                    yield
                    for h in range(8):
                        P.op("dve", lambda e, h=h: e.max(out=cv[:, h, 8:16], in_=cand2[:, h, :]), r=[f"cand2_{h}"], w=[f"cvb{h}"])
                    for h in range(8):
                        P.op("dve", lambda e, h=h: e.max_index(out=ci[:, h, 8:16], in_max=cv[:, h, 8:16], in_values=cand2[:, h, :]),
                             r=[f"cand2_{h}", f"cvb{h}"], w=[f"cib{h}"])
                    P.op("dve", lambda e: e.tensor_single_scalar(cia, ci, 4, ALU.logical_shift_right), r=CIC, w=["cia"])
                    P.op("dve", lambda e: e.tensor_single_scalar(cib, ci, 15, ALU.bitwise_and), r=CIC, w=["cib"])
                    P.op("dve", lambda e: e.tensor_copy(af_, cia), r=["cia"], w=["af"])
                    P.op("dve", lambda e: e.tensor_copy(bf_, cib), r=["cib"], w=["bf"])

                def e4c(tl):
                    io4 = iota16.unsqueeze(1).unsqueeze(1).broadcast_to([128, 8, 16, 16])
                    for (q, src, mm) in ((0, af_, 0), (1, bf_, 1)):
                        P.op("dve", lambda e, src=src: e.tensor_tensor(Eo, src.unsqueeze(3).broadcast_to([128, 8, 16, 16]), io4, ALU.is_equal),
                             r=["af", "bf", "iota_f"], w=C2C + S2C)
                        P.op("dve", lambda e, mm=mm: e.tensor_tensor(Eo, Eo, tifv[:, :, mm, :].unsqueeze(2).broadcast_to([128, 8, 16, 16]), ALU.mult),
                             r=C2C + ["tif"], w=C2C + S2C)
                        P.op("dve", lambda e, q=q: e.tensor_reduce(out=IJG[:, q, :, :], in_=Eo, axis=AX.X, op=ALU.add), r=C2C + S2C, w=["IJG"])
                    P.op("dve", lambda e: e.tensor_tensor(dg, cv, cv[:, :, 0:1].broadcast_to([128, 8, 16]), ALU.subtract), r=CVC, w=["dg"])
                    yield
                    P.op("act", lambda e: e.activation(eg, dg, AF.Exp), r=["dg"], w=["eg"])
                    yield
                    P.op("dve", lambda e: e.tensor_reduce(out=zs, in_=eg, axis=AX.X, op=ALU.add), r=["eg"], w=["zs"])
                    P.op("dve", lambda e: e.reciprocal(rz, zs), r=["zs"], w=["rz"])
                    P.op("dve", lambda e: e.tensor_tensor(IJG[:, 2, :, :], eg, rz.unsqueeze(2).broadcast_to([128, 8, 16]), ALU.mult),
                         r=["eg", "rz", "IJG"], w=["IJG"])
                    yield
                    while True:
                        pb = bank_pool.acquire()
                        if pb is not None:
                            break
                        yield
                    for q in range(3):
                        P.op("pe", lambda e, q=q: e.transpose(bank(pb)[:, q * 128:(q + 1) * 128], IJG[:, q, :, :].rearrange("p h k -> p (h k)"), ident_f),
                             r=["IJG", "ident_f"], w=[f"ps{pb}"])
                    yield
                    P.op("act", lambda e: e.activation(IJGT[gb][:, :, tl * 128:(tl + 1) * 128],
                                                       bank(pb)[:, 0:384].rearrange("p (q t) -> p q t", t=128), AF.Copy),
                         r=[f"ps{pb}"], w=[f"IJGT{gb}"])
                    P.op("act", lambda e: e.activation(IJb[gb][:, :, tl * 128:(tl + 1) * 128],
                                                       bank(pb)[:, 0:256].rearrange("p (q t) -> p q t", t=128), AF.Copy),
                         r=[f"ps{pb}"], w=[f"IJb{gb}"])
                    bank_pool.release(pb)

                th.append(lambda: e1(0))
                th.append("FENCE")
                for grp in range(16):
                    th.append(lambda grp=grp: e2(grp))
                th.append("FENCE")
                for tl in range(2):
                    for quad in range(4):
                        th.append(lambda tl=tl, quad=quad: e3(tl, quad))
                    th.append("FENCE")
                    th.append(lambda tl=tl: e4a(tl))
                    th.append("FENCE")
                    th.append(lambda tl=tl: e4b(tl))
                    th.append("FENCE")
                    th.append(lambda tl=tl: e4c(tl))
                    th.append("FENCE")
                return th

            def prep_G(g, half):
                gb = g % 2
                th = []

                def chunk(ch):
                    while True:
                        s = oh_pool.acquire()
                        if s is not None:
                            break
                        yield
                    c0 = ch * CH
                    iob = iota_b.unsqueeze(1).broadcast_to([128, CH, 128])
                    iobh = iota_b[:, 64 * half:64 * half + 64].unsqueeze(1).broadcast_to([128, CH, 64])
                    P.op("dve", lambda e: e.tensor_tensor(oh1[s], iobh, IJb[gb][:, 0, c0:c0 + CH].unsqueeze(2).broadcast_to([128, CH, 64]), ALU.is_equal),
                         r=[f"IJb{gb}", "iota_b"], w=[f"oh1{s}"])
                    P.op("dve", lambda e: e.tensor_tensor(oh2[s], iob, IJb[gb][:, 1, c0:c0 + CH].unsqueeze(2).broadcast_to([128, CH, 128]), ALU.is_equal),
                         r=[f"IJb{gb}", "iota_b"], w=[f"oh2{s}"])
                    P.op("pool", lambda e: e.tensor_tensor(oh2g[s], oh2[s], IJGT[gb][:, 2, c0:c0 + CH].unsqueeze(2).broadcast_to([128, CH, 128]), ALU.mult),
                         r=[f"IJGT{gb}", f"oh2{s}"], w=[f"oh2g{s}"])
                    yield
                    yield
                    while True:
                        bk = bank_pool.acquire()
                        if bk is not None:
                            break
                        yield
                    for t in range(CH):
                        P.op("pe", lambda e, t=t: e.matmul(bank(bk)[:, t * 64:(t + 1) * 64], lhsT=oh2g[s][:, t, :], rhs=oh1[s][:, t, :],
                                                           start=True, stop=True),
                             r=[f"oh1{s}", f"oh2g{s}"], w=[f"ps{bk}"])
                    oh_pool.release(s)
                    yield
                    P.op("act", lambda e: e.activation(GTh[half][:, :, c0:c0 + CH].rearrange("p i t -> p t i"),
                                                       bank(bk).rearrange("p (t i) -> p t i", i=64), AF.Copy),
                         r=[f"ps{bk}"], w=[f"GT{half}"])
                    bank_pool.release(bk)

                for ch in range(TGE // CH):
                    th.append(lambda ch=ch: chunk(ch))
                return th

            def emit_U(g, i):
                gb = g % 2
                k = st["blk"]
                st["blk"] += 1
                s = k % NU
                pa = 4 + k % 2
                s2 = k % 3
                half = i // HB
                P.op("sp", lambda e: e.dma_start(out=uTb[s], in_=uT_scr[i].rearrange("p (c e) -> p c e", e=128)), w=[f"uTb{s}"], key=f"uTb{s}")
                P.op("sp", lambda e: e.dma_start(out=vb[s], in_=v_scr[i * 128:(i + 1) * 128, :]), w=[f"vb{s}"], key=f"vb{s}")
                for c in range(8):
                    P.op("pe", lambda e, c=c: e.matmul(bank(pa)[:, 0:TGE], lhsT=uTb[s][:, c, :], rhs=xn2T[gb][:, c, :],
                                                       start=(c == 0), stop=(c == 7)),
                         r=[f"uTb{s}", f"xn2T{gb}"], w=[f"ps{pa}"])
                P.op("act", lambda e: e.activation(geb[s2], bank(pa)[:, 0:TGE], AF.Gelu), r=[f"ps{pa}"], w=[f"geb{s2}"])
                P.op("dve", lambda e: e.tensor_tensor(hab[s2], geb[s2], GTh[half][:, i % HB, :], ALU.mult),
                     r=[f"geb{s2}", f"GT{half}"], w=[f"hab{s2}"])
                return (s, s2)

            def emit_V(i, ss):
                s, s2 = ss
                for tl in range(2):
                    for hf_ in range(2):
                        bk = tl * 2 + hf_
                        P.op("pe", lambda e, tl=tl, hf_=hf_, bk=bk: e.matmul(
                            bank(bk), lhsT=hab[s2][:, tl * 128:(tl + 1) * 128], rhs=vb[s][:, hf_ * 512:(hf_ + 1) * 512],
                            start=(i == 0), stop=(i == NB - 1)),
                             r=[f"hab{s2}", f"vb{s}"], w=[f"ps{bk}"])

            def group_end(g):
                gb = g % 2
                ssq2, ln2, rs2, pc = smp_slot()
                hss = [hsb[gb][tl] for tl in range(2)]
                hcs = [f"hsb{gb}{tl}" for tl in range(2)]
                for tl in range(2):
                    for hf_ in range(2):
                        bk = tl * 2 + hf_
                        hs_ = slice(hf_ * 512, (hf_ + 1) * 512)
                        P.op("dve", lambda e, bk=bk, hs_=hs_, tl=tl: e.tensor_tensor(hss[tl][:, hs_], bank(bk), hss[tl][:, hs_], ALU.add),
                             r=[f"ps{bk}", hcs[tl]], w=[hcs[tl]])
                    P.op("dve", lambda e, tl=tl: e.tensor_tensor_reduce(out=junk, in0=hss[tl], in1=hss[tl], scale=1.0, scalar=0.0,
                                                                        op0=ALU.mult, op1=ALU.add, accum_out=ssq2[:, tl:tl + 1]),
                         r=[hcs[tl]], w=[pc + f"s{tl}"])
                yield
                P.op("act", lambda e: e.activation(ln2, ssq2, AF.Ln, bias=EPS, scale=1.0 / D), r=[pc + "s0", pc + "s1"], w=[pc + "l"])
                P.op("act", lambda e: e.activation(rs2, ln2, AF.Exp, scale=-0.5), r=[pc + "l"], w=[pc + "r"])
                yield
                yield
                for tl in range(2):
                    tt = g * 2 + tl
                    P.op("dve", lambda e, tl=tl: e.scalar_tensor_tensor(out=hss[tl], in0=hss[tl], scalar=rs2[:, tl:tl + 1], in1=gfb,
                                                                        op0=ALU.mult, op1=ALU.mult),
                         r=[hcs[tl], pc + "r", "gfb"], w=[hcs[tl]])
                    P.op("sp", lambda e, tl=tl, tt=tt: e.dma_start(out=out[tt * 128:(tt + 1) * 128, :], in_=hss[tl]),
                         r=[hcs[tl]], w=["out"], key=hcs[tl])

            class Stream:
                def __init__(self, items):
                    self.items = list(items)
                    self.active = []

                def pending(self):
                    return sum(1 for x_ in self.items if x_ != "FENCE")

                def busy(self):
                    return bool(self.items or self.active)

                def step(self, nstart):
                    for gen in list(self.active):
                        try:
                            next(gen)
                        except StopIteration:
                            self.active.remove(gen)
                    started = 0
                    while self.items and started < nstart:
                        if self.items[0] == "FENCE":
                            if self.active:
                                break
                            self.items.pop(0)
                            continue
                        gen = self.items.pop(0)()
                        try:
                            next(gen)
                            self.active.append(gen)
                        except StopIteration:
                            pass
                        started += 1

                def drain(self, tag=""):
                    n0 = self.pending(); a0 = len(self.active); k = 0
                    while self.busy():
                        self.step(4); k += 1
                    if debug and (n0 or a0):
                        print(f"drain {tag}: pending={n0} active={a0} steps={k}")

            ge_streams = []
            Stream(prep_topk(0)).drain()
            Stream(prep_G(0, 0)).drain()
            for g in range(NGE):
                sB = Stream(prep_G(g, 1))
                sT = Stream(prep_topk(g + 1) if g + 1 < NGE else [])
                sA = Stream(prep_G(g + 1, 0) if g + 1 < NGE else [])
                sE = ge_streams.pop(0) if ge_streams else None
                pend = {0: emit_U(g, 0), 1: emit_U(g, 1)}
                for i in range(NB):
                    ii = i % HB
                    ep_busy = sE is not None and sE.busy()
                    if ep_busy:
                        sE.step(0)
                    if i < HB:
                        left = max(1, (HB - 14) - ii)
                        sB.step((sB.pending() + left - 1) // left if sB.pending() else 0)
                        if not ep_busy:
                            sT.step(2)
                    else:
                        if i == HB:
                            sT.drain(f'g{g} sT@HB')
                        left = max(1, (HB - 8) - ii)
                        sA.step((sA.pending() + left - 1) // left if sA.pending() else 0)
                    if i + 2 < NB:
                        if (i + 2) == HB:
                            sB.drain(f'g{g} sB@62')
                        pend[i + 2] = emit_U(g, i + 2)
                    emit_V(i, pend.pop(i))
                sB.drain(f'g{g} sB@end')
                sT.drain(f'g{g} sT@end')
                sA.drain(f'g{g} sA@end')
                if sE is not None:
                    sE.drain()
                sEn = Stream([lambda g=g: group_end(g)])
                sEn.step(1)
                ge_streams.append(sEn)
            ge_streams[0].drain()
        P.barrier()

        P.finalize()
        sems = {}
        for en in Prog.ENGS:
            sems[("eng", en)] = es.enter_context(nc.semaphore(f"sem_{en}"))
        for i, k in enumerate(sorted(P.dma_count.keys())):
            sems[("dma", k)] = es.enter_context(nc.semaphore(f"semd_{i}"))
        block = es.enter_context(nc.Block())

        @block.tensor
        def _(e):
            P.emit("pe", e, sems)

        @block.scalar
        def _(e):
            P.emit("act", e, sems)

        @block.vector
        def _(e):
            P.emit("dve", e, sems)

        @block.gpsimd
        def _(e):
            P.emit("pool", e, sems)

        @block.sync
        def _(e):
            P.emit("sp", e, sems)

    mybir.codegen_inst_isa_subclasses(nc)
    return nc


_NC_CACHE = {}


def _prep_inputs(inputs):
    f = lambda a: np.ascontiguousarray(np.asarray(a, dtype=np.float32))
    shared = {
        "norm1_g": f(inputs["norm1_g"]).reshape(D),
        "w_in": f(inputs["w_in"]).reshape(D, 8192),
        "lambda_qk": f(inputs["lambda_qk"]).reshape(256),
        "subln_g": f(inputs["subln_g"]).reshape(128),
        "conv_w": f(inputs["conv_w"]).reshape(3, D),
        "w_attn_o": f(inputs["w_attn_o"]).reshape(D, D),
        "w_conv_o": f(inputs["w_conv_o"]).reshape(D, D),
        "w_out": f(inputs["w_out"]).reshape(D, D),
        "norm2_g": f(inputs["norm2_g"]).reshape(D),
        "w_query": f(inputs["w_query"]).reshape(D, 2048),
        "sub_keys": f(inputs["sub_keys"]).reshape(8, 2, 128, 128),
        "expert_u": f(inputs["expert_u"]).reshape(16384, D),
        "expert_v": f(inputs["expert_v"]).reshape(16384, D),
        "final_g": f(inputs["final_g"]).reshape(D),
    }
    xs = f(inputs["x"])
    in_maps = []
    for b in range(8):
        m = dict(shared)
        m["x"] = np.ascontiguousarray(xs[b])
        in_maps.append(m)
    return in_maps


def kernel(**inputs):
    if "nc" not in _NC_CACHE:
        _NC_CACHE["nc"] = build_nc()
    nc = _NC_CACHE["nc"]
    in_maps = _prep_inputs(inputs)
    res = run_bass_kernel_spmd(nc, in_maps, core_ids=list(range(8)))
    outs = [np.asarray(r["out"], dtype=np.float32).reshape(S, D) for r in res.results]
    return np.stack(outs, axis=0)
```

```python
import os
from contextlib import ExitStack

import numpy as np
import concourse.bass as bass
import concourse.mybir as mybir
from concourse.bass_utils import run_bass_kernel_spmd

F32 = mybir.dt.float32
BF16 = mybir.dt.bfloat16
U8 = mybir.dt.uint8
U32 = mybir.dt.uint32
I32 = mybir.dt.int32
AF = mybir.ActivationFunctionType
ALU = mybir.AluOpType
AX = mybir.AxisListType

S = 2048
D = 1024
NT = 16
EPS = 1e-6
LAM_INIT = 0.2
NB = 128
TGE = 256
NGE = S // TGE


class _Op:
    __slots__ = ("eng", "fn", "deps", "key", "dma_cnt", "need_inc", "cnt", "barrier", "snap")

    def __init__(self, eng, fn, deps, key):
        self.eng = eng
        self.fn = fn
        self.deps = deps
        self.key = key
        self.dma_cnt = 0
        self.need_inc = False
        self.cnt = 0
        self.barrier = False
        self.snap = None


class Prog:
    ENGS = ("pe", "act", "dve", "pool", "sp")

    def __init__(self):
        self.ops = []
        self.last_w = {}
        self.readers = {}
        self.dma_count = {}

    def op(self, eng, fn, r=(), w=(), key=None):
        idx = len(self.ops)
        deps = set()
        for c in r:
            if c in self.last_w:
                deps.add(self.last_w[c])
        for c in w:
            if c in self.last_w:
                deps.add(self.last_w[c])
            for x in self.readers.get(c, ()):
                deps.add(x)
        o = _Op(eng, fn, deps, key)
        if key is not None:
            self.dma_count[key] = self.dma_count.get(key, 0) + 16
            o.dma_cnt = self.dma_count[key]
        self.ops.append(o)
        for c in w:
            self.last_w[c] = idx
            self.readers[c] = []
        for c in r:
            if c not in w:
                self.readers.setdefault(c, []).append(idx)
        return idx

    def barrier(self):
        o = _Op(None, None, set(), None)
        o.barrier = True
        self.ops.append(o)
        self.last_w = {}
        self.readers = {}

    def finalize(self):
        ops = self.ops
        for o in ops:
            if o.barrier:
                continue
            for d in o.deps:
                dep = ops[d]
                if dep.key is None and not (o.eng == "pe" and dep.eng == "pe"):
                    dep.need_inc = True
        last_on = {e: None for e in self.ENGS}
        for i, o in enumerate(ops):
            if o.barrier:
                for e in self.ENGS:
                    if last_on[e] is not None:
                        ops[last_on[e]].need_inc = True
            elif o.key is None:
                last_on[o.eng] = i
        cnt = {e: 0 for e in self.ENGS}
        dcnt = {}
        for o in ops:
            if o.barrier:
                o.snap = (dict(cnt), dict(dcnt))
                continue
            if o.key is not None:
                dcnt[o.key] = o.dma_cnt
            elif o.need_inc:
                cnt[o.eng] += 1
                o.cnt = cnt[o.eng]

    def emit(self, eng_name, e, sems):
        ops = self.ops
        seen = {}

        def wait(s, v):
            if v > 0 and seen.get(s, 0) < v:
                e.wait_ge(sems[s], v)
                seen[s] = v

        for o in ops:
            if o.barrier:
                cnt, dcnt = o.snap
                for b, v in cnt.items():
                    if b != eng_name:
                        wait(("eng", b), v)
                for k, v in dcnt.items():
                    wait(("dma", k), v)
                continue
            if o.eng != eng_name:
                continue
            waits = {}
            for d in o.deps:
                dep = ops[d]
                if dep.key is not None:
                    s, v = ("dma", dep.key), dep.dma_cnt
                else:
                    if eng_name == "pe" and dep.eng == "pe":
                        continue
                    s, v = ("eng", dep.eng), dep.cnt
                if waits.get(s, 0) < v:
                    waits[s] = v
            for s, v in waits.items():
                wait(s, v)
            ins = o.fn(e)
            if o.key is not None:
                ins.then_inc(sems[("dma", o.key)], 16)
            elif o.need_inc:
                ins.then_inc(sems[("eng", eng_name)], 1)


class Arena:
    def __init__(self, ap, size):
        self.ap = ap
        self.size = size
        self.off = 0

    def alloc(self, nbytes, dtype):
        o = (self.off + 63) // 64 * 64
        assert o + nbytes <= self.size, f"SBUF arena overflow {o + nbytes} > {self.size}"
        self.off = o + nbytes
        return self.ap[:, o:o + nbytes].bitcast(dtype)


def build_nc(debug=False, stop_after=None):
    nc = bass.Bass("TRN2", target_bir_lowering=False)

    def din(name, shape, dtype=F32):
        return nc.dram_tensor(name, shape, dtype, kind="ExternalInput").ap()

    x = din("x", [S, D])
    norm1_g = din("norm1_g", [D])
    w_in = din("w_in", [D, 8192])
    lambda_qk = din("lambda_qk", [256])
    subln_g = din("subln_g", [128])
    conv_w = din("conv_w", [3, D])
    w_attn_o = din("w_attn_o", [D, D])
    w_conv_o = din("w_conv_o", [D, D])
    w_out = din("w_out", [D, D])
    norm2_g = din("norm2_g", [D])
    w_query = din("w_query", [D, 2048])
    sub_keys = din("sub_keys", [8, 2, 128, 128])
    expert_u = din("expert_u", [16384, D])
    expert_v = din("expert_v", [16384, D])
    final_g = din("final_g", [D])
    out = nc.dram_tensor("out", [S, D], F32, kind="ExternalOutput").ap()
    skind = "ExternalOutput" if debug else "Internal"
    uT_scr = nc.dram_tensor("uT_scr", [NB, 128, 1024], BF16, kind=skind).ap()
    v_scr = nc.dram_tensor("v_scr", [16384, D], BF16, kind=skind).ap()
    h_scr = nc.dram_tensor("h_scr", [S, D], F32, kind=skind).ap()
    wq_scr = nc.dram_tensor("wq_scr", [D, 2048], BF16, kind="Internal").ap()

    if debug:
        dbg_nT = nc.dram_tensor("dbg_nT", [128, 8 * S], BF16, kind="ExternalOutput").ap()
        dbg_ycT = nc.dram_tensor("dbg_ycT", [128, 8 * S], BF16, kind="ExternalOutput").ap()
        dbg_attnT = nc.dram_tensor("dbg_attnT", [128, 8 * S], BF16, kind="ExternalOutput").ap()
        dbg_mg = nc.dram_tensor("dbg_mg", [128, 8 * S], BF16, kind="ExternalOutput").ap()
        dbg_small = nc.dram_tensor("dbg_small", [128, 256], F32, kind="ExternalOutput").ap()

    P = Prog()
    ARENA_BYTES = 204 * 1024

    with ExitStack() as es:
        arena_t = es.enter_context(nc.sbuf_tensor("arena", [128, ARENA_BYTES], U8))
        ps = es.enter_context(nc.psum_tensor("ps", [128, 4096], F32))
        A = Arena(arena_t, ARENA_BYTES)

        def bank(b):
            return ps[:, b * 512:(b + 1) * 512]

        def bank_bf(b):
            return bank(b).bitcast(BF16)

        ident_bf = A.alloc(256, BF16)
        ident_f = A.alloc(512, F32)
        mask_tri = A.alloc(256, BF16)
        iota_f = A.alloc(512, F32)
        iota_i = A.alloc(512, I32)
        diff_i = A.alloc(512, I32)
        diff_f = A.alloc(512, F32)
        g1b = A.alloc(4096, F32)
        g2b = A.alloc(4096, F32)
        gfb = A.alloc(4096, F32)
        subgb = A.alloc(512, F32)
        cw = A.alloc(96, F32).rearrange("p (c j) -> p c j", j=3)
        lq = A.alloc(1024, F32)
        lam_s = A.alloc(64, F32)
        skT = A.alloc(4096, BF16).rearrange("p (g n) -> p g n", n=128)
        small = A.alloc(64 * 4 * 4, F32)
        junk = A.alloc(2048, BF16)
        junk_f = A.alloc(512, F32)
        smp = A.alloc(6 * 16 * 4, F32)
        iota_b = A.alloc(256, BF16)
        c_mhalf = A.alloc(4, F32)
        c_e = A.alloc(512, F32)
        const_mark = A.off

        sm_ctr = [0]

        def sm():
            k = sm_ctr[0] % 256
            sm_ctr[0] += 1
            return small[:, k:k + 1], f"sm{k}"

        P.op("sp", lambda e: e.dma_start(out=g1b, in_=norm1_g.partition_broadcast(128)), w=["g1b"], key="c0_0")
        P.op("sp", lambda e: e.dma_start(out=g2b, in_=norm2_g.partition_broadcast(128)), w=["g2b"], key="c0_1")
        P.op("sp", lambda e: e.dma_start(out=gfb, in_=final_g.partition_broadcast(128)), w=["gfb"], key="c0_2")
        P.op("sp", lambda e: e.dma_start(out=subgb, in_=subln_g.partition_broadcast(128)), w=["subgb"], key="c0_3")
        P.op("sp", lambda e: e.dma_start(out=lq, in_=lambda_qk.partition_broadcast(128)), w=["lq"], key="c0_4")
        for j in range(3):
            for c in range(8):
                P.op("sp", lambda e, j=j, c=c: e.dma_start(out=cw[:, c, j:j + 1],
                                                           in_=conv_w[j, c * 128:(c + 1) * 128].rearrange("(p o) -> p o", o=1)),
                     w=["cw"], key="c0_5")
        P.op("dve", lambda e: e.memset(c_mhalf, -0.5), w=["c_mhalf"])
        P.op("dve", lambda e: e.memset(c_e, float(np.float32(np.e))), w=["c_e"])
        P.op("pool", lambda e: e.iota(iota_i, pattern=[[1, 128]], base=0, channel_multiplier=0), w=["iota_i"])
        P.op("pool", lambda e: e.iota(diff_i, pattern=[[1, 128]], base=0, channel_multiplier=-1), w=["diff_i"])
        P.op("dve", lambda e: e.tensor_copy(iota_f, iota_i), r=["iota_i"], w=["iota_f"])
        P.op("dve", lambda e: e.tensor_copy(diff_f, diff_i), r=["diff_i"], w=["diff_f"])
        P.op("dve", lambda e: e.tensor_copy(iota_b, iota_i), r=["iota_i"], w=["iota_b"])
        P.op("dve", lambda e: e.tensor_single_scalar(ident_f, diff_f, 0.0, ALU.is_equal), r=["diff_f"], w=["ident_f"])
        P.op("dve", lambda e: e.tensor_single_scalar(ident_bf, diff_f, 0.0, ALU.is_equal), r=["diff_f"], w=["ident_bf"])
        P.op("dve", lambda e: e.tensor_single_scalar(mask_tri, diff_f, 0.0, ALU.is_ge), r=["diff_f"], w=["mask_tri"])
        P.op("dve", lambda e: e.tensor_scalar(subgb, subgb, 1.0 - LAM_INIT, None, ALU.mult), r=["subgb"], w=["subgb"])
        P.op("dve", lambda e: e.tensor_tensor_reduce(out=junk_f[:, 0:64], in0=lq[:, 0:64], in1=lq[:, 64:128],
                                                     scale=1.0, scalar=0.0, op0=ALU.mult, op1=ALU.add,
                                                     accum_out=lam_s[:, 0:1]), r=["lq"], w=["lam0"])
        P.op("dve", lambda e: e.tensor_tensor_reduce(out=junk_f[:, 64:128], in0=lq[:, 128:192], in1=lq[:, 192:256],
                                                     scale=1.0, scalar=0.0, op0=ALU.mult, op1=ALU.add,
                                                     accum_out=lam_s[:, 1:2]), r=["lq"], w=["lam1"])
        P.op("act", lambda e: e.activation(lam_s[:, 2:4], lam_s[:, 0:2], AF.Exp), r=["lam0", "lam1"], w=["lam2"])
        P.op("dve", lambda e: e.tensor_tensor(lam_s[:, 4:5], lam_s[:, 3:4], lam_s[:, 2:3], ALU.subtract),
             r=["lam2"], w=["lam4"])
        neglam = lam_s[:, 5:6]
        P.op("dve", lambda e: e.tensor_scalar(neglam, lam_s[:, 4:5], -LAM_INIT, None, ALU.add),
             r=["lam4"], w=["neglam"])

        m0 = A.off
        sk_nat = A.alloc(4096, BF16).rearrange("p (g k) -> p g k", k=128)
        P.op("pool", lambda e: e.dma_start(out=sk_nat, in_=sub_keys.rearrange("h m n k -> n (h m) k")),
             w=["sk_nat"], key="c1")
        for g4 in range(4):
            for q in range(4):
                g = g4 * 4 + q
                P.op("pe", lambda e, g=g, q=q, g4=g4: e.transpose(bank_bf(g4)[:, q * 128:(q + 1) * 128], sk_nat[:, g, :], ident_bf),
                     r=["sk_nat", "ident_bf"], w=[f"ps{g4}"])
            P.op("act", lambda e, g4=g4: e.activation(skT[:, g4 * 4:(g4 + 1) * 4, :],
                                                      bank_bf(g4)[:, 0:512].rearrange("p (g n) -> p g n", n=128), AF.Copy),
                 r=[f"ps{g4}"], w=["skT"])
        P.barrier()
        A.off = m0

        def rstd_from(ssq_ap, ssq_cell, n):
            lnv, c1 = sm()
            rs, c2 = sm()
            P.op("act", lambda e: e.activation(lnv, ssq_ap, AF.Ln, bias=EPS, scale=1.0 / n), r=[ssq_cell], w=[c1])
            P.op("act", lambda e: e.activation(rs, lnv, AF.Exp, scale=-0.5), r=[c1], w=[c2])
            return rs, c2

        smp_ctr = [0]

        def smp_slot():
            k = smp_ctr[0] % 16
            smp_ctr[0] += 1
            return smp[:, k * 6:k * 6 + 2], smp[:, k * 6 + 2:k * 6 + 4], smp[:, k * 6 + 4:k * 6 + 6], f"smp{k}"

        def rstd_pool(ssq_ap, ssq_cell, n):
            mse, c1 = sm()
            rs, c2 = sm()
            P.op("dve", lambda e: e.tensor_scalar(mse, ssq_ap, 1.0 / n, EPS, ALU.mult, ALU.add), r=[ssq_cell], w=[c1])
            P.op("pool", lambda e: e.tensor_tensor(rs, mse, c_mhalf, ALU.pow), r=[c1, "c_mhalf"], w=[c2])
            return rs, c2

        mP = A.off
        ub = [A.alloc(2048, BF16) for _ in range(3)]
        uTo = [A.alloc(2048, BF16) for _ in range(3)]
        conv_jobs = []
        if stop_after != "M":
            for r in range(8):
                conv_jobs.append(lambda r=r: P.op("pool", lambda e: e.dma_start(out=wq_scr[r * 128:(r + 1) * 128, :], in_=w_query[r * 128:(r + 1) * 128, :]),
                                                  w=["wq_scr"], key="wqconv"))
            for r in range(32):
                conv_jobs.append(lambda r=r: P.op("pool", lambda e: e.dma_start(out=v_scr[r * 512:(r + 1) * 512, :], in_=expert_v[r * 512:(r + 1) * 512, :]),
                                                  w=["v_scr"], key="vconv"))

        def p_block(b):
            nonlocal sb_ctr
            s = b % 3
            pb = sb_ctr % 4
            sb_ctr += 1
            P.op("pool", lambda e: e.dma_start(out=ub[s], in_=expert_u[b * 128:(b + 1) * 128, :]), w=[f"ub{s}"], key=f"ub{s}")
            for c in range(8):
                P.op("pe", lambda e, c=c: e.transpose(bank_bf(pb)[:, c * 128:(c + 1) * 128], ub[s][:, c * 128:(c + 1) * 128], ident_bf),
                     r=[f"ub{s}"], w=[f"ps{pb}"])
            if b % 2 == 0:
                P.op("act", lambda e: e.activation(uTo[s], bank_bf(pb), AF.Copy), r=[f"ps{pb}"], w=[f"uTo{s}"])
            else:
                P.op("dve", lambda e: e.tensor_copy(uTo[s], bank_bf(pb)), r=[f"ps{pb}"], w=[f"uTo{s}"])
            P.op("sp", lambda e: e.dma_start(out=uT_scr[b], in_=uTo[s]), r=[f"uTo{s}"], w=["uT_scr"], key=f"uTo{s}")

        p_next = [0]

        nT = A.alloc(32768, BF16).rearrange("p (c t) -> p c t", t=S)
        ycT = A.alloc(32768, BF16).rearrange("p (c t) -> p c t", t=S)
        attnT = A.alloc(32768, BF16).rearrange("p (c t) -> p c t", t=S)
        NW = 8
        wring = [A.alloc(2048, BF16).rearrange("p (c n) -> p c n", n=128) for _ in range(NW)]
        w_ctr = [0]

        def load_panel(src_ap):
            s = w_ctr[0] % NW
            w_ctr[0] += 1
            P.op("pool", lambda e: e.dma_start(out=wring[s], in_=src_ap), w=[f"w{s}"], key=f"w{s}")
            return wring[s], f"w{s}"

        w_in_v = w_in.rearrange("(c p) n -> p c n", p=128)
        ps_ctr = [0]

        mS = A.off
        xt = [A.alloc(4096, F32) for _ in range(2)]
        nb = [A.alloc(2048, BF16) for _ in range(2)]
        for tt in range(NT):
            s = tt % 2
            pb = tt % 4
            P.op("sp", lambda e, tt=tt, s=s: e.dma_start(out=xt[s], in_=x[tt * 128:(tt + 1) * 128, :]), w=[f"xt{s}"], key=f"xt{s}")
            ssq, cq = sm()
            P.op("dve", lambda e, s=s, ssq=ssq: e.tensor_tensor_reduce(out=junk, in0=xt[s], in1=xt[s], scale=1.0, scalar=0.0,
                                                                       op0=ALU.mult, op1=ALU.add, accum_out=ssq),
                 r=[f"xt{s}"], w=[cq])
            rs, cr = rstd_from(ssq, cq, D)
            P.op("dve", lambda e, s=s, rs=rs: e.scalar_tensor_tensor(out=nb[s], in0=xt[s], scalar=rs, in1=g1b, op0=ALU.mult, op1=ALU.mult),
                 r=[f"xt{s}", cr, "g1b"], w=[f"nb{s}"])
            for c in range(8):
                P.op("pe", lambda e, s=s, c=c, pb=pb: e.transpose(bank_bf(pb)[:, c * 128:(c + 1) * 128],
                                                                nb[s][:, c * 128:(c + 1) * 128], ident_bf),
                     r=[f"nb{s}"], w=[f"ps{pb}"])
            P.op("act", lambda e, tt=tt, pb=pb: e.activation(nT[:, :, tt * 128:(tt + 1) * 128],
                                                            bank_bf(pb).rearrange("p (c t) -> p c t", t=128), AF.Copy),
                 r=[f"ps{pb}"], w=[f"nT{tt // 4}"])
        A.off = mS

        uconv = [A.alloc(2050 * 4, F32) for _ in range(2)]
        tmp1 = [A.alloc(2048, F32) for _ in range(2)]
        zt = [A.alloc(2048, F32) for _ in range(2)]
        for k in range(2):
            P.op("dve", lambda e, k=k: e.memset(uconv[k][:, 0:2], 0.0), w=[f"uconv{k}"])
        step = 0
        def conv_panels(cch):
            return (load_panel(w_in_v[:, :, 3072 + cch * 128:3072 + (cch + 1) * 128]),
                    load_panel(w_in_v[:, :, 4096 + cch * 128:4096 + (cch + 1) * 128]),
                    load_panel(w_in_v[:, :, 5120 + cch * 128:5120 + (cch + 1) * 128]))

        cpan = {0: conv_panels(0)}
        for cch in range(8):
            if cch + 1 < 8:
                cpan[cch + 1] = conv_panels(cch + 1)
            (wcb, kcb), (wcc, kcc), (wcx, kcx) = cpan.pop(cch)
            uc = uconv[cch % 2]
            ucc = f"uconv{cch % 2}"
            for tg in range(4):
                bset = (step % 2) * 3
                step += 1
                bA, bB, bC = bset, bset + 1, bset + 2
                for (bk, wp, wk) in ((bA, wcx, kcx), (bB, wcc, kcc), (bC, wcb, kcb)):
                    for c in range(8):
                        P.op("pe", lambda e, bk=bk, wp=wp, c=c, tg=tg: e.matmul(bank(bk), lhsT=wp[:, c, :],
                                                                            rhs=nT[:, c, tg * 512:(tg + 1) * 512],
                                                                            start=(c == 0), stop=(c == 7)),
                             r=[wk, f"nT{tg}"], w=[f"ps{bk}"])
                s = tg % 2
                o0 = tg * 512
                P.op("act", lambda e, s=s, bA=bA: e.activation(tmp1[s], bank(bA), AF.Copy), r=[f"ps{bA}"], w=[f"tmp1{s}"])
                P.op("dve", lambda e, s=s, bB=bB, uc=uc, o0=o0: e.tensor_tensor(uc[:, 2 + o0:2 + o0 + 512], bank(bB), tmp1[s], ALU.mult),
                     r=[f"ps{bB}", f"tmp1{s}"], w=[ucc])
                P.op("dve", lambda e, s=s, uc=uc, o0=o0, cch=cch: e.tensor_scalar(zt[s], uc[:, 2 + o0:2 + o0 + 512], cw[:, cch, 2:3], None, ALU.mult),
                     r=[ucc, "cw"], w=[f"zt{s}"])
                P.op("dve", lambda e, s=s, uc=uc, o0=o0, cch=cch: e.scalar_tensor_tensor(out=zt[s], in0=uc[:, 1 + o0:1 + o0 + 512], scalar=cw[:, cch, 1:2],
                                                                                       in1=zt[s], op0=ALU.mult, op1=ALU.add),
                     r=[ucc, "cw", f"zt{s}"], w=[f"zt{s}"])
                P.op("dve", lambda e, s=s, uc=uc, o0=o0, cch=cch: e.scalar_tensor_tensor(out=zt[s], in0=uc[:, o0:o0 + 512], scalar=cw[:, cch, 0:1],
                                                                                       in1=zt[s], op0=ALU.mult, op1=ALU.add),
                     r=[ucc, "cw", f"zt{s}"], w=[f"zt{s}"])
                P.op("dve", lambda e, s=s, bC=bC, cch=cch, o0=o0: e.tensor_tensor(ycT[:, cch, o0:o0 + 512], bank(bC), zt[s], ALU.mult),
                     r=[f"ps{bC}", f"zt{s}"], w=[f"ycT{tg}"])
        A.off = mS

        qT = A.alloc(4096, BF16)
        kT = A.alloc(4096, BF16)
        vsb = A.alloc(16 * 130 * 2, BF16).rearrange("p (t e) -> p t e", e=130)
        NPT = 6
        pt = [A.alloc(1024, BF16) for _ in range(NPT)]
        of1 = [A.alloc(512, F32) for _ in range(4)]
        of2 = [A.alloc(512, F32) for _ in range(4)]
        onb = [A.alloc(256, BF16) for _ in range(4)]
        P.op("dve", lambda e: e.memset(vsb[:, :, 128:129], 1.0), w=["vsb1"])
        pt_ctr = 0
        sb_ctr = 0
        fin_ctr = 0

        def oacc(a):
            bk = 4 + a // 2
            o = (a % 2) * 130
            return bank(bk)[:, o:o + 129], f"ps{bk}"

        def head_panels(h):
            return (load_panel(w_in_v[:, :, h * 128:(h + 1) * 128]),
                    load_panel(w_in_v[:, :, 1024 + h * 128:1024 + (h + 1) * 128]),
                    load_panel(w_in_v[:, :, 2048 + h * 128:2048 + (h + 1) * 128]))

        hpan = {0: head_panels(0)}
        for h in range(8):
            if h + 1 < 8:
                hpan[h + 1] = head_panels(h + 1)
            for _ in range(5):
                if conv_jobs:
                    conv_jobs.pop(0)()
            (wq, kq), (wk, kk), (wv, kv) = hpan.pop(h)
            for (dst, dname, wp, wkey) in ((qT, "qT", wq, kq), (kT, "kT", wk, kk)):
                for tg in range(4):
                    bk = sb_ctr % 4
                    sb_ctr += 1
                    for c in range(8):
                        P.op("pe", lambda e, bk=bk, wp=wp, c=c, tg=tg: e.matmul(bank(bk), lhsT=wp[:, c, :],
                                                                            rhs=nT[:, c, tg * 512:(tg + 1) * 512],
                                                                            start=(c == 0), stop=(c == 7)),
                             r=[wkey, f"nT{tg}"], w=[f"ps{bk}"])
                    P.op("act", lambda e, bk=bk, dst=dst, tg=tg: e.activation(dst[:, tg * 512:(tg + 1) * 512], bank(bk), AF.Copy),
                         r=[f"ps{bk}"], w=[f"{dname}{tg}"])
            for t4 in range(4):
                bk = sb_ctr % 4
                sb_ctr += 1
                for tq in range(4):
                    tt = t4 * 4 + tq
                    for c in range(8):
                        P.op("pe", lambda e, bk=bk, tq=tq, tt=tt, c=c, wv=wv: e.matmul(bank(bk)[:, tq * 128:(tq + 1) * 128],
                                                                                   lhsT=nT[:, c, tt * 128:(tt + 1) * 128], rhs=wv[:, c, :],
                                                                                   start=(c == 0), stop=(c == 7)),
                             r=[kv, f"nT{t4}"], w=[f"ps{bk}"])
                P.op("dve", lambda e, bk=bk, t4=t4: e.tensor_copy(vsb[:, t4 * 4:(t4 + 1) * 4, 0:128],
                                                                bank(bk).rearrange("p (t e) -> p t e", e=128)),
                     r=[f"ps{bk}"], w=["vsb"])
            for qg in range(4):
                steps = [(j, m) for j in range(4 * qg + 4) for m in range(2)]

                def emit_S(j, m, qg=qg):
                    nonlocal sb_ctr, pt_ctr
                    col0 = max(qg * 512, j * 128)
                    ncols = (qg + 1) * 512 - col0
                    diag = j >= 4 * qg
                    bk = sb_ctr % 4
                    sb_ctr += 1
                    sl = pt_ctr % NPT
                    pt_ctr += 1
                    P.op("pe", lambda e, bk=bk, m=m, j=j, col0=col0, ncols=ncols: e.matmul(
                        bank(bk)[:, 0:ncols], lhsT=kT[64 * m:64 * m + 64, j * 128:(j + 1) * 128],
                        rhs=qT[64 * m:64 * m + 64, col0:col0 + ncols], start=True, stop=True),
                         r=[f"kT{j // 4}", f"qT{qg}"], w=[f"ps{bk}"])
                    P.op("act", lambda e, bk=bk, sl=sl, ncols=ncols: e.activation(pt[sl][:, 0:ncols], bank(bk)[:, 0:ncols], AF.Exp, scale=0.125),
                         r=[f"ps{bk}"], w=[f"pt{sl}"])
                    if diag:
                        P.op("dve", lambda e, sl=sl: e.tensor_tensor(pt[sl][:, 0:128], pt[sl][:, 0:128], mask_tri, ALU.mult),
                             r=[f"pt{sl}"], w=[f"pt{sl}"])
                    return (sl, col0)

                def emit_PV(j, m, st, qg=qg):
                    sl, col0 = st
                    for i in range(max(4 * qg, j), 4 * qg + 4):
                        off = i * 128 - col0
                        aidx = m * 4 + (i - 4 * qg)
                        oa, oc = oacc(aidx)
                        P.op("pe", lambda e, oa=oa, sl=sl, off=off, j=j, i=i, aidx=aidx: e.matmul(
                            oa, lhsT=pt[sl][:, off:off + 128], rhs=vsb[:, j, 0:129],
                            start=(j == 0 and aidx % 2 == 0), stop=(j == i), skip_group_check=True),
                             r=[f"pt{sl}", "vsb", "vsb1"], w=[oc])

                DEPTH = 2
                sts = {}
                for k in range(min(DEPTH, len(steps))):
                    sts[k] = emit_S(*steps[k])
                for k in range(len(steps)):
                    if k + DEPTH < len(steps):
                        sts[k + DEPTH] = emit_S(*steps[k + DEPTH])
                    emit_PV(steps[k][0], steps[k][1], sts[k])
                fin = []
                for il in range(4):
                    o0a, c0 = oacc(il)
                    o1a, c1 = oacc(4 + il)
                    r0, cr0 = sm()
                    r1, cr1 = sm()
                    r1n, cr1n = sm()
                    P.op("dve", lambda e, r0=r0, o0a=o0a: e.reciprocal(r0, o0a[:, 128:129]), r=[c0], w=[cr0])
                    P.op("dve", lambda e, r1=r1, o1a=o1a: e.reciprocal(r1, o1a[:, 128:129]), r=[c1], w=[cr1])
                    fin.append((o0a, c0, o1a, c1, r0, cr0, r1, cr1, r1n, cr1n))
                for il in range(4):
                    (o0a, c0, o1a, c1, r0, cr0, r1, cr1, r1n, cr1n) = fin[il]
                    P.op("dve", lambda e, r1=r1, r1n=r1n: e.tensor_tensor(r1n, r1, neglam, ALU.mult), r=[cr1, "neglam"], w=[cr1n])
                    P.op("dve", lambda e, il=il, o0a=o0a, r0=r0: e.tensor_scalar(of1[il], o0a[:, 0:128], r0, None, ALU.mult),
                         r=[c0, cr0], w=[f"of1{il}"])
                for il in range(4):
                    (o0a, c0, o1a, c1, r0, cr0, r1, cr1, r1n, cr1n) = fin[il]
                    P.op("dve", lambda e, il=il, o1a=o1a, r1n=r1n: e.scalar_tensor_tensor(out=of2[il], in0=o1a[:, 0:128], scalar=r1n, in1=of1[il],
                                                                                       op0=ALU.mult, op1=ALU.add),
                         r=[c1, cr1n, f"of1{il}"], w=[f"of2{il}"])
                sq = []
                for il in range(4):
                    ssq, cq = sm()
                    P.op("dve", lambda e, il=il, ssq=ssq: e.tensor_tensor_reduce(out=junk_f, in0=of2[il], in1=of2[il], scale=1.0, scalar=0.0,
                                                                                 op0=ALU.mult, op1=ALU.add, accum_out=ssq),
                         r=[f"of2{il}"], w=[cq])
                    sq.append((ssq, cq))
                rss = [rstd_from(ssq, cq, 128) for (ssq, cq) in sq]
                for il in range(4):
                    rs, cr = rss[il]
                    P.op("dve", lambda e, il=il, rs=rs: e.scalar_tensor_tensor(out=onb[il], in0=of2[il], scalar=rs, in1=subgb,
                                                                             op0=ALU.mult, op1=ALU.mult),
                         r=[f"of2{il}", cr, "subgb"], w=[f"onb{il}"])
                for il in range(4):
                    i = 4 * qg + il
                    bk = sb_ctr % 4
                    sb_ctr += 1
                    P.op("pe", lambda e, bk=bk, il=il: e.transpose(bank_bf(bk)[:, 0:128], onb[il], ident_bf), r=[f"onb{il}"], w=[f"ps{bk}"])
                    P.op("act", lambda e, bk=bk, h=h, i=i: e.activation(attnT[:, h, i * 128:(i + 1) * 128], bank_bf(bk)[:, 0:128], AF.Copy),
                         r=[f"ps{bk}"], w=[f"attnT{i // 4}"])
                if stop_after != "M":
                    for _ in range(4):
                        if p_next[0] < NB:
                            p_block(p_next[0])
                            p_next[0] += 1
        A.off = mS

        mergedT = A.alloc(32768, BF16).rearrange("p (c t) -> p c t", t=S)
        sga = [A.alloc(2048, F32) for _ in range(2)]
        sgc = [A.alloc(2048, F32) for _ in range(2)]
        t1 = [A.alloc(2048, F32) for _ in range(2)]
        t2 = [A.alloc(2048, F32) for _ in range(2)]
        wao_v = w_attn_o.rearrange("(c p) n -> p c n", p=128)
        wco_v = w_conv_o.rearrange("(c p) n -> p c n", p=128)
        step = 0
        def merge_panels(cch):
            cs = slice(cch * 128, (cch + 1) * 128)
            return (load_panel(wao_v[:, :, cs]), load_panel(wco_v[:, :, cs]),
                    load_panel(w_in_v[:, :, 6144 + cch * 128:6144 + (cch + 1) * 128]),
                    load_panel(w_in_v[:, :, 7168 + cch * 128:7168 + (cch + 1) * 128]))

        mpan = {0: merge_panels(0)}
        for cch in range(8):
            if cch + 1 < 8:
                mpan[cch + 1] = merge_panels(cch + 1)
            (wao, kao), (wco, kco), (wga, kga), (wgc, kgc) = mpan.pop(cch)
            for tg in range(4):
                bset = (step % 2) * 4
                s = step % 2
                step += 1
                bA, bB, bC, bD = bset, bset + 1, bset + 2, bset + 3
                ts = slice(tg * 512, (tg + 1) * 512)
                for (bk, wp, wk_, src, sname) in ((bA, wao, kao, attnT, "attnT"), (bB, wco, kco, ycT, "ycT"),
                                                 (bC, wga, kga, nT, "nT"), (bD, wgc, kgc, nT, "nT")):
                    for c in range(8):
                        P.op("pe", lambda e, bk=bk, wp=wp, c=c, src=src, ts=ts: e.matmul(bank(bk), lhsT=wp[:, c, :], rhs=src[:, c, ts],
                                                                                     start=(c == 0), stop=(c == 7)),
                             r=[wk_, f"{sname}{tg}"], w=[f"ps{bk}"])
                P.op("act", lambda e, s=s, bC=bC: e.activation(sga[s], bank(bC), AF.Sigmoid), r=[f"ps{bC}"], w=[f"sga{s}"])
                P.op("act", lambda e, s=s, bD=bD: e.activation(sgc[s], bank(bD), AF.Sigmoid), r=[f"ps{bD}"], w=[f"sgc{s}"])
                P.op("dve", lambda e, s=s, bA=bA: e.tensor_tensor(t1[s], bank(bA), sga[s], ALU.mult), r=[f"ps{bA}", f"sga{s}"], w=[f"t1{s}"])
                P.op("dve", lambda e, s=s, bB=bB: e.tensor_tensor(t2[s], bank(bB), sgc[s], ALU.mult), r=[f"ps{bB}", f"sgc{s}"], w=[f"t2{s}"])
                P.op("dve", lambda e, s=s, cch=cch, ts=ts: e.tensor_tensor(mergedT[:, cch, ts], t1[s], t2[s], ALU.add),
                     r=[f"t1{s}", f"t2{s}"], w=[f"mg{tg}"])
        P.barrier()
        if debug:
            P.op("sp", lambda e: e.dma_start(out=dbg_nT, in_=nT.rearrange("p c t -> p (c t)")), w=["dbg1"], key="dbg1")
            P.op("sp", lambda e: e.dma_start(out=dbg_ycT, in_=ycT.rearrange("p c t -> p (c t)")), w=["dbg2"], key="dbg2")
            P.op("sp", lambda e: e.dma_start(out=dbg_attnT, in_=attnT.rearrange("p c t -> p (c t)")), w=["dbg3"], key="dbg3")
            P.op("sp", lambda e: e.dma_start(out=dbg_mg, in_=mergedT.rearrange("p c t -> p (c t)")), w=["dbg4"], key="dbg4")
            P.op("sp", lambda e: e.dma_start(out=dbg_small, in_=small), w=["dbg5"], key="dbg5")
            P.barrier()
        A.off = mP
        wout = A.alloc(16384, BF16).rearrange("p (c n) -> p c n", n=1024)
        xt2 = [A.alloc(4096, F32) for _ in range(2)]
        ht = [A.alloc(4096, F32) for _ in range(2)]
        assert A.off <= mP + 65536
        for c in range(8):
            P.op("pool", lambda e, c=c: e.dma_start(out=wout[:, c, :], in_=w_out[c * 128:(c + 1) * 128, :]), w=["wout"], key="wout")
        for tt in range(NT):
            s = tt % 2
            P.op("sp", lambda e, tt=tt, s=s: e.dma_start(out=xt2[s], in_=x[tt * 128:(tt + 1) * 128, :]), w=[f"xt2{s}"], key=f"xt2{s}")
            for half in range(2):
                bk = (tt * 2 + half) % 8
                hs_ = slice(half * 512, (half + 1) * 512)
                for c in range(8):
                    P.op("pe", lambda e, bk=bk, c=c, tt=tt, hs_=hs_: e.matmul(bank(bk), lhsT=mergedT[:, c, tt * 128:(tt + 1) * 128],
                                                                          rhs=wout[:, c, hs_], start=(c == 0), stop=(c == 7)),
                         r=["wout", f"mg{tt // 4}"], w=[f"ps{bk}"])
                P.op("dve", lambda e, bk=bk, s=s, hs_=hs_: e.tensor_tensor(ht[s][:, hs_], bank(bk), xt2[s][:, hs_], ALU.add),
                     r=[f"ps{bk}", f"xt2{s}"], w=[f"ht{s}"])
            P.op("sp", lambda e, tt=tt, s=s: e.dma_start(out=h_scr[tt * 128:(tt + 1) * 128, :], in_=ht[s]), r=[f"ht{s}"], w=["h_scr"], key=f"ht{s}")
        P.barrier()
        A.off = mP

        if stop_after != "M":
            HB = NB // 2
            GTh = [A.alloc(HB * TGE * 2, BF16).rearrange("p (i t) -> p i t", t=TGE) for _ in range(2)]
            xn2T = [A.alloc(8 * TGE * 2, BF16).rearrange("p (c t) -> p c t", t=TGE) for _ in range(2)]
            hsb = [[A.alloc(4096, F32) for _ in range(2)] for _ in range(2)]
            IJGT = [A.alloc(3 * TGE * 4, F32).rearrange("p (q t) -> p q t", t=TGE) for _ in range(2)]
            IJb = [A.alloc(2 * TGE * 2, BF16).rearrange("p (q t) -> p q t", t=TGE) for _ in range(2)]
            qTa = A.alloc(16 * TGE * 2, BF16).rearrange("p (g t) -> p g t", t=TGE)
            S2 = A.alloc(8192, F32).rearrange("p (g n) -> p g n", n=128)
            cand2 = S2.rearrange("p g n -> p (g n)").rearrange("p (h c) -> p h c", c=256)
            Eo = cand2.rearrange("p h (k a) -> p h k a", a=16)
            xnb = [A.alloc(2048, BF16) for _ in range(2)]
            tv = A.alloc(1024, F32).rearrange("p (g k) -> p g k", k=16)
            ti = A.alloc(1024, U32).rearrange("p (g k) -> p g k", k=16)
            tif = A.alloc(1024, F32).rearrange("p (g k) -> p g k", k=16)
            cand = A.alloc(8192, F32).rearrange("p (h c) -> p h c", c=256)
            cv = A.alloc(512, F32).rearrange("p (h k) -> p h k", k=16)
            ci = A.alloc(512, U32).rearrange("p (h k) -> p h k", k=16)
            cia = A.alloc(512, U32).rearrange("p (h k) -> p h k", k=16)
            cib = A.alloc(512, U32).rearrange("p (h k) -> p h k", k=16)
            af_ = A.alloc(512, F32).rearrange("p (h k) -> p h k", k=16)
            bf_ = A.alloc(512, F32).rearrange("p (h k) -> p h k", k=16)
            IJG = A.alloc(3 * 512, F32).rearrange("p (q h k) -> p q h k", q=3, k=16)
            dg = A.alloc(512, F32).rearrange("p (h k) -> p h k", k=16)
            eg = A.alloc(512, F32).rearrange("p (h k) -> p h k", k=16)
            zs = A.alloc(32, F32)
            rz = A.alloc(32, F32)
            CH = 8
            NOH = 3
            oh1 = [A.alloc(CH * 64 * 2, BF16).rearrange("p (t i) -> p t i", i=64) for _ in range(NOH)]
            oh2 = [A.alloc(CH * 128 * 2, BF16).rearrange("p (t i) -> p t i", i=128) for _ in range(NOH)]
            oh2g = [A.alloc(CH * 128 * 2, BF16).rearrange("p (t i) -> p t i", i=128) for _ in range(NOH)]
            NU = 6
            uTb = [A.alloc(2048, BF16).rearrange("p (c e) -> p c e", e=128) for _ in range(NU)]
            vb = [A.alloc(2048, BF16) for _ in range(NU)]
            geb = [A.alloc(TGE * 2, BF16) for _ in range(3)]
            hab = [A.alloc(TGE * 2, BF16) for _ in range(3)]
            NWQ = 2
            wqr = [A.alloc(2048, BF16).rearrange("p (c n) -> p c n", n=128) for _ in range(NWQ)]
            wq_v = wq_scr.rearrange("(c p) n -> p c n", p=128)
            iota16 = iota_f[:, 0:16]
            st = {"wq": 0, "oh": 0, "blk": 0, "pp": 0}

            class RPool:
                def __init__(self, items):
                    self.free = list(items)

                def acquire(self):
                    return self.free.pop(0) if self.free else None

                def release(self, x):
                    self.free.append(x)

            bank_pool = RPool([6, 7])
            oh_pool = RPool(list(range(NOH)))
            wq_pool = RPool(list(range(NWQ)))

            def prep_topk(g):
                gb = g % 2
                th = []

                def e1(_unused):
                    ssq2, ln2, rs2, pc = smp_slot()
                    hss = [hsb[gb][tl] for tl in range(2)]
                    hcs = [f"hsb{gb}{tl}" for tl in range(2)]
                    for tl in range(2):
                        tt = g * 2 + tl
                        P.op("sp", lambda e, tl=tl, tt=tt: e.dma_start(out=hss[tl], in_=h_scr[tt * 128:(tt + 1) * 128, :]),
                             r=["h_scr"], w=[hcs[tl]], key=hcs[tl])
                    yield
                    yield
                    for tl in range(2):
                        P.op("dve", lambda e, tl=tl: e.tensor_tensor_reduce(out=junk, in0=hss[tl], in1=hss[tl], scale=1.0, scalar=0.0,
                                                                            op0=ALU.mult, op1=ALU.add, accum_out=ssq2[:, tl:tl + 1]),
                             r=[hcs[tl]], w=[pc + f"s{tl}"])
                    yield
                    P.op("act", lambda e: e.activation(ln2, ssq2, AF.Ln, bias=EPS, scale=1.0 / D), r=[pc + "s0", pc + "s1"], w=[pc + "l"])
                    P.op("act", lambda e: e.activation(rs2, ln2, AF.Exp, scale=-0.5), r=[pc + "l"], w=[pc + "r"])
                    yield
                    yield
                    for tl in range(2):
                        P.op("dve", lambda e, tl=tl: e.scalar_tensor_tensor(out=xnb[tl], in0=hss[tl], scalar=rs2[:, tl:tl + 1], in1=g2b,
                                                                            op0=ALU.mult, op1=ALU.mult),
                             r=[hcs[tl], pc + "r", "g2b"], w=[f"xnb{tl}"])
                    yield
                    for tl in range(2):
                        while True:
                            pb = bank_pool.acquire()
                            if pb is not None:
                                break
                            yield
                        for c in range(8):
                            P.op("pe", lambda e, c=c, tl=tl, pb=pb: e.transpose(bank_bf(pb)[:, c * 128:(c + 1) * 128],
                                                                             xnb[tl][:, c * 128:(c + 1) * 128], ident_bf),
                                 r=[f"xnb{tl}"], w=[f"ps{pb}"])
                        yield
                        P.op("act", lambda e, tl=tl, pb=pb: e.activation(xn2T[gb][:, :, tl * 128:(tl + 1) * 128],
                                                                        bank_bf(pb).rearrange("p (c t) -> p c t", t=128), AF.Copy),
                             r=[f"ps{pb}"], w=[f"xn2T{gb}"])
                        bank_pool.release(pb)

                def e2(grp):
                    while True:
                        sl = wq_pool.acquire()
                        if sl is not None:
                            break
                        yield
                    P.op("sp", lambda e: e.dma_start(out=wqr[sl], in_=wq_v[:, :, grp * 128:(grp + 1) * 128]), w=[f"wq{sl}"], key=f"wq{sl}")
                    yield
                    yield
                    yield
                    while True:
                        bk = bank_pool.acquire()
                        if bk is not None:
                            break
                        yield
                    for c in range(8):
                        P.op("pe", lambda e, c=c: e.matmul(bank(bk)[:, 0:TGE], lhsT=wqr[sl][:, c, :], rhs=xn2T[gb][:, c, :],
                                                           start=(c == 0), stop=(c == 7)),
                             r=[f"wq{sl}", f"xn2T{gb}"], w=[f"ps{bk}"])
                    wq_pool.release(sl)
                    yield
                    P.op("act", lambda e: e.activation(qTa[:, grp, :], bank(bk)[:, 0:TGE], AF.Copy), r=[f"ps{bk}"], w=[f"qTa{grp}"])
                    bank_pool.release(bk)

                def e3(tl, quad):
                    while True:
                        bk = bank_pool.acquire()
                        if bk is not None:
                            break
                        yield
                    grps = [quad * 4 + q for q in range(4)]
                    scs = {grp: bank(bk)[:, (grp % 4) * 128:(grp % 4 + 1) * 128] for grp in grps}
                    for grp in grps:
                        P.op("pe", lambda e, grp=grp: e.matmul(scs[grp], lhsT=qTa[:, grp, tl * 128:(tl + 1) * 128], rhs=skT[:, grp, :],
                                                               start=True, stop=True),
                             r=[f"qTa{grp}", "skT"], w=[f"ps{bk}"])
                    yield
                    for grp in grps:
                        P.op("dve", lambda e, grp=grp: e.max(out=tv[:, grp, 0:8], in_=scs[grp]), r=[f"ps{bk}"], w=[f"tva{grp}"])
                    for grp in grps:
                        P.op("dve", lambda e, grp=grp: e.max_index(out=ti[:, grp, 0:8], in_max=tv[:, grp, 0:8], in_values=scs[grp]),
                             r=[f"ps{bk}", f"tva{grp}"], w=[f"tia{grp}"])
                    for grp in grps:
                        P.op("dve", lambda e, grp=grp: e.match_replace(out=S2[:, grp, :], in_to_replace=tv[:, grp, 0:8], in_values=scs[grp],
                                                                     imm_value=-1e30),
                             r=[f"ps{bk}", f"tva{grp}"], w=[f"S2{grp}"])
                    for grp in grps:
                        P.op("dve", lambda e, grp=grp: e.max(out=tv[:, grp, 8:16], in_=S2[:, grp, :]), r=[f"S2{grp}"], w=[f"tvb{grp}"])
                    for grp in grps:
                        P.op("dve", lambda e, grp=grp: e.max_index(out=ti[:, grp, 8:16], in_max=tv[:, grp, 8:16], in_values=S2[:, grp, :]),
                             r=[f"S2{grp}", f"tvb{grp}"], w=[f"tib{grp}"])
                    bank_pool.release(bk)

                TVC = [f"tva{g_}" for g_ in range(16)] + [f"tvb{g_}" for g_ in range(16)]
                TIC = [f"tia{g_}" for g_ in range(16)] + [f"tib{g_}" for g_ in range(16)]
                S2C = [f"S2{g_}" for g_ in range(16)]
                CVC = [f"cva{h_}" for h_ in range(8)] + [f"cvb{h_}" for h_ in range(8)]
                CIC = [f"cia{h_}" for h_ in range(8)] + [f"cib{h_}" for h_ in range(8)]
                C2C = [f"cand2_{h_}" for h_ in range(8)]
                tvv = tv.rearrange("p (h m) k -> p h m k", m=2)
                tifv = tif.rearrange("p (h m) k -> p h m k", m=2)
                candv = cand.rearrange("p h (a b) -> p h a b", b=16)

                def e4a(tl):
                    yield
                    P.op("dve", lambda e: e.tensor_copy(tif, ti), r=TIC, w=["tif"])
                    P.op("dve", lambda e: e.tensor_tensor(candv, tvv[:, :, 0, :].unsqueeze(3).broadcast_to([128, 8, 16, 16]),
                                                          tvv[:, :, 1, :].unsqueeze(2).broadcast_to([128, 8, 16, 16]), ALU.add),
                         r=TVC, w=[f"cand{h_}" for h_ in range(8)])
                    for h in range(8):
                        P.op("dve", lambda e, h=h: e.max(out=cv[:, h, 0:8], in_=cand[:, h, :]), r=[f"cand{h}"], w=[f"cva{h}"])
                    for h in range(8):
                        P.op("dve", lambda e, h=h: e.max_index(out=ci[:, h, 0:8], in_max=cv[:, h, 0:8], in_values=cand[:, h, :]),
                             r=[f"cand{h}", f"cva{h}"], w=[f"cia{h}"])
                    for h in range(8):
                        P.op("dve", lambda e, h=h: e.match_replace(out=cand2[:, h, :], in_to_replace=cv[:, h, 0:8], in_values=cand[:, h, :],
                                                                 imm_value=-1e30), r=[f"cand{h}", f"cva{h}"], w=[f"cand2_{h}"] + S2C[2 * h:2 * h + 2])

                def e4b(tl):
                    yield
                    for h in range(8):
                        P.op("dve", lambda e, h=h: e.max(out=cv[:, h, 8:16], in_=cand2[:, h, :]), r=[f"cand2_{h}"], w=[f"cvb{h}"])
                    for h in range(8):
                        P.op("dve", lambda e, h=h: e.max_index(out=ci[:, h, 8:16], in_max=cv[:, h, 8:16], in_values=cand2[:, h, :]),
                             r=[f"cand2_{h}", f"cvb{h}"], w=[f"cib{h}"])
                    P.op("dve", lambda e: e.tensor_single_scalar(cia, ci, 4, ALU.logical_shift_right), r=CIC, w=["cia"])
                    P.op("dve", lambda e: e.tensor_single_scalar(cib, ci, 15, ALU.bitwise_and), r=CIC, w=["cib"])
                    P.op("dve", lambda e: e.tensor_copy(af_, cia), r=["cia"], w=["af"])
                    P.op("dve", lambda e: e.tensor_copy(bf_, cib), r=["cib"], w=["bf"])

                def e4c(tl):
                    io4 = iota16.unsqueeze(1).unsqueeze(1).broadcast_to([128, 8, 16, 16])
                    for (q, src, mm) in ((0, af_, 0), (1, bf_, 1)):
                        P.op("dve", lambda e, src=src: e.tensor_tensor(Eo, src.unsqueeze(3).broadcast_to([128, 8, 16, 16]), io4, ALU.is_equal),
                             r=["af", "bf", "iota_f"], w=C2C + S2C)
                        P.op("dve", lambda e, mm=mm: e.tensor_tensor(Eo, Eo, tifv[:, :, mm, :].unsqueeze(2).broadcast_to([128, 8, 16, 16]), ALU.mult),
                             r=C2C + ["tif"], w=C2C + S2C)
                        P.op("dve", lambda e, q=q: e.tensor_reduce(out=IJG[:, q, :, :], in_=Eo, axis=AX.X, op=ALU.add), r=C2C + S2C, w=["IJG"])
                    P.op("dve", lambda e: e.tensor_tensor(dg, cv, cv[:, :, 0:1].broadcast_to([128, 8, 16]), ALU.subtract), r=CVC, w=["dg"])
                    yield
                    P.op("act", lambda e: e.activation(eg, dg, AF.Exp), r=["dg"], w=["eg"])
                    yield
                    P.op("dve", lambda e: e.tensor_reduce(out=zs, in_=eg, axis=AX.X, op=ALU.add), r=["eg"], w=["zs"])
                    P.op("dve", lambda e: e.reciprocal(rz, zs), r=["zs"], w=["rz"])
                    P.op("dve", lambda e: e.tensor_tensor(IJG[:, 2, :, :], eg, rz.unsqueeze(2).broadcast_to([128, 8, 16]), ALU.mult),
                         r=["eg", "rz", "IJG"], w=["IJG"])
                    yield
                    while True:
                        pb = bank_pool.acquire()
                        if pb is not None:
                            break
                        yield
                    for q in range(3):
                        P.op("pe", lambda e, q=q: e.transpose(bank(pb)[:, q * 128:(q + 1) * 128], IJG[:, q, :, :].rearrange("p h k -> p (h k)"), ident_f),
                             r=["IJG", "ident_f"], w=[f"ps{pb}"])
                    yield
                    P.op("act", lambda e: e.activation(IJGT[gb][:, :, tl * 128:(tl + 1) * 128],
                                                       bank(pb)[:, 0:384].rearrange("p (q t) -> p q t", t=128), AF.Copy),
                         r=[f"ps{pb}"], w=[f"IJGT{gb}"])
                    P.op("act", lambda e: e.activation(IJb[gb][:, :, tl * 128:(tl + 1) * 128],
                                                       bank(pb)[:, 0:256].rearrange("p (q t) -> p q t", t=128), AF.Copy),
                         r=[f"ps{pb}"], w=[f"IJb{gb}"])
                    bank_pool.release(pb)

                th.append(lambda: e1(0))
                th.append("FENCE")
                for grp in range(16):
                    th.append(lambda grp=grp: e2(grp))
                th.append("FENCE")
                for tl in range(2):
                    for quad in range(4):
                        th.append(lambda tl=tl, quad=quad: e3(tl, quad))
                    th.append("FENCE")
                    th.append(lambda tl=tl: e4a(tl))
                    th.append("FENCE")
                    th.append(lambda tl=tl: e4b(tl))
                    th.append("FENCE")
                    th.append(lambda tl=tl: e4c(tl))
                    th.append("FENCE")
                return th

            def prep_G(g, half):
                gb = g % 2
                th = []

                def chunk(ch):
                    while True:
                        s = oh_pool.acquire()
                        if s is not None:
                            break
                        yield
                    c0 = ch * CH
                    iob = iota_b.unsqueeze(1).broadcast_to([128, CH, 128])
                    iobh = iota_b[:, 64 * half:64 * half + 64].unsqueeze(1).broadcast_to([128, CH, 64])
                    P.op("dve", lambda e: e.tensor_tensor(oh1[s], iobh, IJb[gb][:, 0, c0:c0 + CH].unsqueeze(2).broadcast_to([128, CH, 64]), ALU.is_equal),
                         r=[f"IJb{gb}", "iota_b"], w=[f"oh1{s}"])
                    P.op("dve", lambda e: e.tensor_tensor(oh2[s], iob, IJb[gb][:, 1, c0:c0 + CH].unsqueeze(2).broadcast_to([128, CH, 128]), ALU.is_equal),
                         r=[f"IJb{gb}", "iota_b"], w=[f"oh2{s}"])
                    P.op("pool", lambda e: e.tensor_tensor(oh2g[s], oh2[s], IJGT[gb][:, 2, c0:c0 + CH].unsqueeze(2).broadcast_to([128, CH, 128]), ALU.mult),
                         r=[f"IJGT{gb}", f"oh2{s}"], w=[f"oh2g{s}"])
                    yield
                    yield
                    while True:
                        bk = bank_pool.acquire()
                        if bk is not None:
                            break
                        yield
                    for t in range(CH):
                        P.op("pe", lambda e, t=t: e.matmul(bank(bk)[:, t * 64:(t + 1) * 64], lhsT=oh2g[s][:, t, :], rhs=oh1[s][:, t, :],
                                                           start=True, stop=True),
                             r=[f"oh1{s}", f"oh2g{s}"], w=[f"ps{bk}"])
                    oh_pool.release(s)
                    yield
                    P.op("act", lambda e: e.activation(GTh[half][:, :, c0:c0 + CH].rearrange("p i t -> p t i"),
                                                       bank(bk).rearrange("p (t i) -> p t i", i=64), AF.Copy),
                         r=[f"ps{bk}"], w=[f"GT{half}"])
                    bank_pool.release(bk)

                for ch in range(TGE // CH):
                    th.append(lambda ch=ch: chunk(ch))
                return th

            def emit_U(g, i):
                gb = g % 2
                k = st["blk"]
                st["blk"] += 1
                s = k % NU
                pa = 4 + k % 2
                s2 = k % 3
                half = i // HB
                P.op("sp", lambda e: e.dma_start(out=uTb[s], in_=uT_scr[i].rearrange("p (c e) -> p c e", e=128)), w=[f"uTb{s}"], key=f"uTb{s}")
                P.op("sp", lambda e: e.dma_start(out=vb[s], in_=v_scr[i * 128:(i + 1) * 128, :]), w=[f"vb{s}"], key=f"vb{s}")
                for c in range(8):
                    P.op("pe", lambda e, c=c: e.matmul(bank(pa)[:, 0:TGE], lhsT=uTb[s][:, c, :], rhs=xn2T[gb][:, c, :],
                                                       start=(c == 0), stop=(c == 7)),
                         r=[f"uTb{s}", f"xn2T{gb}"], w=[f"ps{pa}"])
                P.op("act", lambda e: e.activation(geb[s2], bank(pa)[:, 0:TGE], AF.Gelu), r=[f"ps{pa}"], w=[f"geb{s2}"])
                P.op("dve", lambda e: e.tensor_tensor(hab[s2], geb[s2], GTh[half][:, i % HB, :], ALU.mult),
                     r=[f"geb{s2}", f"GT{half}"], w=[f"hab{s2}"])
                return (s, s2)

            def emit_V(i, ss):
                s, s2 = ss
                for tl in range(2):
                    for hf_ in range(2):
                        bk = tl * 2 + hf_
                        P.op("pe", lambda e, tl=tl, hf_=hf_, bk=bk: e.matmul(
                            bank(bk), lhsT=hab[s2][:, tl * 128:(tl + 1) * 128], rhs=vb[s][:, hf_ * 512:(hf_ + 1) * 512],
                            start=(i == 0), stop=(i == NB - 1)),
                             r=[f"hab{s2}", f"vb{s}"], w=[f"ps{bk}"])

            def group_end(g):
                gb = g % 2
                ssq2, ln2, rs2, pc = smp_slot()
                hss = [hsb[gb][tl] for tl in range(2)]
                hcs = [f"hsb{gb}{tl}" for tl in range(2)]
                for tl in range(2):
                    for hf_ in range(2):
                        bk = tl * 2 + hf_
                        hs_ = slice(hf_ * 512, (hf_ + 1) * 512)
                        P.op("dve", lambda e, bk=bk, hs_=hs_, tl=tl: e.tensor_tensor(hss[tl][:, hs_], bank(bk), hss[tl][:, hs_], ALU.add),
                             r=[f"ps{bk}", hcs[tl]], w=[hcs[tl]])
                    P.op("dve", lambda e, tl=tl: e.tensor_tensor_reduce(out=junk, in0=hss[tl], in1=hss[tl], scale=1.0, scalar=0.0,
                                                                        op0=ALU.mult, op1=ALU.add, accum_out=ssq2[:, tl:tl + 1]),
                         r=[hcs[tl]], w=[pc + f"s{tl}"])
                yield
                P.op("act", lambda e: e.activation(ln2, ssq2, AF.Ln, bias=EPS, scale=1.0 / D), r=[pc + "s0", pc + "s1"], w=[pc + "l"])
                P.op("act", lambda e: e.activation(rs2, ln2, AF.Exp, scale=-0.5), r=[pc + "l"], w=[pc + "r"])
                yield
                yield
                for tl in range(2):
                    tt = g * 2 + tl
                    P.op("dve", lambda e, tl=tl: e.scalar_tensor_tensor(out=hss[tl], in0=hss[tl], scalar=rs2[:, tl:tl + 1], in1=gfb,
                                                                        op0=ALU.mult, op1=ALU.mult),
                         r=[hcs[tl], pc + "r", "gfb"], w=[hcs[tl]])
                    P.op("sp", lambda e, tl=tl, tt=tt: e.dma_start(out=out[tt * 128:(tt + 1) * 128, :], in_=hss[tl]),
                         r=[hcs[tl]], w=["out"], key=hcs[tl])

            class Stream:
                def __init__(self, items):
                    self.items = list(items)
                    self.active = []

                def pending(self):
                    return sum(1 for x_ in self.items if x_ != "FENCE")

                def busy(self):
                    return bool(self.items or self.active)

                def step(self, nstart):
                    for gen in list(self.active):
                        try:
                            next(gen)
                        except StopIteration:
                            self.active.remove(gen)
                    started = 0
                    while self.items and started < nstart:
                        if self.items[0] == "FENCE":
                            if self.active:
                                break
                            self.items.pop(0)
                            continue
                        gen = self.items.pop(0)()
                        try:
                            next(gen)
                            self.active.append(gen)
                        except StopIteration:
                            pass
                        started += 1

                def drain(self, tag=""):
                    n0 = self.pending(); a0 = len(self.active); k = 0
                    while self.busy():
                        self.step(4); k += 1
                    if debug and (n0 or a0):
                        print(f"drain {tag}: pending={n0} active={a0} steps={k}")

            ge_streams = []
            Stream(prep_topk(0)).drain()
            Stream(prep_G(0, 0)).drain()
            for g in range(NGE):
                sB = Stream(prep_G(g, 1))
                sT = Stream(prep_topk(g + 1) if g + 1 < NGE else [])
                sA = Stream(prep_G(g + 1, 0) if g + 1 < NGE else [])
                sE = ge_streams.pop(0) if ge_streams else None
                pend = {0: emit_U(g, 0), 1: emit_U(g, 1)}
                for i in range(NB):
                    ii = i % HB
                    ep_busy = sE is not None and sE.busy()
                    if ep_busy:
                        sE.step(0)
                    if i < HB:
                        left = max(1, (HB - 14) - ii)
                        sB.step((sB.pending() + left - 1) // left if sB.pending() else 0)
                        if not ep_busy:
                            sT.step(2)
                    else:
                        if i == HB:
                            sT.drain(f'g{g} sT@HB')
                        left = max(1, (HB - 8) - ii)
                        sA.step((sA.pending() + left - 1) // left if sA.pending() else 0)
                    if i + 2 < NB:
                        if (i + 2) == HB:
                            sB.drain(f'g{g} sB@62')
                        pend[i + 2] = emit_U(g, i + 2)
                    emit_V(i, pend.pop(i))
                sB.drain(f'g{g} sB@end')
                sT.drain(f'g{g} sT@end')
                sA.drain(f'g{g} sA@end')
                if sE is not None:
                    sE.drain()
                sEn = Stream([lambda g=g: group_end(g)])
                sEn.step(1)
                ge_streams.append(sEn)
            ge_streams[0].drain()
        P.barrier()

        P.finalize()
        sems = {}
        for en in Prog.ENGS:
            sems[("eng", en)] = es.enter_context(nc.semaphore(f"sem_{en}"))
        for i, k in enumerate(sorted(P.dma_count.keys())):
            sems[("dma", k)] = es.enter_context(nc.semaphore(f"semd_{i}"))
        block = es.enter_context(nc.Block())

        @block.tensor
        def _(e):
            P.emit("pe", e, sems)

        @block.scalar
        def _(e):
            P.emit("act", e, sems)

        @block.vector
        def _(e):
            P.emit("dve", e, sems)

        @block.gpsimd
        def _(e):
            P.emit("pool", e, sems)

        @block.sync
        def _(e):
            P.emit("sp", e, sems)

    mybir.codegen_inst_isa_subclasses(nc)
    return nc


_NC_CACHE = {}


def _prep_inputs(inputs):
    f = lambda a: np.ascontiguousarray(np.asarray(a, dtype=np.float32))
    shared = {
        "norm1_g": f(inputs["norm1_g"]).reshape(D),
        "w_in": f(inputs["w_in"]).reshape(D, 8192),
        "lambda_qk": f(inputs["lambda_qk"]).reshape(256),
        "subln_g": f(inputs["subln_g"]).reshape(128),
        "conv_w": f(inputs["conv_w"]).reshape(3, D),
        "w_attn_o": f(inputs["w_attn_o"]).reshape(D, D),
        "w_conv_o": f(inputs["w_conv_o"]).reshape(D, D),
        "w_out": f(inputs["w_out"]).reshape(D, D),
        "norm2_g": f(inputs["norm2_g"]).reshape(D),
        "w_query": f(inputs["w_query"]).reshape(D, 2048),
        "sub_keys": f(inputs["sub_keys"]).reshape(8, 2, 128, 128),
        "expert_u": f(inputs["expert_u"]).reshape(16384, D),
        "expert_v": f(inputs["expert_v"]).reshape(16384, D),
        "final_g": f(inputs["final_g"]).reshape(D),
    }
    xs = f(inputs["x"])
    in_maps = []
    for b in range(8):
        m = dict(shared)
        m["x"] = np.ascontiguousarray(xs[b])
        in_maps.append(m)
    return in_maps


def kernel(**inputs):
    if "nc" not in _NC_CACHE:
        _NC_CACHE["nc"] = build_nc()
    nc = _NC_CACHE["nc"]
    in_maps = _prep_inputs(inputs)
    res = run_bass_kernel_spmd(nc, in_maps, core_ids=list(range(8)))
    outs = [np.asarray(r["out"], dtype=np.float32).reshape(S, D) for r in res.results]
    return np.stack(outs, axis=0)
```

```python
import os
from contextlib import ExitStack

import numpy as np
import concourse.bass as bass
import concourse.mybir as mybir
from concourse.bass_utils import run_bass_kernel_spmd

F32 = mybir.dt.float32
BF16 = mybir.dt.bfloat16
U8 = mybir.dt.uint8
U32 = mybir.dt.uint32
I32 = mybir.dt.int32
AF = mybir.ActivationFunctionType
ALU = mybir.AluOpType
AX = mybir.AxisListType

S = 2048
D = 1024
NT = 16
EPS = 1e-6
LAM_INIT = 0.2
NB = 128
TGE = 256
NGE = S // TGE


class _Op:
    __slots__ = ("eng", "fn", "deps", "key", "dma_cnt", "need_inc", "cnt", "barrier", "snap")

    def __init__(self, eng, fn, deps, key):
        self.eng = eng
        self.fn = fn
        self.deps = deps
        self.key = key
        self.dma_cnt = 0
        self.need_inc = False
        self.cnt = 0
        self.barrier = False
        self.snap = None


class Prog:
    ENGS = ("pe", "act", "dve", "pool", "sp")

    def __init__(self):
        self.ops = []
        self.last_w = {}
        self.readers = {}
        self.dma_count = {}

    def op(self, eng, fn, r=(), w=(), key=None):
        idx = len(self.ops)
        deps = set()
        for c in r:
            if c in self.last_w:
                deps.add(self.last_w[c])
        for c in w:
            if c in self.last_w:
                deps.add(self.last_w[c])
            for x in self.readers.get(c, ()):
                deps.add(x)
        o = _Op(eng, fn, deps, key)
        if key is not None:
            self.dma_count[key] = self.dma_count.get(key, 0) + 16
            o.dma_cnt = self.dma_count[key]
        self.ops.append(o)
        for c in w:
            self.last_w[c] = idx
            self.readers[c] = []
        for c in r:
            if c not in w:
                self.readers.setdefault(c, []).append(idx)
        return idx

    def barrier(self):
        o = _Op(None, None, set(), None)
        o.barrier = True
        self.ops.append(o)
        self.last_w = {}
        self.readers = {}

    def finalize(self):
        ops = self.ops
        for o in ops:
            if o.barrier:
                continue
            for d in o.deps:
                dep = ops[d]
                if dep.key is None and not (o.eng == "pe" and dep.eng == "pe"):
                    dep.need_inc = True
        last_on = {e: None for e in self.ENGS}
        for i, o in enumerate(ops):
            if o.barrier:
                for e in self.ENGS:
                    if last_on[e] is not None:
                        ops[last_on[e]].need_inc = True
            elif o.key is None:
                last_on[o.eng] = i
        cnt = {e: 0 for e in self.ENGS}
        dcnt = {}
        for o in ops:
            if o.barrier:
                o.snap = (dict(cnt), dict(dcnt))
                continue
            if o.key is not None:
                dcnt[o.key] = o.dma_cnt
            elif o.need_inc:
                cnt[o.eng] += 1
                o.cnt = cnt[o.eng]

    def emit(self, eng_name, e, sems):
        ops = self.ops
        seen = {}

        def wait(s, v):
            if v > 0 and seen.get(s, 0) < v:
                e.wait_ge(sems[s], v)
                seen[s] = v

        for o in ops:
            if o.barrier:
                cnt, dcnt = o.snap
                for b, v in cnt.items():
                    if b != eng_name:
                        wait(("eng", b), v)
                for k, v in dcnt.items():
                    wait(("dma", k), v)
                continue
            if o.eng != eng_name:
                continue
            waits = {}
            for d in o.deps:
                dep = ops[d]
                if dep.key is not None:
                    s, v = ("dma", dep.key), dep.dma_cnt
                else:
                    if eng_name == "pe" and dep.eng == "pe":
                        continue
                    s, v = ("eng", dep.eng), dep.cnt
                if waits.get(s, 0) < v:
                    waits[s] = v
            for s, v in waits.items():
                wait(s, v)
            ins = o.fn(e)
            if o.key is not None:
                ins.then_inc(sems[("dma", o.key)], 16)
            elif o.need_inc:
                ins.then_inc(sems[("eng", eng_name)], 1)


class Arena:
    def __init__(self, ap, size):
        self.ap = ap
        self.size = size
        self.off = 0

    def alloc(self, nbytes, dtype):
        o = (self.off + 63) // 64 * 64
        assert o + nbytes <= self.size, f"SBUF arena overflow {o + nbytes} > {self.size}"
        self.off = o + nbytes
        return self.ap[:, o:o + nbytes].bitcast(dtype)


def build_nc(debug=False, stop_after=None):
    nc = bass.Bass("TRN2", target_bir_lowering=False)

    def din(name, shape, dtype=F32):
        return nc.dram_tensor(name, shape, dtype, kind="ExternalInput").ap()

    x = din("x", [S, D])
    norm1_g = din("norm1_g", [D])
    w_in = din("w_in", [D, 8192])
    lambda_qk = din("lambda_qk", [256])
    subln_g = din("subln_g", [128])
    conv_w = din("conv_w", [3, D])
    w_attn_o = din("w_attn_o", [D, D])
    w_conv_o = din("w_conv_o", [D, D])
    w_out = din("w_out", [D, D])
    norm2_g = din("norm2_g", [D])
    w_query = din("w_query", [D, 2048])
    sub_keys = din("sub_keys", [8, 2, 128, 128])
    expert_u = din("expert_u", [16384, D])
    expert_v = din("expert_v", [16384, D])
    final_g = din("final_g", [D])
    out = nc.dram_tensor("out", [S, D], F32, kind="ExternalOutput").ap()
    skind = "ExternalOutput" if debug else "Internal"
    uT_scr = nc.dram_tensor("uT_scr", [NB, 128, 1024], BF16, kind=skind).ap()
    v_scr = nc.dram_tensor("v_scr", [16384, D], BF16, kind=skind).ap()
    h_scr = nc.dram_tensor("h_scr", [S, D], F32, kind=skind).ap()
    wq_scr = nc.dram_tensor("wq_scr", [D, 2048], BF16, kind="Internal").ap()

    if debug:
        dbg_nT = nc.dram_tensor("dbg_nT", [128, 8 * S], BF16, kind="ExternalOutput").ap()
        dbg_ycT = nc.dram_tensor("dbg_ycT", [128, 8 * S], BF16, kind="ExternalOutput").ap()
        dbg_attnT = nc.dram_tensor("dbg_attnT", [128, 8 * S], BF16, kind="ExternalOutput").ap()
        dbg_mg = nc.dram_tensor("dbg_mg", [128, 8 * S], BF16, kind="ExternalOutput").ap()
        dbg_small = nc.dram_tensor("dbg_small", [128, 256], F32, kind="ExternalOutput").ap()

    P = Prog()
    ARENA_BYTES = 204 * 1024

    with ExitStack() as es:
        arena_t = es.enter_context(nc.sbuf_tensor("arena", [128, ARENA_BYTES], U8))
        ps = es.enter_context(nc.psum_tensor("ps", [128, 4096], F32))
        A = Arena(arena_t, ARENA_BYTES)

        def bank(b):
            return ps[:, b * 512:(b + 1) * 512]

        def bank_bf(b):
            return bank(b).bitcast(BF16)

        ident_bf = A.alloc(256, BF16)
        ident_f = A.alloc(512, F32)
        mask_tri = A.alloc(256, BF16)
        iota_f = A.alloc(512, F32)
        iota_i = A.alloc(512, I32)
        diff_i = A.alloc(512, I32)
        diff_f = A.alloc(512, F32)
        g1b = A.alloc(4096, F32)
        g2b = A.alloc(4096, F32)
        gfb = A.alloc(4096, F32)
        subgb = A.alloc(512, F32)
        cw = A.alloc(96, F32).rearrange("p (c j) -> p c j", j=3)
        lq = A.alloc(1024, F32)
        lam_s = A.alloc(64, F32)
        skT = A.alloc(4096, BF16).rearrange("p (g n) -> p g n", n=128)
        small = A.alloc(64 * 4 * 4, F32)
        junk = A.alloc(2048, BF16)
        junk_f = A.alloc(512, F32)
        smp = A.alloc(6 * 16 * 4, F32)
        iota_b = A.alloc(256, BF16)
        c_mhalf = A.alloc(4, F32)
        c_e = A.alloc(512, F32)
        const_mark = A.off

        sm_ctr = [0]

        def sm():
            k = sm_ctr[0] % 256
            sm_ctr[0] += 1
            return small[:, k:k + 1], f"sm{k}"

        P.op("sp", lambda e: e.dma_start(out=g1b, in_=norm1_g.partition_broadcast(128)), w=["g1b"], key="c0_0")
        P.op("sp", lambda e: e.dma_start(out=g2b, in_=norm2_g.partition_broadcast(128)), w=["g2b"], key="c0_1")
        P.op("sp", lambda e: e.dma_start(out=gfb, in_=final_g.partition_broadcast(128)), w=["gfb"], key="c0_2")
        P.op("sp", lambda e: e.dma_start(out=subgb, in_=subln_g.partition_broadcast(128)), w=["subgb"], key="c0_3")
        P.op("sp", lambda e: e.dma_start(out=lq, in_=lambda_qk.partition_broadcast(128)), w=["lq"], key="c0_4")
        for j in range(3):
            for c in range(8):
                P.op("sp", lambda e, j=j, c=c: e.dma_start(out=cw[:, c, j:j + 1],
                                                           in_=conv_w[j, c * 128:(c + 1) * 128].rearrange("(p o) -> p o", o=1)),
                     w=["cw"], key="c0_5")
        P.op("dve", lambda e: e.memset(c_mhalf, -0.5), w=["c_mhalf"])
        P.op("dve", lambda e: e.memset(c_e, float(np.float32(np.e))), w=["c_e"])
        P.op("pool", lambda e: e.iota(iota_i, pattern=[[1, 128]], base=0, channel_multiplier=0), w=["iota_i"])
        P.op("pool", lambda e: e.iota(diff_i, pattern=[[1, 128]], base=0, channel_multiplier=-1), w=["diff_i"])
        P.op("dve", lambda e: e.tensor_copy(iota_f, iota_i), r=["iota_i"], w=["iota_f"])
        P.op("dve", lambda e: e.tensor_copy(diff_f, diff_i), r=["diff_i"], w=["diff_f"])
        P.op("dve", lambda e: e.tensor_copy(iota_b, iota_i), r=["iota_i"], w=["iota_b"])
        P.op("dve", lambda e: e.tensor_single_scalar(ident_f, diff_f, 0.0, ALU.is_equal), r=["diff_f"], w=["ident_f"])
        P.op("dve", lambda e: e.tensor_single_scalar(ident_bf, diff_f, 0.0, ALU.is_equal), r=["diff_f"], w=["ident_bf"])
        P.op("dve", lambda e: e.tensor_single_scalar(mask_tri, diff_f, 0.0, ALU.is_ge), r=["diff_f"], w=["mask_tri"])
        P.op("dve", lambda e: e.tensor_scalar(subgb, subgb, 1.0 - LAM_INIT, None, ALU.mult), r=["subgb"], w=["subgb"])
        P.op("dve", lambda e: e.tensor_tensor_reduce(out=junk_f[:, 0:64], in0=lq[:, 0:64], in1=lq[:, 64:128],
                                                     scale=1.0, scalar=0.0, op0=ALU.mult, op1=ALU.add,
                                                     accum_out=lam_s[:, 0:1]), r=["lq"], w=["lam0"])
        P.op("dve", lambda e: e.tensor_tensor_reduce(out=junk_f[:, 64:128], in0=lq[:, 128:192], in1=lq[:, 192:256],
                                                     scale=1.0, scalar=0.0, op0=ALU.mult, op1=ALU.add,
                                                     accum_out=lam_s[:, 1:2]), r=["lq"], w=["lam1"])
        P.op("act", lambda e: e.activation(lam_s[:, 2:4], lam_s[:, 0:2], AF.Exp), r=["lam0", "lam1"], w=["lam2"])
        P.op("dve", lambda e: e.tensor_tensor(lam_s[:, 4:5], lam_s[:, 3:4], lam_s[:, 2:3], ALU.subtract),
             r=["lam2"], w=["lam4"])
        neglam = lam_s[:, 5:6]
        P.op("dve", lambda e: e.tensor_scalar(neglam, lam_s[:, 4:5], -LAM_INIT, None, ALU.add),
             r=["lam4"], w=["neglam"])

        m0 = A.off
        sk_nat = A.alloc(4096, BF16).rearrange("p (g k) -> p g k", k=128)
        P.op("pool", lambda e: e.dma_start(out=sk_nat, in_=sub_keys.rearrange("h m n k -> n (h m) k")),
             w=["sk_nat"], key="c1")
        for g4 in range(4):
            for q in range(4):
                g = g4 * 4 + q
                P.op("pe", lambda e, g=g, q=q, g4=g4: e.transpose(bank_bf(g4)[:, q * 128:(q + 1) * 128], sk_nat[:, g, :], ident_bf),
                     r=["sk_nat", "ident_bf"], w=[f"ps{g4}"])
            P.op("act", lambda e, g4=g4: e.activation(skT[:, g4 * 4:(g4 + 1) * 4, :],
                                                      bank_bf(g4)[:, 0:512].rearrange("p (g n) -> p g n", n=128), AF.Copy),
                 r=[f"ps{g4}"], w=["skT"])
        P.barrier()
        A.off = m0

        def rstd_from(ssq_ap, ssq_cell, n):
            lnv, c1 = sm()
            rs, c2 = sm()
            P.op("act", lambda e: e.activation(lnv, ssq_ap, AF.Ln, bias=EPS, scale=1.0 / n), r=[ssq_cell], w=[c1])
            P.op("act", lambda e: e.activation(rs, lnv, AF.Exp, scale=-0.5), r=[c1], w=[c2])
            return rs, c2

        smp_ctr = [0]

        def smp_slot():
            k = smp_ctr[0] % 16
            smp_ctr[0] += 1
            return smp[:, k * 6:k * 6 + 2], smp[:, k * 6 + 2:k * 6 + 4], smp[:, k * 6 + 4:k * 6 + 6], f"smp{k}"

        def rstd_pool(ssq_ap, ssq_cell, n):
            mse, c1 = sm()
            rs, c2 = sm()
            P.op("dve", lambda e: e.tensor_scalar(mse, ssq_ap, 1.0 / n, EPS, ALU.mult, ALU.add), r=[ssq_cell], w=[c1])
            P.op("pool", lambda e: e.tensor_tensor(rs, mse, c_mhalf, ALU.pow), r=[c1, "c_mhalf"], w=[c2])
            return rs, c2

        mP = A.off
        ub = [A.alloc(2048, BF16) for _ in range(3)]
        uTo = [A.alloc(2048, BF16) for _ in range(3)]
        conv_jobs = []
        if stop_after != "M":
            for r in range(8):
                conv_jobs.append(lambda r=r: P.op("pool", lambda e: e.dma_start(out=wq_scr[r * 128:(r + 1) * 128, :], in_=w_query[r * 128:(r + 1) * 128, :]),
                                                  w=["wq_scr"], key="wqconv"))
            for r in range(32):
                conv_jobs.append(lambda r=r: P.op("pool", lambda e: e.dma_start(out=v_scr[r * 512:(r + 1) * 512, :], in_=expert_v[r * 512:(r + 1) * 512, :]),
                                                  w=["v_scr"], key="vconv"))

        ub8 = []

        def p_load(b):
            s = b % 8
            P.op("pool", lambda e: e.dma_start(out=ub8[s], in_=expert_u[b * 128:(b + 1) * 128, :]), w=[f"ub{s}"], key=f"ub{s}")

        def p_compute(b):
            nonlocal sb_ctr
            s = b % 8
            so = b % 3
            pb = sb_ctr % 4
            sb_ctr += 1
            for c in range(8):
                P.op("pe", lambda e, c=c: e.transpose(bank_bf(pb)[:, c * 128:(c + 1) * 128], ub8[s][:, c * 128:(c + 1) * 128], ident_bf),
                     r=[f"ub{s}"], w=[f"ps{pb}"])
            if b % 2 == 0:
                P.op("act", lambda e: e.activation(uTo[so], bank_bf(pb), AF.Copy), r=[f"ps{pb}"], w=[f"uTo{so}"])
            else:
                P.op("dve", lambda e: e.tensor_copy(uTo[so], bank_bf(pb)), r=[f"ps{pb}"], w=[f"uTo{so}"])
            P.op("sp", lambda e: e.dma_start(out=uT_scr[b], in_=uTo[so]), r=[f"uTo{so}"], w=["uT_scr"], key=f"uTo{so}")

        p_next = [0]

        nT = A.alloc(32768, BF16).rearrange("p (c t) -> p c t", t=S)
        ycT = A.alloc(32768, BF16).rearrange("p (c t) -> p c t", t=S)
        attnT = A.alloc(32768, BF16).rearrange("p (c t) -> p c t", t=S)
        NW = 8
        wring = [A.alloc(2048, BF16).rearrange("p (c n) -> p c n", n=128) for _ in range(NW)]
        w_ctr = [0]

        def load_panel(src_ap):
            s = w_ctr[0] % NW
            w_ctr[0] += 1
            P.op("pool", lambda e: e.dma_start(out=wring[s], in_=src_ap), w=[f"w{s}"], key=f"w{s}")
            return wring[s], f"w{s}"

        w_in_v = w_in.rearrange("(c p) n -> p c n", p=128)
        ps_ctr = [0]

        mS = A.off
        xt = [A.alloc(4096, F32) for _ in range(2)]
        nb = [A.alloc(2048, BF16) for _ in range(2)]
        for tt in range(NT):
            s = tt % 2
            pb = tt % 4
            P.op("sp", lambda e, tt=tt, s=s: e.dma_start(out=xt[s], in_=x[tt * 128:(tt + 1) * 128, :]), w=[f"xt{s}"], key=f"xt{s}")
            ssq, cq = sm()
            P.op("dve", lambda e, s=s, ssq=ssq: e.tensor_tensor_reduce(out=junk, in0=xt[s], in1=xt[s], scale=1.0, scalar=0.0,
                                                                       op0=ALU.mult, op1=ALU.add, accum_out=ssq),
                 r=[f"xt{s}"], w=[cq])
            rs, cr = rstd_from(ssq, cq, D)
            P.op("dve", lambda e, s=s, rs=rs: e.scalar_tensor_tensor(out=nb[s], in0=xt[s], scalar=rs, in1=g1b, op0=ALU.mult, op1=ALU.mult),
                 r=[f"xt{s}", cr, "g1b"], w=[f"nb{s}"])
            for c in range(8):
                P.op("pe", lambda e, s=s, c=c, pb=pb: e.transpose(bank_bf(pb)[:, c * 128:(c + 1) * 128],
                                                                nb[s][:, c * 128:(c + 1) * 128], ident_bf),
                     r=[f"nb{s}"], w=[f"ps{pb}"])
            P.op("act", lambda e, tt=tt, pb=pb: e.activation(nT[:, :, tt * 128:(tt + 1) * 128],
                                                            bank_bf(pb).rearrange("p (c t) -> p c t", t=128), AF.Copy),
                 r=[f"ps{pb}"], w=[f"nT{tt // 4}"])
        P.barrier()
        A.off = mS

        uconv = [A.alloc(2050 * 4, F32) for _ in range(2)]
        tmp1 = [A.alloc(2048, F32) for _ in range(2)]
        zt = [A.alloc(2048, F32) for _ in range(2)]
        for k in range(2):
            P.op("dve", lambda e, k=k: e.memset(uconv[k][:, 0:2], 0.0), w=[f"uconv{k}"])
        step = 0
        def conv_panels(cch):
            return (load_panel(w_in_v[:, :, 3072 + cch * 128:3072 + (cch + 1) * 128]),
                    load_panel(w_in_v[:, :, 4096 + cch * 128:4096 + (cch + 1) * 128]),
                    load_panel(w_in_v[:, :, 5120 + cch * 128:5120 + (cch + 1) * 128]))

        cpan = {0: conv_panels(0)}
        for cch in range(8):
            if cch + 1 < 8:
                cpan[cch + 1] = conv_panels(cch + 1)
            (wcb, kcb), (wcc, kcc), (wcx, kcx) = cpan.pop(cch)
            uc = uconv[cch % 2]
            ucc = f"uconv{cch % 2}"
            for tg in range(4):
                bset = (step % 2) * 3
                step += 1
                bA, bB, bC = bset, bset + 1, bset + 2
                for (bk, wp, wk) in ((bA, wcx, kcx), (bB, wcc, kcc), (bC, wcb, kcb)):
                    for c in range(8):
                        P.op("pe", lambda e, bk=bk, wp=wp, c=c, tg=tg: e.matmul(bank(bk), lhsT=wp[:, c, :],
                                                                            rhs=nT[:, c, tg * 512:(tg + 1) * 512],
                                                                            start=(c == 0), stop=(c == 7)),
                             r=[wk, f"nT{tg}"], w=[f"ps{bk}"])
                s = tg % 2
                o0 = tg * 512
                P.op("act", lambda e, s=s, bA=bA: e.activation(tmp1[s], bank(bA), AF.Copy), r=[f"ps{bA}"], w=[f"tmp1{s}"])
                P.op("dve", lambda e, s=s, bB=bB, uc=uc, o0=o0: e.tensor_tensor(uc[:, 2 + o0:2 + o0 + 512], bank(bB), tmp1[s], ALU.mult),
                     r=[f"ps{bB}", f"tmp1{s}"], w=[ucc])
                P.op("dve", lambda e, s=s, uc=uc, o0=o0, cch=cch: e.tensor_scalar(zt[s], uc[:, 2 + o0:2 + o0 + 512], cw[:, cch, 2:3], None, ALU.mult),
                     r=[ucc, "cw"], w=[f"zt{s}"])
                P.op("dve", lambda e, s=s, uc=uc, o0=o0, cch=cch: e.scalar_tensor_tensor(out=zt[s], in0=uc[:, 1 + o0:1 + o0 + 512], scalar=cw[:, cch, 1:2],
                                                                                       in1=zt[s], op0=ALU.mult, op1=ALU.add),
                     r=[ucc, "cw", f"zt{s}"], w=[f"zt{s}"])
                P.op("dve", lambda e, s=s, uc=uc, o0=o0, cch=cch: e.scalar_tensor_tensor(out=zt[s], in0=uc[:, o0:o0 + 512], scalar=cw[:, cch, 0:1],
                                                                                       in1=zt[s], op0=ALU.mult, op1=ALU.add),
                     r=[ucc, "cw", f"zt{s}"], w=[f"zt{s}"])
                P.op("dve", lambda e, s=s, bC=bC, cch=cch, o0=o0: e.tensor_tensor(ycT[:, cch, o0:o0 + 512], bank(bC), zt[s], ALU.mult),
                     r=[f"ps{bC}", f"zt{s}"], w=[f"ycT{tg}"])
        P.barrier()
        A.off = mS

        qT = A.alloc(4096, BF16)
        kT = A.alloc(4096, BF16)
        vsb = A.alloc(16 * 130 * 2, BF16).rearrange("p (t e) -> p t e", e=130)
        NPT = 6
        pt = [A.alloc(1024, BF16) for _ in range(NPT)]
        of1 = [A.alloc(512, F32) for _ in range(4)]
        of2 = [A.alloc(512, F32) for _ in range(4)]
        onb = [A.alloc(256, BF16) for _ in range(4)]
        ub8.extend(A.alloc(2048, BF16) for _ in range(8))
        if stop_after != "M":
            for b_ in range(4):
                p_load(b_)
        P.op("dve", lambda e: e.memset(vsb[:, :, 128:129], 1.0), w=["vsb1"])
        pt_ctr = 0
        sb_ctr = 0
        fin_ctr = 0

        def oacc(a):
            bk = 4 + a // 2
            o = (a % 2) * 130
            return bank(bk)[:, o:o + 129], f"ps{bk}"

        def head_panels(h):
            return (load_panel(w_in_v[:, :, h * 128:(h + 1) * 128]),
                    load_panel(w_in_v[:, :, 1024 + h * 128:1024 + (h + 1) * 128]),
                    load_panel(w_in_v[:, :, 2048 + h * 128:2048 + (h + 1) * 128]))

        hpan = {0: head_panels(0)}
        for h in range(8):
            if h + 1 < 8:
                hpan[h + 1] = head_panels(h + 1)
            for _ in range(5):
                if conv_jobs:
                    conv_jobs.pop(0)()
            (wq, kq), (wk, kk), (wv, kv) = hpan.pop(h)
            for (dst, dname, wp, wkey) in ((qT, "qT", wq, kq), (kT, "kT", wk, kk)):
                for tg in range(4):
                    bk = sb_ctr % 4
                    sb_ctr += 1
                    for c in range(8):
                        P.op("pe", lambda e, bk=bk, wp=wp, c=c, tg=tg: e.matmul(bank(bk), lhsT=wp[:, c, :],
                                                                            rhs=nT[:, c, tg * 512:(tg + 1) * 512],
                                                                            start=(c == 0), stop=(c == 7)),
                             r=[wkey, f"nT{tg}"], w=[f"ps{bk}"])
                    P.op("act", lambda e, bk=bk, dst=dst, tg=tg: e.activation(dst[:, tg * 512:(tg + 1) * 512], bank(bk), AF.Copy),
                         r=[f"ps{bk}"], w=[f"{dname}{tg}"])
            for t4 in range(4):
                bk = sb_ctr % 4
                sb_ctr += 1
                for tq in range(4):
                    tt = t4 * 4 + tq
                    for c in range(8):
                        P.op("pe", lambda e, bk=bk, tq=tq, tt=tt, c=c, wv=wv: e.matmul(bank(bk)[:, tq * 128:(tq + 1) * 128],
                                                                                   lhsT=nT[:, c, tt * 128:(tt + 1) * 128], rhs=wv[:, c, :],
                                                                                   start=(c == 0), stop=(c == 7)),
                             r=[kv, f"nT{t4}"], w=[f"ps{bk}"])
                P.op("dve", lambda e, bk=bk, t4=t4: e.tensor_copy(vsb[:, t4 * 4:(t4 + 1) * 4, 0:128],
                                                                bank(bk).rearrange("p (t e) -> p t e", e=128)),
                     r=[f"ps{bk}"], w=["vsb"])
            for qg in range(4):
                steps = [(j, m) for j in range(4 * qg + 4) for m in range(2)]

                def emit_S(j, m, qg=qg):
                    nonlocal sb_ctr, pt_ctr
                    col0 = max(qg * 512, j * 128)
                    ncols = (qg + 1) * 512 - col0
                    diag = j >= 4 * qg
                    bk = sb_ctr % 4
                    sb_ctr += 1
                    sl = pt_ctr % NPT
                    pt_ctr += 1
                    P.op("pe", lambda e, bk=bk, m=m, j=j, col0=col0, ncols=ncols: e.matmul(
                        bank(bk)[:, 0:ncols], lhsT=kT[64 * m:64 * m + 64, j * 128:(j + 1) * 128],
                        rhs=qT[64 * m:64 * m + 64, col0:col0 + ncols], start=True, stop=True),
                         r=[f"kT{j // 4}", f"qT{qg}"], w=[f"ps{bk}"])
                    P.op("act", lambda e, bk=bk, sl=sl, ncols=ncols: e.activation(pt[sl][:, 0:ncols], bank(bk)[:, 0:ncols], AF.Exp, scale=0.125),
                         r=[f"ps{bk}"], w=[f"pt{sl}"])
                    if diag:
                        P.op("dve", lambda e, sl=sl: e.tensor_tensor(pt[sl][:, 0:128], pt[sl][:, 0:128], mask_tri, ALU.mult),
                             r=[f"pt{sl}"], w=[f"pt{sl}"])
                    return (sl, col0)

                def emit_PV(j, m, st, qg=qg):
                    sl, col0 = st
                    for i in range(max(4 * qg, j), 4 * qg + 4):
                        off = i * 128 - col0
                        aidx = m * 4 + (i - 4 * qg)
                        oa, oc = oacc(aidx)
                        P.op("pe", lambda e, oa=oa, sl=sl, off=off, j=j, i=i, aidx=aidx: e.matmul(
                            oa, lhsT=pt[sl][:, off:off + 128], rhs=vsb[:, j, 0:129],
                            start=(j == 0 and aidx % 2 == 0), stop=(j == i), skip_group_check=True),
                             r=[f"pt{sl}", "vsb", "vsb1"], w=[oc])

                DEPTH = 2
                sts = {}
                for k in range(min(DEPTH, len(steps))):
                    sts[k] = emit_S(*steps[k])
                for k in range(len(steps)):
                    if k + DEPTH < len(steps):
                        sts[k + DEPTH] = emit_S(*steps[k + DEPTH])
                    emit_PV(steps[k][0], steps[k][1], sts[k])
                fin = []
                for il in range(4):
                    o0a, c0 = oacc(il)
                    o1a, c1 = oacc(4 + il)
                    r0, cr0 = sm()
                    r1, cr1 = sm()
                    r1n, cr1n = sm()
                    P.op("dve", lambda e, r0=r0, o0a=o0a: e.reciprocal(r0, o0a[:, 128:129]), r=[c0], w=[cr0])
                    P.op("dve", lambda e, r1=r1, o1a=o1a: e.reciprocal(r1, o1a[:, 128:129]), r=[c1], w=[cr1])
                    fin.append((o0a, c0, o1a, c1, r0, cr0, r1, cr1, r1n, cr1n))
                for il in range(4):
                    (o0a, c0, o1a, c1, r0, cr0, r1, cr1, r1n, cr1n) = fin[il]
                    P.op("dve", lambda e, r1=r1, r1n=r1n: e.tensor_tensor(r1n, r1, neglam, ALU.mult), r=[cr1, "neglam"], w=[cr1n])
                    P.op("dve", lambda e, il=il, o0a=o0a, r0=r0: e.tensor_scalar(of1[il], o0a[:, 0:128], r0, None, ALU.mult),
                         r=[c0, cr0], w=[f"of1{il}"])
                for il in range(4):
                    (o0a, c0, o1a, c1, r0, cr0, r1, cr1, r1n, cr1n) = fin[il]
                    P.op("dve", lambda e, il=il, o1a=o1a, r1n=r1n: e.scalar_tensor_tensor(out=of2[il], in0=o1a[:, 0:128], scalar=r1n, in1=of1[il],
                                                                                       op0=ALU.mult, op1=ALU.add),
                         r=[c1, cr1n, f"of1{il}"], w=[f"of2{il}"])
                sq = []
                for il in range(4):
                    ssq, cq = sm()
                    P.op("dve", lambda e, il=il, ssq=ssq: e.tensor_tensor_reduce(out=junk_f, in0=of2[il], in1=of2[il], scale=1.0, scalar=0.0,
                                                                                 op0=ALU.mult, op1=ALU.add, accum_out=ssq),
                         r=[f"of2{il}"], w=[cq])
                    sq.append((ssq, cq))
                rss = [rstd_from(ssq, cq, 128) for (ssq, cq) in sq]
                for il in range(4):
                    rs, cr = rss[il]
                    P.op("dve", lambda e, il=il, rs=rs: e.scalar_tensor_tensor(out=onb[il], in0=of2[il], scalar=rs, in1=subgb,
                                                                             op0=ALU.mult, op1=ALU.mult),
                         r=[f"of2{il}", cr, "subgb"], w=[f"onb{il}"])
                for il in range(4):
                    i = 4 * qg + il
                    bk = sb_ctr % 4
                    sb_ctr += 1
                    P.op("pe", lambda e, bk=bk, il=il: e.transpose(bank_bf(bk)[:, 0:128], onb[il], ident_bf), r=[f"onb{il}"], w=[f"ps{bk}"])
                    P.op("act", lambda e, bk=bk, h=h, i=i: e.activation(attnT[:, h, i * 128:(i + 1) * 128], bank_bf(bk)[:, 0:128], AF.Copy),
                         r=[f"ps{bk}"], w=[f"attnT{i // 4}"])
                if stop_after != "M":
                    for b_ in range(p_next[0] + 4, p_next[0] + 8):
                        if b_ < NB:
                            p_load(b_)
                    for _ in range(4):
                        if p_next[0] < NB:
                            p_compute(p_next[0])
                            p_next[0] += 1
        P.barrier()
        A.off = mS

        mergedT = A.alloc(32768, BF16).rearrange("p (c t) -> p c t", t=S)
        sga = [A.alloc(2048, F32) for _ in range(2)]
        sgc = [A.alloc(2048, F32) for _ in range(2)]
        t1 = [A.alloc(2048, F32) for _ in range(2)]
        t2 = [A.alloc(2048, F32) for _ in range(2)]
        wao_v = w_attn_o.rearrange("(c p) n -> p c n", p=128)
        wco_v = w_conv_o.rearrange("(c p) n -> p c n", p=128)
        step = 0
        def merge_panels(cch):
            cs = slice(cch * 128, (cch + 1) * 128)
            return (load_panel(wao_v[:, :, cs]), load_panel(wco_v[:, :, cs]),
                    load_panel(w_in_v[:, :, 6144 + cch * 128:6144 + (cch + 1) * 128]),
                    load_panel(w_in_v[:, :, 7168 + cch * 128:7168 + (cch + 1) * 128]))

        mpan = {0: merge_panels(0)}
        for cch in range(8):
            if cch + 1 < 8:
                mpan[cch + 1] = merge_panels(cch + 1)
            (wao, kao), (wco, kco), (wga, kga), (wgc, kgc) = mpan.pop(cch)
            for tg in range(4):
                bset = (step % 2) * 4
                s = step % 2
                step += 1
                bA, bB, bC, bD = bset, bset + 1, bset + 2, bset + 3
                ts = slice(tg * 512, (tg + 1) * 512)
                for (bk, wp, wk_, src, sname) in ((bA, wao, kao, attnT, "attnT"), (bB, wco, kco, ycT, "ycT"),
                                                 (bC, wga, kga, nT, "nT"), (bD, wgc, kgc, nT, "nT")):
                    for c in range(8):
                        P.op("pe", lambda e, bk=bk, wp=wp, c=c, src=src, ts=ts: e.matmul(bank(bk), lhsT=wp[:, c, :], rhs=src[:, c, ts],
                                                                                     start=(c == 0), stop=(c == 7)),
                             r=[wk_, f"{sname}{tg}"], w=[f"ps{bk}"])
                P.op("act", lambda e, s=s, bC=bC: e.activation(sga[s], bank(bC), AF.Sigmoid), r=[f"ps{bC}"], w=[f"sga{s}"])
                P.op("act", lambda e, s=s, bD=bD: e.activation(sgc[s], bank(bD), AF.Sigmoid), r=[f"ps{bD}"], w=[f"sgc{s}"])
                P.op("dve", lambda e, s=s, bA=bA: e.tensor_tensor(t1[s], bank(bA), sga[s], ALU.mult), r=[f"ps{bA}", f"sga{s}"], w=[f"t1{s}"])
                P.op("dve", lambda e, s=s, bB=bB: e.tensor_tensor(t2[s], bank(bB), sgc[s], ALU.mult), r=[f"ps{bB}", f"sgc{s}"], w=[f"t2{s}"])
                P.op("dve", lambda e, s=s, cch=cch, ts=ts: e.tensor_tensor(mergedT[:, cch, ts], t1[s], t2[s], ALU.add),
                     r=[f"t1{s}", f"t2{s}"], w=[f"mg{tg}"])
        P.barrier()
        if debug:
            P.op("sp", lambda e: e.dma_start(out=dbg_nT, in_=nT.rearrange("p c t -> p (c t)")), w=["dbg1"], key="dbg1")
            P.op("sp", lambda e: e.dma_start(out=dbg_ycT, in_=ycT.rearrange("p c t -> p (c t)")), w=["dbg2"], key="dbg2")
            P.op("sp", lambda e: e.dma_start(out=dbg_attnT, in_=attnT.rearrange("p c t -> p (c t)")), w=["dbg3"], key="dbg3")
            P.op("sp", lambda e: e.dma_start(out=dbg_mg, in_=mergedT.rearrange("p c t -> p (c t)")), w=["dbg4"], key="dbg4")
            P.op("sp", lambda e: e.dma_start(out=dbg_small, in_=small), w=["dbg5"], key="dbg5")
            P.barrier()
        A.off = mP
        wout = A.alloc(16384, BF16).rearrange("p (c n) -> p c n", n=1024)
        xt2 = [A.alloc(4096, F32) for _ in range(2)]
        ht = [A.alloc(4096, F32) for _ in range(2)]
        assert A.off <= mP + 65536
        for c in range(8):
            P.op("pool", lambda e, c=c: e.dma_start(out=wout[:, c, :], in_=w_out[c * 128:(c + 1) * 128, :]), w=["wout"], key="wout")
        for tt in range(NT):
            s = tt % 2
            P.op("sp", lambda e, tt=tt, s=s: e.dma_start(out=xt2[s], in_=x[tt * 128:(tt + 1) * 128, :]), w=[f"xt2{s}"], key=f"xt2{s}")
            for half in range(2):
                bk = (tt * 2 + half) % 8
                hs_ = slice(half * 512, (half + 1) * 512)
                for c in range(8):
                    P.op("pe", lambda e, bk=bk, c=c, tt=tt, hs_=hs_: e.matmul(bank(bk), lhsT=mergedT[:, c, tt * 128:(tt + 1) * 128],
                                                                          rhs=wout[:, c, hs_], start=(c == 0), stop=(c == 7)),
                         r=["wout", f"mg{tt // 4}"], w=[f"ps{bk}"])
                P.op("dve", lambda e, bk=bk, s=s, hs_=hs_: e.tensor_tensor(ht[s][:, hs_], bank(bk), xt2[s][:, hs_], ALU.add),
                     r=[f"ps{bk}", f"xt2{s}"], w=[f"ht{s}"])
            P.op("sp", lambda e, tt=tt, s=s: e.dma_start(out=h_scr[tt * 128:(tt + 1) * 128, :], in_=ht[s]), r=[f"ht{s}"], w=["h_scr"], key=f"ht{s}")
        P.barrier()
        A.off = mP

        if stop_after != "M":
            HB = NB // 2
            GTh = [A.alloc(HB * TGE * 2, BF16).rearrange("p (i t) -> p i t", t=TGE) for _ in range(2)]
            xn2T = [A.alloc(8 * TGE * 2, BF16).rearrange("p (c t) -> p c t", t=TGE) for _ in range(2)]
            hsb = [[A.alloc(4096, F32) for _ in range(2)] for _ in range(2)]
            IJGT = [A.alloc(3 * TGE * 4, F32).rearrange("p (q t) -> p q t", t=TGE) for _ in range(2)]
            IJb = [A.alloc(2 * TGE * 2, BF16).rearrange("p (q t) -> p q t", t=TGE) for _ in range(2)]
            qTa = A.alloc(16 * TGE * 2, BF16).rearrange("p (g t) -> p g t", t=TGE)
            S2 = A.alloc(8192, F32).rearrange("p (g n) -> p g n", n=128)
            cand2 = S2.rearrange("p g n -> p (g n)").rearrange("p (h c) -> p h c", c=256)
            Eo = cand2.rearrange("p h (k a) -> p h k a", a=16)
            xnb = [A.alloc(2048, BF16) for _ in range(2)]
            tv = A.alloc(1024, F32).rearrange("p (g k) -> p g k", k=16)
            ti = A.alloc(1024, U32).rearrange("p (g k) -> p g k", k=16)
            tif = A.alloc(1024, F32).rearrange("p (g k) -> p g k", k=16)
            cand = A.alloc(8192, F32).rearrange("p (h c) -> p h c", c=256)
            cv = A.alloc(512, F32).rearrange("p (h k) -> p h k", k=16)
            ci = A.alloc(512, U32).rearrange("p (h k) -> p h k", k=16)
            cia = A.alloc(512, U32).rearrange("p (h k) -> p h k", k=16)
            cib = A.alloc(512, U32).rearrange("p (h k) -> p h k", k=16)
            af_ = A.alloc(512, F32).rearrange("p (h k) -> p h k", k=16)
            bf_ = A.alloc(512, F32).rearrange("p (h k) -> p h k", k=16)
            IJG = A.alloc(3 * 512, F32).rearrange("p (q h k) -> p q h k", q=3, k=16)
            dg = A.alloc(512, F32).rearrange("p (h k) -> p h k", k=16)
            eg = A.alloc(512, F32).rearrange("p (h k) -> p h k", k=16)
            zs = A.alloc(32, F32)
            rz = A.alloc(32, F32)
            CH = 8
            NOH = 3
            oh1 = [A.alloc(CH * 64 * 2, BF16).rearrange("p (t i) -> p t i", i=64) for _ in range(NOH)]
            oh2 = [A.alloc(CH * 128 * 2, BF16).rearrange("p (t i) -> p t i", i=128) for _ in range(NOH)]
            oh2g = [A.alloc(CH * 128 * 2, BF16).rearrange("p (t i) -> p t i", i=128) for _ in range(NOH)]
            NU = 6
            uTb = [A.alloc(2048, BF16).rearrange("p (c e) -> p c e", e=128) for _ in range(NU)]
            vb = [A.alloc(2048, BF16) for _ in range(NU)]
            geb = [A.alloc(TGE * 2, BF16) for _ in range(3)]
            hab = [A.alloc(TGE * 2, BF16) for _ in range(3)]
            NWQ = 2
            wqr = [A.alloc(2048, BF16).rearrange("p (c n) -> p c n", n=128) for _ in range(NWQ)]
            wq_v = wq_scr.rearrange("(c p) n -> p c n", p=128)
            iota16 = iota_f[:, 0:16]
            st = {"wq": 0, "oh": 0, "blk": 0, "pp": 0}

            class RPool:
                def __init__(self, items):
                    self.free = list(items)

                def acquire(self):
                    return self.free.pop(0) if self.free else None

                def release(self, x):
                    self.free.append(x)

            bank_pool = RPool([6, 7])
            oh_pool = RPool(list(range(NOH)))
            wq_pool = RPool(list(range(NWQ)))

            def prep_topk(g):
                gb = g % 2
                th = []

                def e1(_unused):
                    ssq2, ln2, rs2, pc = smp_slot()
                    hss = [hsb[gb][tl] for tl in range(2)]
                    hcs = [f"hsb{gb}{tl}" for tl in range(2)]
                    for tl in range(2):
                        tt = g * 2 + tl
                        P.op("sp", lambda e, tl=tl, tt=tt: e.dma_start(out=hss[tl], in_=h_scr[tt * 128:(tt + 1) * 128, :]),
                             r=["h_scr"], w=[hcs[tl]], key=hcs[tl])
                    yield
                    yield
                    for tl in range(2):
                        P.op("dve", lambda e, tl=tl: e.tensor_tensor_reduce(out=junk, in0=hss[tl], in1=hss[tl], scale=1.0, scalar=0.0,
                                                                            op0=ALU.mult, op1=ALU.add, accum_out=ssq2[:, tl:tl + 1]),
                             r=[hcs[tl]], w=[pc + f"s{tl}"])
                    yield
                    P.op("act", lambda e: e.activation(ln2, ssq2, AF.Ln, bias=EPS, scale=1.0 / D), r=[pc + "s0", pc + "s1"], w=[pc + "l"])
                    P.op("act", lambda e: e.activation(rs2, ln2, AF.Exp, scale=-0.5), r=[pc + "l"], w=[pc + "r"])
                    yield
                    yield
                    for tl in range(2):
                        P.op("dve", lambda e, tl=tl: e.scalar_tensor_tensor(out=xnb[tl], in0=hss[tl], scalar=rs2[:, tl:tl + 1], in1=g2b,
                                                                            op0=ALU.mult, op1=ALU.mult),
                             r=[hcs[tl], pc + "r", "g2b"], w=[f"xnb{tl}"])
                    yield
                    for tl in range(2):
                        while True:
                            pb = bank_pool.acquire()
                            if pb is not None:
                                break
                            yield
                        for c in range(8):
                            P.op("pe", lambda e, c=c, tl=tl, pb=pb: e.transpose(bank_bf(pb)[:, c * 128:(c + 1) * 128],
                                                                             xnb[tl][:, c * 128:(c + 1) * 128], ident_bf),
                                 r=[f"xnb{tl}"], w=[f"ps{pb}"])
                        yield
                        P.op("act", lambda e, tl=tl, pb=pb: e.activation(xn2T[gb][:, :, tl * 128:(tl + 1) * 128],
                                                                        bank_bf(pb).rearrange("p (c t) -> p c t", t=128), AF.Copy),
                             r=[f"ps{pb}"], w=[f"xn2T{gb}"])
                        bank_pool.release(pb)

                def e2(grp):
                    while True:
                        sl = wq_pool.acquire()
                        if sl is not None:
                            break
                        yield
                    P.op("sp", lambda e: e.dma_start(out=wqr[sl], in_=wq_v[:, :, grp * 128:(grp + 1) * 128]), w=[f"wq{sl}"], key=f"wq{sl}")
                    yield
                    yield
                    yield
                    while True:
                        bk = bank_pool.acquire()
                        if bk is not None:
                            break
                        yield
                    for c in range(8):
                        P.op("pe", lambda e, c=c: e.matmul(bank(bk)[:, 0:TGE], lhsT=wqr[sl][:, c, :], rhs=xn2T[gb][:, c, :],
                                                           start=(c == 0), stop=(c == 7)),
                             r=[f"wq{sl}", f"xn2T{gb}"], w=[f"ps{bk}"])
                    wq_pool.release(sl)
                    yield
                    P.op("act", lambda e: e.activation(qTa[:, grp, :], bank(bk)[:, 0:TGE], AF.Copy), r=[f"ps{bk}"], w=[f"qTa{grp}"])
                    bank_pool.release(bk)

                def e3(tl, quad):
                    while True:
                        bk = bank_pool.acquire()
                        if bk is not None:
                            break
                        yield
                    grps = [quad * 4 + q for q in range(4)]
                    scs = {grp: bank(bk)[:, (grp % 4) * 128:(grp % 4 + 1) * 128] for grp in grps}
                    for grp in grps:
                        P.op("pe", lambda e, grp=grp: e.matmul(scs[grp], lhsT=qTa[:, grp, tl * 128:(tl + 1) * 128], rhs=skT[:, grp, :],
                                                               start=True, stop=True),
                             r=[f"qTa{grp}", "skT"], w=[f"ps{bk}"])
                    yield
                    for grp in grps:
                        P.op("dve", lambda e, grp=grp: e.max(out=tv[:, grp, 0:8], in_=scs[grp]), r=[f"ps{bk}"], w=[f"tva{grp}"])
                    for grp in grps:
                        P.op("dve", lambda e, grp=grp: e.max_index(out=ti[:, grp, 0:8], in_max=tv[:, grp, 0:8], in_values=scs[grp]),
                             r=[f"ps{bk}", f"tva{grp}"], w=[f"tia{grp}"])
                    for grp in grps:
                        P.op("dve", lambda e, grp=grp: e.match_replace(out=S2[:, grp, :], in_to_replace=tv[:, grp, 0:8], in_values=scs[grp],
                                                                     imm_value=-1e30),
                             r=[f"ps{bk}", f"tva{grp}"], w=[f"S2{grp}"])
                    for grp in grps:
                        P.op("dve", lambda e, grp=grp: e.max(out=tv[:, grp, 8:16], in_=S2[:, grp, :]), r=[f"S2{grp}"], w=[f"tvb{grp}"])
                    for grp in grps:
                        P.op("dve", lambda e, grp=grp: e.max_index(out=ti[:, grp, 8:16], in_max=tv[:, grp, 8:16], in_values=S2[:, grp, :]),
                             r=[f"S2{grp}", f"tvb{grp}"], w=[f"tib{grp}"])
                    bank_pool.release(bk)

                TVC = [f"tva{g_}" for g_ in range(16)] + [f"tvb{g_}" for g_ in range(16)]
                TIC = [f"tia{g_}" for g_ in range(16)] + [f"tib{g_}" for g_ in range(16)]
                S2C = [f"S2{g_}" for g_ in range(16)]
                CVC = [f"cva{h_}" for h_ in range(8)] + [f"cvb{h_}" for h_ in range(8)]
                CIC = [f"cia{h_}" for h_ in range(8)] + [f"cib{h_}" for h_ in range(8)]
                C2C = [f"cand2_{h_}" for h_ in range(8)]
                tvv = tv.rearrange("p (h m) k -> p h m k", m=2)
                tifv = tif.rearrange("p (h m) k -> p h m k", m=2)
                candv = cand.rearrange("p h (a b) -> p h a b", b=16)

                def e4a(tl):
                    yield
                    P.op("dve", lambda e: e.tensor_copy(tif, ti), r=TIC, w=["tif"])
                    P.op("dve", lambda e: e.tensor_tensor(candv, tvv[:, :, 0, :].unsqueeze(3).broadcast_to([128, 8, 16, 16]),
                                                          tvv[:, :, 1, :].unsqueeze(2).broadcast_to([128, 8, 16, 16]), ALU.add),
                         r=TVC, w=[f"cand{h_}" for h_ in range(8)])
                    for h in range(8):
                        P.op("dve", lambda e, h=h: e.max(out=cv[:, h, 0:8], in_=cand[:, h, :]), r=[f"cand{h}"], w=[f"cva{h}"])
                    for h in range(8):
                        P.op("dve", lambda e, h=h: e.max_index(out=ci[:, h, 0:8], in_max=cv[:, h, 0:8], in_values=cand[:, h, :]),
                             r=[f"cand{h}", f"cva{h}"], w=[f"cia{h}"])
                    for h in range(8):
                        P.op("dve", lambda e, h=h: e.match_replace(out=cand2[:, h, :], in_to_replace=cv[:, h, 0:8], in_values=cand[:, h, :],
                                                                 imm_value=-1e30), r=[f"cand{h}", f"cva{h}"], w=[f"cand2_{h}"] + S2C[2 * h:2 * h + 2])

                def e4b(tl):
                    yield
                    for h in range(8):
                        P.op("dve", lambda e, h=h: e.max(out=cv[:, h, 8:16], in_=cand2[:, h, :]), r=[f"cand2_{h}"], w=[f"cvb{h}"])
                    for h in range(8):
                        P.op("dve", lambda e, h=h: e.max_index(out=ci[:, h, 8:16], in_max=cv[:, h, 8:16], in_values=cand2[:, h, :]),
                             r=[f"cand2_{h}", f"cvb{h}"], w=[f"cib{h}"])
                    P.op("dve", lambda e: e.tensor_single_scalar(cia, ci, 4, ALU.logical_shift_right), r=CIC, w=["cia"])
                    P.op("dve", lambda e: e.tensor_single_scalar(cib, ci, 15, ALU.bitwise_and), r=CIC, w=["cib"])
                    P.op("dve", lambda e: e.tensor_copy(af_, cia), r=["cia"], w=["af"])
                    P.op("dve", lambda e: e.tensor_copy(bf_, cib), r=["cib"], w=["bf"])

                def e4c(tl):
                    io4 = iota16.unsqueeze(1).unsqueeze(1).broadcast_to([128, 8, 16, 16])
                    for (q, src, mm) in ((0, af_, 0), (1, bf_, 1)):
                        P.op("dve", lambda e, src=src: e.tensor_tensor(Eo, src.unsqueeze(3).broadcast_to([128, 8, 16, 16]), io4, ALU.is_equal),
                             r=["af", "bf", "iota_f"], w=C2C + S2C)
                        P.op("dve", lambda e, mm=mm: e.tensor_tensor(Eo, Eo, tifv[:, :, mm, :].unsqueeze(2).broadcast_to([128, 8, 16, 16]), ALU.mult),
                             r=C2C + ["tif"], w=C2C + S2C)
                        P.op("dve", lambda e, q=q: e.tensor_reduce(out=IJG[:, q, :, :], in_=Eo, axis=AX.X, op=ALU.add), r=C2C + S2C, w=["IJG"])
                    P.op("dve", lambda e: e.tensor_tensor(dg, cv, cv[:, :, 0:1].broadcast_to([128, 8, 16]), ALU.subtract), r=CVC, w=["dg"])
                    yield
                    P.op("act", lambda e: e.activation(eg, dg, AF.Exp), r=["dg"], w=["eg"])
                    yield
                    P.op("dve", lambda e: e.tensor_reduce(out=zs, in_=eg, axis=AX.X, op=ALU.add), r=["eg"], w=["zs"])
                    P.op("dve", lambda e: e.reciprocal(rz, zs), r=["zs"], w=["rz"])
                    P.op("dve", lambda e: e.tensor_tensor(IJG[:, 2, :, :], eg, rz.unsqueeze(2).broadcast_to([128, 8, 16]), ALU.mult),
                         r=["eg", "rz", "IJG"], w=["IJG"])
                    yield
                    while True:
                        pb = bank_pool.acquire()
                        if pb is not None:
                            break
                        yield
                    for q in range(3):
                        P.op("pe", lambda e, q=q: e.transpose(bank(pb)[:, q * 128:(q + 1) * 128], IJG[:, q, :, :].rearrange("p h k -> p (h k)"), ident_f),
                             r=["IJG", "ident_f"], w=[f"ps{pb}"])
                    yield
                    P.op("act", lambda e: e.activation(IJGT[gb][:, :, tl * 128:(tl + 1) * 128],
                                                       bank(pb)[:, 0:384].rearrange("p (q t) -> p q t", t=128), AF.Copy),
                         r=[f"ps{pb}"], w=[f"IJGT{gb}"])
                    P.op("act", lambda e: e.activation(IJb[gb][:, :, tl * 128:(tl + 1) * 128],
                                                       bank(pb)[:, 0:256].rearrange("p (q t) -> p q t", t=128), AF.Copy),
                         r=[f"ps{pb}"], w=[f"IJb{gb}"])
                    bank_pool.release(pb)

                th.append(lambda: e1(0))
                th.append("FENCE")
                for grp in range(16):
                    th.append(lambda grp=grp: e2(grp))
                th.append("FENCE")
                for tl in range(2):
                    for quad in range(4):
                        th.append(lambda tl=tl, quad=quad: e3(tl, quad))
                    th.append("FENCE")
                    th.append(lambda tl=tl: e4a(tl))
                    th.append("FENCE")
                    th.append(lambda tl=tl: e4b(tl))
                    th.append("FENCE")
                    th.append(lambda tl=tl: e4c(tl))
                    th.append("FENCE")
                return th

            def prep_G(g, half):
                gb = g % 2
                th = []

                def chunk(ch):
                    while True:
                        s = oh_pool.acquire()
                        if s is not None:
                            break
                        yield
                    c0 = ch * CH
                    iob = iota_b.unsqueeze(1).broadcast_to([128, CH, 128])
                    iobh = iota_b[:, 64 * half:64 * half + 64].unsqueeze(1).broadcast_to([128, CH, 64])
                    P.op("dve", lambda e: e.tensor_tensor(oh1[s], iobh, IJb[gb][:, 0, c0:c0 + CH].unsqueeze(2).broadcast_to([128, CH, 64]), ALU.is_equal),
                         r=[f"IJb{gb}", "iota_b"], w=[f"oh1{s}"])
                    P.op("dve", lambda e: e.tensor_tensor(oh2[s], iob, IJb[gb][:, 1, c0:c0 + CH].unsqueeze(2).broadcast_to([128, CH, 128]), ALU.is_equal),
                         r=[f"IJb{gb}", "iota_b"], w=[f"oh2{s}"])
                    P.op("pool", lambda e: e.tensor_tensor(oh2g[s], oh2[s], IJGT[gb][:, 2, c0:c0 + CH].unsqueeze(2).broadcast_to([128, CH, 128]), ALU.mult),
                         r=[f"IJGT{gb}", f"oh2{s}"], w=[f"oh2g{s}"])
                    yield
                    yield
                    while True:
                        bk = bank_pool.acquire()
                        if bk is not None:
                            break
                        yield
                    for t in range(CH):
                        P.op("pe", lambda e, t=t: e.matmul(bank(bk)[:, t * 64:(t + 1) * 64], lhsT=oh2g[s][:, t, :], rhs=oh1[s][:, t, :],
                                                           start=True, stop=True),
                             r=[f"oh1{s}", f"oh2g{s}"], w=[f"ps{bk}"])
                    oh_pool.release(s)
                    yield
                    P.op("act", lambda e: e.activation(GTh[half][:, :, c0:c0 + CH].rearrange("p i t -> p t i"),
                                                       bank(bk).rearrange("p (t i) -> p t i", i=64), AF.Copy),
                         r=[f"ps{bk}"], w=[f"GT{half}"])
                    bank_pool.release(bk)

                for ch in range(TGE // CH):
                    th.append(lambda ch=ch: chunk(ch))
                return th

            def emit_U(g, i):
                gb = g % 2
                k = st["blk"]
                st["blk"] += 1
                s = k % NU
                pa = 4 + k % 2
                s2 = k % 3
                half = i // HB
                P.op("sp", lambda e: e.dma_start(out=uTb[s], in_=uT_scr[i].rearrange("p (c e) -> p c e", e=128)), w=[f"uTb{s}"], key=f"uTb{s}")
                P.op("sp", lambda e: e.dma_start(out=vb[s], in_=v_scr[i * 128:(i + 1) * 128, :]), w=[f"vb{s}"], key=f"vb{s}")
                for c in range(8):
                    P.op("pe", lambda e, c=c: e.matmul(bank(pa)[:, 0:TGE], lhsT=uTb[s][:, c, :], rhs=xn2T[gb][:, c, :],
                                                       start=(c == 0), stop=(c == 7)),
                         r=[f"uTb{s}", f"xn2T{gb}"], w=[f"ps{pa}"])
                P.op("act", lambda e: e.activation(geb[s2], bank(pa)[:, 0:TGE], AF.Gelu), r=[f"ps{pa}"], w=[f"geb{s2}"])
                P.op("dve", lambda e: e.tensor_tensor(hab[s2], geb[s2], GTh[half][:, i % HB, :], ALU.mult),
                     r=[f"geb{s2}", f"GT{half}"], w=[f"hab{s2}"])
                return (s, s2)

            def emit_V(i, ss):
                s, s2 = ss
                for tl in range(2):
                    for hf_ in range(2):
                        bk = tl * 2 + hf_
                        P.op("pe", lambda e, tl=tl, hf_=hf_, bk=bk: e.matmul(
                            bank(bk), lhsT=hab[s2][:, tl * 128:(tl + 1) * 128], rhs=vb[s][:, hf_ * 512:(hf_ + 1) * 512],
                            start=(i == 0), stop=(i == NB - 1)),
                             r=[f"hab{s2}", f"vb{s}"], w=[f"ps{bk}"])

            def group_end(g):
                gb = g % 2
                ssq2, ln2, rs2, pc = smp_slot()
                hss = [hsb[gb][tl] for tl in range(2)]
                hcs = [f"hsb{gb}{tl}" for tl in range(2)]
                for tl in range(2):
                    for hf_ in range(2):
                        bk = tl * 2 + hf_
                        hs_ = slice(hf_ * 512, (hf_ + 1) * 512)
                        P.op("dve", lambda e, bk=bk, hs_=hs_, tl=tl: e.tensor_tensor(hss[tl][:, hs_], bank(bk), hss[tl][:, hs_], ALU.add),
                             r=[f"ps{bk}", hcs[tl]], w=[hcs[tl]])
                    P.op("dve", lambda e, tl=tl: e.tensor_tensor_reduce(out=junk, in0=hss[tl], in1=hss[tl], scale=1.0, scalar=0.0,
                                                                        op0=ALU.mult, op1=ALU.add, accum_out=ssq2[:, tl:tl + 1]),
                         r=[hcs[tl]], w=[pc + f"s{tl}"])
                yield
                P.op("act", lambda e: e.activation(ln2, ssq2, AF.Ln, bias=EPS, scale=1.0 / D), r=[pc + "s0", pc + "s1"], w=[pc + "l"])
                P.op("act", lambda e: e.activation(rs2, ln2, AF.Exp, scale=-0.5), r=[pc + "l"], w=[pc + "r"])
                yield
                yield
                for tl in range(2):
                    tt = g * 2 + tl
                    P.op("dve", lambda e, tl=tl: e.scalar_tensor_tensor(out=hss[tl], in0=hss[tl], scalar=rs2[:, tl:tl + 1], in1=gfb,
                                                                        op0=ALU.mult, op1=ALU.mult),
                         r=[hcs[tl], pc + "r", "gfb"], w=[hcs[tl]])
                    P.op("sp", lambda e, tl=tl, tt=tt: e.dma_start(out=out[tt * 128:(tt + 1) * 128, :], in_=hss[tl]),
                         r=[hcs[tl]], w=["out"], key=hcs[tl])

            class Stream:
                def __init__(self, items):
                    self.items = list(items)
                    self.active = []

                def pending(self):
                    return sum(1 for x_ in self.items if x_ != "FENCE")

                def busy(self):
                    return bool(self.items or self.active)

                def step(self, nstart):
                    for gen in list(self.active):
                        try:
                            next(gen)
                        except StopIteration:
                            self.active.remove(gen)
                    started = 0
                    while self.items and started < nstart:
                        if self.items[0] == "FENCE":
                            if self.active:
                                break
                            self.items.pop(0)
                            continue
                        gen = self.items.pop(0)()
                        try:
                            next(gen)
                            self.active.append(gen)
                        except StopIteration:
                            pass
                        started += 1

                def drain(self, tag=""):
                    n0 = self.pending(); a0 = len(self.active); k = 0
                    while self.busy():
                        self.step(4); k += 1
                    if debug and (n0 or a0):
                        print(f"drain {tag}: pending={n0} active={a0} steps={k}")

            ge_streams = []
            Stream(prep_topk(0)).drain()
            Stream(prep_G(0, 0)).drain()
            for g in range(NGE):
                sB = Stream(prep_G(g, 1))
                sT = Stream(prep_topk(g + 1) if g + 1 < NGE else [])
                sA = Stream(prep_G(g + 1, 0) if g + 1 < NGE else [])
                sE = ge_streams.pop(0) if ge_streams else None
                pend = {0: emit_U(g, 0), 1: emit_U(g, 1)}
                for i in range(NB):
                    ii = i % HB
                    ep_busy = sE is not None and sE.busy()
                    if ep_busy:
                        sE.step(0)
                    if i < HB:
                        left = max(1, (HB - 14) - ii)
                        sB.step((sB.pending() + left - 1) // left if sB.pending() else 0)
                        if not ep_busy:
                            sT.step(2)
                    else:
                        if i == HB:
                            sT.drain(f'g{g} sT@HB')
                        left = max(1, (HB - 8) - ii)
                        sA.step((sA.pending() + left - 1) // left if sA.pending() else 0)
                    if i + 2 < NB:
                        if (i + 2) == HB:
                            sB.drain(f'g{g} sB@62')
                        pend[i + 2] = emit_U(g, i + 2)
                    emit_V(i, pend.pop(i))
                sB.drain(f'g{g} sB@end')
                sT.drain(f'g{g} sT@end')
                sA.drain(f'g{g} sA@end')
                if sE is not None:
                    sE.drain()
                sEn = Stream([lambda g=g: group_end(g)])
                sEn.step(1)
                ge_streams.append(sEn)
            ge_streams[0].drain()
        P.barrier()

        P.finalize()
        sems = {}
        for en in Prog.ENGS:
            sems[("eng", en)] = es.enter_context(nc.semaphore(f"sem_{en}"))
        for i, k in enumerate(sorted(P.dma_count.keys())):
            sems[("dma", k)] = es.enter_context(nc.semaphore(f"semd_{i}"))
        block = es.enter_context(nc.Block())

        @block.tensor
        def _(e):
            P.emit("pe", e, sems)

        @block.scalar
        def _(e):
            P.emit("act", e, sems)

        @block.vector
        def _(e):
            P.emit("dve", e, sems)

        @block.gpsimd
        def _(e):
            P.emit("pool", e, sems)

        @block.sync
        def _(e):
            P.emit("sp", e, sems)

    mybir.codegen_inst_isa_subclasses(nc)
    return nc


_NC_CACHE = {}


def _prep_inputs(inputs):
    f = lambda a: np.ascontiguousarray(np.asarray(a, dtype=np.float32))
    shared = {
        "norm1_g": f(inputs["norm1_g"]).reshape(D),
        "w_in": f(inputs["w_in"]).reshape(D, 8192),
        "lambda_qk": f(inputs["lambda_qk"]).reshape(256),
        "subln_g": f(inputs["subln_g"]).reshape(128),
        "conv_w": f(inputs["conv_w"]).reshape(3, D),
        "w_attn_o": f(inputs["w_attn_o"]).reshape(D, D),
        "w_conv_o": f(inputs["w_conv_o"]).reshape(D, D),
        "w_out": f(inputs["w_out"]).reshape(D, D),
        "norm2_g": f(inputs["norm2_g"]).reshape(D),
        "w_query": f(inputs["w_query"]).reshape(D, 2048),
        "sub_keys": f(inputs["sub_keys"]).reshape(8, 2, 128, 128),
        "expert_u": f(inputs["expert_u"]).reshape(16384, D),
        "expert_v": f(inputs["expert_v"]).reshape(16384, D),
        "final_g": f(inputs["final_g"]).reshape(D),
    }
    xs = f(inputs["x"])
    in_maps = []
    for b in range(8):
        m = dict(shared)
        m["x"] = np.ascontiguousarray(xs[b])
        in_maps.append(m)
    return in_maps


def kernel(**inputs):
    if "nc" not in _NC_CACHE:
        _NC_CACHE["nc"] = build_nc()
    nc = _NC_CACHE["nc"]
    in_maps = _prep_inputs(inputs)
    res = run_bass_kernel_spmd(nc, in_maps, core_ids=list(range(8)))
    outs = [np.asarray(r["out"], dtype=np.float32).reshape(S, D) for r in res.results]
    return np.stack(outs, axis=0)
```

```python
import os
from contextlib import ExitStack

import numpy as np
import concourse.bass as bass
import concourse.mybir as mybir
from concourse.bass_utils import run_bass_kernel_spmd

F32 = mybir.dt.float32
BF16 = mybir.dt.bfloat16
U8 = mybir.dt.uint8
U32 = mybir.dt.uint32
I32 = mybir.dt.int32
AF = mybir.ActivationFunctionType
ALU = mybir.AluOpType
AX = mybir.AxisListType

S = 2048
D = 1024
NT = 16
EPS = 1e-6
LAM_INIT = 0.2
NB = 128
TGE = 256
NGE = S // TGE


class _Op:
    __slots__ = ("eng", "fn", "deps", "key", "dma_cnt", "need_inc", "cnt", "barrier", "snap")

    def __init__(self, eng, fn, deps, key):
        self.eng = eng
        self.fn = fn
        self.deps = deps
        self.key = key
        self.dma_cnt = 0
        self.need_inc = False
        self.cnt = 0
        self.barrier = False
        self.snap = None


class Prog:
    ENGS = ("pe", "act", "dve", "pool", "sp")

    def __init__(self):
        self.ops = []
        self.last_w = {}
        self.readers = {}
        self.dma_count = {}

    def op(self, eng, fn, r=(), w=(), key=None):
        idx = len(self.ops)
        deps = set()
        for c in r:
            if c in self.last_w:
                deps.add(self.last_w[c])
        for c in w:
            if c in self.last_w:
                deps.add(self.last_w[c])
            for x in self.readers.get(c, ()):
                deps.add(x)
        o = _Op(eng, fn, deps, key)
        if key is not None:
            self.dma_count[key] = self.dma_count.get(key, 0) + 16
            o.dma_cnt = self.dma_count[key]
        self.ops.append(o)
        for c in w:
            self.last_w[c] = idx
            self.readers[c] = []
        for c in r:
            if c not in w:
                self.readers.setdefault(c, []).append(idx)
        return idx

    def barrier(self):
        o = _Op(None, None, set(), None)
        o.barrier = True
        self.ops.append(o)
        self.last_w = {}
        self.readers = {}

    def finalize(self):
        ops = self.ops
        for o in ops:
            if o.barrier:
                continue
            for d in o.deps:
                dep = ops[d]
                if dep.key is None and not (o.eng == "pe" and dep.eng == "pe"):
                    dep.need_inc = True
        last_on = {e: None for e in self.ENGS}
        for i, o in enumerate(ops):
            if o.barrier:
                for e in self.ENGS:
                    if last_on[e] is not None:
                        ops[last_on[e]].need_inc = True
            elif o.key is None:
                last_on[o.eng] = i
        cnt = {e: 0 for e in self.ENGS}
        dcnt = {}
        for o in ops:
            if o.barrier:
                o.snap = (dict(cnt), dict(dcnt))
                continue
            if o.key is not None:
                dcnt[o.key] = o.dma_cnt
            elif o.need_inc:
                cnt[o.eng] += 1
                o.cnt = cnt[o.eng]

    def emit(self, eng_name, e, sems):
        ops = self.ops
        seen = {}

        def wait(s, v):
            if v > 0 and seen.get(s, 0) < v:
                e.wait_ge(sems[s], v)
                seen[s] = v

        for o in ops:
            if o.barrier:
                cnt, dcnt = o.snap
                for b, v in cnt.items():
                    if b != eng_name:
                        wait(("eng", b), v)
                for k, v in dcnt.items():
                    wait(("dma", k), v)
                continue
            if o.eng != eng_name:
                continue
            waits = {}
            for d in o.deps:
                dep = ops[d]
                if dep.key is not None:
                    s, v = ("dma", dep.key), dep.dma_cnt
                else:
                    if eng_name == "pe" and dep.eng == "pe":
                        continue
                    s, v = ("eng", dep.eng), dep.cnt
                if waits.get(s, 0) < v:
                    waits[s] = v
            for s, v in waits.items():
                wait(s, v)
            ins = o.fn(e)
            if o.key is not None:
                ins.then_inc(sems[("dma", o.key)], 16)
            elif o.need_inc:
                ins.then_inc(sems[("eng", eng_name)], 1)


class Arena:
    def __init__(self, ap, size):
        self.ap = ap
        self.size = size
        self.off = 0

    def alloc(self, nbytes, dtype):
        o = (self.off + 63) // 64 * 64
        assert o + nbytes <= self.size, f"SBUF arena overflow {o + nbytes} > {self.size}"
        self.off = o + nbytes
        return self.ap[:, o:o + nbytes].bitcast(dtype)


def build_nc(debug=False, stop_after=None):
    nc = bass.Bass("TRN2", target_bir_lowering=False)

    def din(name, shape, dtype=F32):
        return nc.dram_tensor(name, shape, dtype, kind="ExternalInput").ap()

    x = din("x", [S, D])
    norm1_g = din("norm1_g", [D])
    w_in = din("w_in", [D, 8192])
    lambda_qk = din("lambda_qk", [256])
    subln_g = din("subln_g", [128])
    conv_w = din("conv_w", [3, D])
    w_attn_o = din("w_attn_o", [D, D])
    w_conv_o = din("w_conv_o", [D, D])
    w_out = din("w_out", [D, D])
    norm2_g = din("norm2_g", [D])
    w_query = din("w_query", [D, 2048])
    sub_keys = din("sub_keys", [8, 2, 128, 128])
    expert_u = din("expert_u", [16384, D])
    expert_v = din("expert_v", [16384, D])
    final_g = din("final_g", [D])
    out = nc.dram_tensor("out", [S, D], F32, kind="ExternalOutput").ap()
    skind = "ExternalOutput" if debug else "Internal"
    uT_scr = nc.dram_tensor("uT_scr", [NB, 128, 1024], BF16, kind=skind).ap()
    v_scr = nc.dram_tensor("v_scr", [16384, D], BF16, kind=skind).ap()
    h_scr = nc.dram_tensor("h_scr", [S, D], F32, kind=skind).ap()
    wq_scr = nc.dram_tensor("wq_scr", [D, 2048], BF16, kind="Internal").ap()

    if debug:
        dbg_nT = nc.dram_tensor("dbg_nT", [128, 8 * S], BF16, kind="ExternalOutput").ap()
        dbg_ycT = nc.dram_tensor("dbg_ycT", [128, 8 * S], BF16, kind="ExternalOutput").ap()
        dbg_attnT = nc.dram_tensor("dbg_attnT", [128, 8 * S], BF16, kind="ExternalOutput").ap()
        dbg_mg = nc.dram_tensor("dbg_mg", [128, 8 * S], BF16, kind="ExternalOutput").ap()
        dbg_small = nc.dram_tensor("dbg_small", [128, 256], F32, kind="ExternalOutput").ap()

    P = Prog()
    ARENA_BYTES = 204 * 1024

    with ExitStack() as es:
        arena_t = es.enter_context(nc.sbuf_tensor("arena", [128, ARENA_BYTES], U8))
        ps = es.enter_context(nc.psum_tensor("ps", [128, 4096], F32))
        A = Arena(arena_t, ARENA_BYTES)

        def bank(b):
            return ps[:, b * 512:(b + 1) * 512]

        def bank_bf(b):
            return bank(b).bitcast(BF16)

        ident_bf = A.alloc(256, BF16)
        ident_f = A.alloc(512, F32)
        mask_tri = A.alloc(256, BF16)
        iota_f = A.alloc(512, F32)
        iota_i = A.alloc(512, I32)
        diff_i = A.alloc(512, I32)
        diff_f = A.alloc(512, F32)
        g1b = A.alloc(4096, F32)
        g2b = A.alloc(4096, F32)
        gfb = A.alloc(4096, F32)
        subgb = A.alloc(512, F32)
        cw = A.alloc(96, F32).rearrange("p (c j) -> p c j", j=3)
        lq = A.alloc(1024, F32)
        lam_s = A.alloc(64, F32)
        skT = A.alloc(4096, BF16).rearrange("p (g n) -> p g n", n=128)
        small = A.alloc(64 * 4 * 4, F32)
        junk = A.alloc(2048, BF16)
        junk_f = A.alloc(512, F32)
        smp = A.alloc(6 * 16 * 4, F32)
        iota_b = A.alloc(256, BF16)
        c_mhalf = A.alloc(4, F32)
        c_e = A.alloc(512, F32)
        const_mark = A.off

        sm_ctr = [0]

        def sm():
            k = sm_ctr[0] % 256
            sm_ctr[0] += 1
            return small[:, k:k + 1], f"sm{k}"

        P.op("sp", lambda e: e.dma_start(out=g1b, in_=norm1_g.partition_broadcast(128)), w=["g1b"], key="c0_0")
        P.op("sp", lambda e: e.dma_start(out=g2b, in_=norm2_g.partition_broadcast(128)), w=["g2b"], key="c0_1")
        P.op("sp", lambda e: e.dma_start(out=gfb, in_=final_g.partition_broadcast(128)), w=["gfb"], key="c0_2")
        P.op("sp", lambda e: e.dma_start(out=subgb, in_=subln_g.partition_broadcast(128)), w=["subgb"], key="c0_3")
        P.op("sp", lambda e: e.dma_start(out=lq, in_=lambda_qk.partition_broadcast(128)), w=["lq"], key="c0_4")
        for j in range(3):
            for c in range(8):
                P.op("sp", lambda e, j=j, c=c: e.dma_start(out=cw[:, c, j:j + 1],
                                                           in_=conv_w[j, c * 128:(c + 1) * 128].rearrange("(p o) -> p o", o=1)),
                     w=[f"cw_{j}_{c}"], key="c0_5")
        P.op("dve", lambda e: e.memset(c_mhalf, -0.5), w=["c_mhalf"])
        P.op("dve", lambda e: e.memset(c_e, float(np.float32(np.e))), w=["c_e"])
        P.op("pool", lambda e: e.iota(iota_i, pattern=[[1, 128]], base=0, channel_multiplier=0), w=["iota_i"])
        P.op("pool", lambda e: e.iota(diff_i, pattern=[[1, 128]], base=0, channel_multiplier=-1), w=["diff_i"])
        P.op("dve", lambda e: e.tensor_copy(iota_f, iota_i), r=["iota_i"], w=["iota_f"])
        P.op("dve", lambda e: e.tensor_copy(diff_f, diff_i), r=["diff_i"], w=["diff_f"])
        P.op("dve", lambda e: e.tensor_copy(iota_b, iota_i), r=["iota_i"], w=["iota_b"])
        P.op("dve", lambda e: e.tensor_single_scalar(ident_f, diff_f, 0.0, ALU.is_equal), r=["diff_f"], w=["ident_f"])
        P.op("dve", lambda e: e.tensor_single_scalar(ident_bf, diff_f, 0.0, ALU.is_equal), r=["diff_f"], w=["ident_bf"])
        P.op("dve", lambda e: e.tensor_single_scalar(mask_tri, diff_f, 0.0, ALU.is_ge), r=["diff_f"], w=["mask_tri"])
        P.op("dve", lambda e: e.tensor_scalar(subgb, subgb, 1.0 - LAM_INIT, None, ALU.mult), r=["subgb"], w=["subgb"])
        P.op("dve", lambda e: e.tensor_tensor_reduce(out=junk_f[:, 0:64], in0=lq[:, 0:64], in1=lq[:, 64:128],
                                                     scale=1.0, scalar=0.0, op0=ALU.mult, op1=ALU.add,
                                                     accum_out=lam_s[:, 0:1]), r=["lq"], w=["lam0"])
        P.op("dve", lambda e: e.tensor_tensor_reduce(out=junk_f[:, 64:128], in0=lq[:, 128:192], in1=lq[:, 192:256],
                                                     scale=1.0, scalar=0.0, op0=ALU.mult, op1=ALU.add,
                                                     accum_out=lam_s[:, 1:2]), r=["lq"], w=["lam1"])
        P.op("act", lambda e: e.activation(lam_s[:, 2:4], lam_s[:, 0:2], AF.Exp), r=["lam0", "lam1"], w=["lam2"])
        P.op("dve", lambda e: e.tensor_tensor(lam_s[:, 4:5], lam_s[:, 3:4], lam_s[:, 2:3], ALU.subtract),
             r=["lam2"], w=["lam4"])
        neglam = lam_s[:, 5:6]
        P.op("dve", lambda e: e.tensor_scalar(neglam, lam_s[:, 4:5], -LAM_INIT, None, ALU.add),
             r=["lam4"], w=["neglam"])

        m0 = A.off
        sk_nat = A.alloc(4096, BF16).rearrange("p (g k) -> p g k", k=128)
        P.op("pool", lambda e: e.dma_start(out=sk_nat, in_=sub_keys.rearrange("h m n k -> n (h m) k")),
             w=["sk_nat"], key="c1")
        for g4 in range(4):
            for q in range(4):
                g = g4 * 4 + q
                P.op("pe", lambda e, g=g, q=q, g4=g4: e.transpose(bank_bf(g4)[:, q * 128:(q + 1) * 128], sk_nat[:, g, :], ident_bf),
                     r=["sk_nat", "ident_bf"], w=[f"ps{g4}"])
            P.op("act", lambda e, g4=g4: e.activation(skT[:, g4 * 4:(g4 + 1) * 4, :],
                                                      bank_bf(g4)[:, 0:512].rearrange("p (g n) -> p g n", n=128), AF.Copy),
                 r=[f"ps{g4}"], w=["skT"])
        P.barrier()
        A.off = m0

        def rstd_from(ssq_ap, ssq_cell, n):
            lnv, c1 = sm()
            rs, c2 = sm()
            P.op("act", lambda e: e.activation(lnv, ssq_ap, AF.Ln, bias=EPS, scale=1.0 / n), r=[ssq_cell], w=[c1])
            P.op("act", lambda e: e.activation(rs, lnv, AF.Exp, scale=-0.5), r=[c1], w=[c2])
            return rs, c2

        smp_ctr = [0]

        def smp_slot():
            k = smp_ctr[0] % 16
            smp_ctr[0] += 1
            return smp[:, k * 6:k * 6 + 2], smp[:, k * 6 + 2:k * 6 + 4], smp[:, k * 6 + 4:k * 6 + 6], f"smp{k}"

        def rstd_pool(ssq_ap, ssq_cell, n):
            mse, c1 = sm()
            rs, c2 = sm()
            P.op("dve", lambda e: e.tensor_scalar(mse, ssq_ap, 1.0 / n, EPS, ALU.mult, ALU.add), r=[ssq_cell], w=[c1])
            P.op("pool", lambda e: e.tensor_tensor(rs, mse, c_mhalf, ALU.pow), r=[c1, "c_mhalf"], w=[c2])
            return rs, c2

        mP = A.off
        ub = [A.alloc(2048, BF16) for _ in range(3)]
        uTo = [A.alloc(2048, BF16) for _ in range(3)]
        conv_jobs = []
        if stop_after != "M":
            for r in range(8):
                conv_jobs.append(lambda r=r: P.op("pool", lambda e: e.dma_start(out=wq_scr[r * 128:(r + 1) * 128, :], in_=w_query[r * 128:(r + 1) * 128, :]),
                                                  w=["wq_scr"], key="wqconv"))
            for r in range(32):
                conv_jobs.append(lambda r=r: P.op("pool", lambda e: e.dma_start(out=v_scr[r * 512:(r + 1) * 512, :], in_=expert_v[r * 512:(r + 1) * 512, :]),
                                                  w=["v_scr"], key="vconv"))

        ub8 = []

        def p_load(b):
            s = b % 8
            P.op("pool", lambda e: e.dma_start(out=ub8[s], in_=expert_u[b * 128:(b + 1) * 128, :]), w=[f"ub{s}"], key=f"ub{s}")

        def p_compute(b):
            nonlocal sb_ctr
            s = b % 8
            so = b % 3
            pb = sb_ctr % 4
            sb_ctr += 1
            for c in range(8):
                P.op("pe", lambda e, c=c: e.transpose(bank_bf(pb)[:, c * 128:(c + 1) * 128], ub8[s][:, c * 128:(c + 1) * 128], ident_bf),
                     r=[f"ub{s}"], w=[f"ps{pb}"])
            if b % 2 == 0:
                P.op("act", lambda e: e.activation(uTo[so], bank_bf(pb), AF.Copy), r=[f"ps{pb}"], w=[f"uTo{so}"])
            else:
                P.op("dve", lambda e: e.tensor_copy(uTo[so], bank_bf(pb)), r=[f"ps{pb}"], w=[f"uTo{so}"])
            P.op("sp", lambda e: e.dma_start(out=uT_scr[b], in_=uTo[so]), r=[f"uTo{so}"], w=["uT_scr"], key=f"uTo{so}")

        p_next = [0]

        nT = A.alloc(32768, BF16).rearrange("p (c t) -> p c t", t=S)
        ycT = A.alloc(32768, BF16).rearrange("p (c t) -> p c t", t=S)
        attnT = A.alloc(32768, BF16).rearrange("p (c t) -> p c t", t=S)
        NW = 8
        wring = [A.alloc(2048, BF16).rearrange("p (c n) -> p c n", n=128) for _ in range(NW)]
        w_ctr = [0]

        def load_panel(src_ap):
            s = w_ctr[0] % NW
            w_ctr[0] += 1
            P.op("pool", lambda e: e.dma_start(out=wring[s], in_=src_ap), w=[f"w{s}"], key=f"w{s}")
            return wring[s], f"w{s}"

        w_in_v = w_in.rearrange("(c p) n -> p c n", p=128)
        ps_ctr = [0]

        mS = A.off
        xt = [A.alloc(4096, F32) for _ in range(4)]
        nb = [A.alloc(2048, BF16) for _ in range(4)]
        for tt in range(NT):
            s = tt % 4
            pb = tt % 4
            P.op("sp", lambda e, tt=tt, s=s: e.dma_start(out=xt[s], in_=x[tt * 128:(tt + 1) * 128, :]), w=[f"xt{s}"], key=f"xt{s}")
            ssq, cq = sm()
            P.op("dve", lambda e, s=s, ssq=ssq: e.tensor_tensor_reduce(out=junk, in0=xt[s], in1=xt[s], scale=1.0, scalar=0.0,
                                                                       op0=ALU.mult, op1=ALU.add, accum_out=ssq),
                 r=[f"xt{s}"], w=[cq])
            rs, cr = rstd_from(ssq, cq, D)
            P.op("dve", lambda e, s=s, rs=rs: e.scalar_tensor_tensor(out=nb[s], in0=xt[s], scalar=rs, in1=g1b, op0=ALU.mult, op1=ALU.mult),
                 r=[f"xt{s}", cr, "g1b"], w=[f"nb{s}"])
            for c in range(8):
                P.op("pe", lambda e, s=s, c=c, pb=pb: e.transpose(bank_bf(pb)[:, c * 128:(c + 1) * 128],
                                                                nb[s][:, c * 128:(c + 1) * 128], ident_bf),
                     r=[f"nb{s}"], w=[f"ps{pb}"])
            P.op("act", lambda e, tt=tt, pb=pb: e.activation(nT[:, :, tt * 128:(tt + 1) * 128],
                                                            bank_bf(pb).rearrange("p (c t) -> p c t", t=128), AF.Copy),
                 r=[f"ps{pb}"], w=[f"nT{tt // 4}"])
        P.barrier()
        A.off = mS

        uconv = [A.alloc(2050 * 4, F32) for _ in range(2)]
        tmp1 = [A.alloc(2048, F32) for _ in range(2)]
        zt = [A.alloc(2048, F32) for _ in range(2)]
        for k in range(2):
            P.op("dve", lambda e, k=k: e.memset(uconv[k][:, 0:2], 0.0), w=[f"uconv{k}"])
        step = 0
        def conv_panels(cch):
            return (load_panel(w_in_v[:, :, 3072 + cch * 128:3072 + (cch + 1) * 128]),
                    load_panel(w_in_v[:, :, 4096 + cch * 128:4096 + (cch + 1) * 128]),
                    load_panel(w_in_v[:, :, 5120 + cch * 128:5120 + (cch + 1) * 128]))

        cpan = {0: conv_panels(0)}
        for cch in range(8):
            if cch + 1 < 8:
                cpan[cch + 1] = conv_panels(cch + 1)
            (wcb, kcb), (wcc, kcc), (wcx, kcx) = cpan.pop(cch)
            uc = uconv[cch % 2]
            ucc = f"uconv{cch % 2}"
            for tg in range(4):
                bset = (step % 2) * 3
                step += 1
                bA, bB, bC = bset, bset + 1, bset + 2
                for (bk, wp, wk) in ((bA, wcx, kcx), (bB, wcc, kcc), (bC, wcb, kcb)):
                    for c in range(8):
                        P.op("pe", lambda e, bk=bk, wp=wp, c=c, tg=tg: e.matmul(bank(bk), lhsT=wp[:, c, :],
                                                                            rhs=nT[:, c, tg * 512:(tg + 1) * 512],
                                                                            start=(c == 0), stop=(c == 7)),
                             r=[wk, f"nT{tg}"], w=[f"ps{bk}"])
                s = tg % 2
                o0 = tg * 512
                P.op("act", lambda e, s=s, bA=bA: e.activation(tmp1[s], bank(bA), AF.Copy), r=[f"ps{bA}"], w=[f"tmp1{s}"])
                P.op("dve", lambda e, s=s, bB=bB, uc=uc, o0=o0: e.tensor_tensor(uc[:, 2 + o0:2 + o0 + 512], bank(bB), tmp1[s], ALU.mult),
                     r=[f"ps{bB}", f"tmp1{s}"], w=[ucc])
                P.op("dve", lambda e, s=s, uc=uc, o0=o0, cch=cch: e.tensor_scalar(zt[s], uc[:, 2 + o0:2 + o0 + 512], cw[:, cch, 2:3], None, ALU.mult),
                     r=[ucc, "cw"], w=[f"zt{s}"])
                P.op("dve", lambda e, s=s, uc=uc, o0=o0, cch=cch: e.scalar_tensor_tensor(out=zt[s], in0=uc[:, 1 + o0:1 + o0 + 512], scalar=cw[:, cch, 1:2],
                                                                                       in1=zt[s], op0=ALU.mult, op1=ALU.add),
                     r=[ucc, "cw", f"zt{s}"], w=[f"zt{s}"])
                P.op("dve", lambda e, s=s, uc=uc, o0=o0, cch=cch: e.scalar_tensor_tensor(out=zt[s], in0=uc[:, o0:o0 + 512], scalar=cw[:, cch, 0:1],
                                                                                       in1=zt[s], op0=ALU.mult, op1=ALU.add),
                     r=[ucc, "cw", f"zt{s}"], w=[f"zt{s}"])
                P.op("dve", lambda e, s=s, bC=bC, cch=cch, o0=o0: e.tensor_tensor(ycT[:, cch, o0:o0 + 512], bank(bC), zt[s], ALU.mult),
                     r=[f"ps{bC}", f"zt{s}"], w=[f"ycT{tg}"])
        P.barrier()
        A.off = mS

        qT = A.alloc(4096, BF16)
        kT = A.alloc(4096, BF16)
        vsb = A.alloc(16 * 130 * 2, BF16).rearrange("p (t e) -> p t e", e=130)
        NPT = 6
        pt = [A.alloc(1024, BF16) for _ in range(NPT)]
        of1 = [A.alloc(512, F32) for _ in range(4)]
        of2 = [A.alloc(512, F32) for _ in range(4)]
        onb = [A.alloc(256, BF16) for _ in range(4)]
        ub8.extend(A.alloc(2048, BF16) for _ in range(8))
        if stop_after != "M":
            for b_ in range(4):
                p_load(b_)
        P.op("dve", lambda e: e.memset(vsb[:, :, 128:129], 1.0), w=["vsb1"])
        pt_ctr = 0
        sb_ctr = 0
        fin_ctr = 0

        def oacc(a):
            bk = 4 + a // 2
            o = (a % 2) * 130
            return bank(bk)[:, o:o + 129], f"ps{bk}"

        def head_panels(h):
            return (load_panel(w_in_v[:, :, h * 128:(h + 1) * 128]),
                    load_panel(w_in_v[:, :, 1024 + h * 128:1024 + (h + 1) * 128]),
                    load_panel(w_in_v[:, :, 2048 + h * 128:2048 + (h + 1) * 128]))

        hpan = {0: head_panels(0)}
        for h in range(8):
            if h + 1 < 8:
                hpan[h + 1] = head_panels(h + 1)
            for _ in range(5):
                if conv_jobs:
                    conv_jobs.pop(0)()
            (wq, kq), (wk, kk), (wv, kv) = hpan.pop(h)
            for (dst, dname, wp, wkey) in ((qT, "qT", wq, kq), (kT, "kT", wk, kk)):
                for tg in range(4):
                    bk = sb_ctr % 4
                    sb_ctr += 1
                    for c in range(8):
                        P.op("pe", lambda e, bk=bk, wp=wp, c=c, tg=tg: e.matmul(bank(bk), lhsT=wp[:, c, :],
                                                                            rhs=nT[:, c, tg * 512:(tg + 1) * 512],
                                                                            start=(c == 0), stop=(c == 7)),
                             r=[wkey, f"nT{tg}"], w=[f"ps{bk}"])
                    P.op("act", lambda e, bk=bk, dst=dst, tg=tg: e.activation(dst[:, tg * 512:(tg + 1) * 512], bank(bk), AF.Copy),
                         r=[f"ps{bk}"], w=[f"{dname}{tg}"])
            for t4 in range(4):
                bk = sb_ctr % 4
                sb_ctr += 1
                for tq in range(4):
                    tt = t4 * 4 + tq
                    for c in range(8):
                        P.op("pe", lambda e, bk=bk, tq=tq, tt=tt, c=c, wv=wv: e.matmul(bank(bk)[:, tq * 128:(tq + 1) * 128],
                                                                                   lhsT=nT[:, c, tt * 128:(tt + 1) * 128], rhs=wv[:, c, :],
                                                                                   start=(c == 0), stop=(c == 7)),
                             r=[kv, f"nT{t4}"], w=[f"ps{bk}"])
                P.op("dve", lambda e, bk=bk, t4=t4: e.tensor_copy(vsb[:, t4 * 4:(t4 + 1) * 4, 0:128],
                                                                bank(bk).rearrange("p (t e) -> p t e", e=128)),
                     r=[f"ps{bk}"], w=["vsb"])
            for qg in range(4):
                steps = [(j, m) for j in range(4 * qg + 4) for m in range(2)]

                def emit_S(j, m, qg=qg):
                    nonlocal sb_ctr, pt_ctr
                    col0 = max(qg * 512, j * 128)
                    ncols = (qg + 1) * 512 - col0
                    diag = j >= 4 * qg
                    bk = sb_ctr % 4
                    sb_ctr += 1
                    sl = pt_ctr % NPT
                    pt_ctr += 1
                    P.op("pe", lambda e, bk=bk, m=m, j=j, col0=col0, ncols=ncols: e.matmul(
                        bank(bk)[:, 0:ncols], lhsT=kT[64 * m:64 * m + 64, j * 128:(j + 1) * 128],
                        rhs=qT[64 * m:64 * m + 64, col0:col0 + ncols], start=True, stop=True),
                         r=[f"kT{j // 4}", f"qT{qg}"], w=[f"ps{bk}"])
                    P.op("act", lambda e, bk=bk, sl=sl, ncols=ncols: e.activation(pt[sl][:, 0:ncols], bank(bk)[:, 0:ncols], AF.Exp, scale=0.125),
                         r=[f"ps{bk}"], w=[f"pt{sl}"])
                    if diag:
                        P.op("dve", lambda e, sl=sl: e.tensor_tensor(pt[sl][:, 0:128], pt[sl][:, 0:128], mask_tri, ALU.mult),
                             r=[f"pt{sl}"], w=[f"pt{sl}"])
                    return (sl, col0)

                def emit_PV(j, m, st, qg=qg):
                    sl, col0 = st
                    for i in range(max(4 * qg, j), 4 * qg + 4):
                        off = i * 128 - col0
                        aidx = m * 4 + (i - 4 * qg)
                        oa, oc = oacc(aidx)
                        P.op("pe", lambda e, oa=oa, sl=sl, off=off, j=j, i=i, aidx=aidx: e.matmul(
                            oa, lhsT=pt[sl][:, off:off + 128], rhs=vsb[:, j, 0:129],
                            start=(j == 0 and aidx % 2 == 0), stop=(j == i), skip_group_check=True),
                             r=[f"pt{sl}", "vsb", "vsb1"], w=[oc])

                DEPTH = 2
                sts = {}
                for k in range(min(DEPTH, len(steps))):
                    sts[k] = emit_S(*steps[k])
                for k in range(len(steps)):
                    if k + DEPTH < len(steps):
                        sts[k + DEPTH] = emit_S(*steps[k + DEPTH])
                    emit_PV(steps[k][0], steps[k][1], sts[k])
                fin = []
                for il in range(4):
                    o0a, c0 = oacc(il)
                    o1a, c1 = oacc(4 + il)
                    r0, cr0 = sm()
                    r1, cr1 = sm()
                    r1n, cr1n = sm()
                    P.op("dve", lambda e, r0=r0, o0a=o0a: e.reciprocal(r0, o0a[:, 128:129]), r=[c0], w=[cr0])
                    P.op("dve", lambda e, r1=r1, o1a=o1a: e.reciprocal(r1, o1a[:, 128:129]), r=[c1], w=[cr1])
                    fin.append((o0a, c0, o1a, c1, r0, cr0, r1, cr1, r1n, cr1n))
                for il in range(4):
                    (o0a, c0, o1a, c1, r0, cr0, r1, cr1, r1n, cr1n) = fin[il]
                    P.op("dve", lambda e, r1=r1, r1n=r1n: e.tensor_tensor(r1n, r1, neglam, ALU.mult), r=[cr1, "neglam"], w=[cr1n])
                    P.op("dve", lambda e, il=il, o0a=o0a, r0=r0: e.tensor_scalar(of1[il], o0a[:, 0:128], r0, None, ALU.mult),
                         r=[c0, cr0], w=[f"of1{il}"])
                for il in range(4):
                    (o0a, c0, o1a, c1, r0, cr0, r1, cr1, r1n, cr1n) = fin[il]
                    P.op("dve", lambda e, il=il, o1a=o1a, r1n=r1n: e.scalar_tensor_tensor(out=of2[il], in0=o1a[:, 0:128], scalar=r1n, in1=of1[il],
                                                                                       op0=ALU.mult, op1=ALU.add),
                         r=[c1, cr1n, f"of1{il}"], w=[f"of2{il}"])
                sq = []
                for il in range(4):
                    ssq, cq = sm()
                    P.op("dve", lambda e, il=il, ssq=ssq: e.tensor_tensor_reduce(out=junk_f, in0=of2[il], in1=of2[il], scale=1.0, scalar=0.0,
                                                                                 op0=ALU.mult, op1=ALU.add, accum_out=ssq),
                         r=[f"of2{il}"], w=[cq])
                    sq.append((ssq, cq))
                rss = [rstd_from(ssq, cq, 128) for (ssq, cq) in sq]
                for il in range(4):
                    rs, cr = rss[il]
                    P.op("dve", lambda e, il=il, rs=rs: e.scalar_tensor_tensor(out=onb[il], in0=of2[il], scalar=rs, in1=subgb,
                                                                             op0=ALU.mult, op1=ALU.mult),
                         r=[f"of2{il}", cr, "subgb"], w=[f"onb{il}"])
                for il in range(4):
                    i = 4 * qg + il
                    bk = sb_ctr % 4
                    sb_ctr += 1
                    P.op("pe", lambda e, bk=bk, il=il: e.transpose(bank_bf(bk)[:, 0:128], onb[il], ident_bf), r=[f"onb{il}"], w=[f"ps{bk}"])
                    P.op("act", lambda e, bk=bk, h=h, i=i: e.activation(attnT[:, h, i * 128:(i + 1) * 128], bank_bf(bk)[:, 0:128], AF.Copy),
                         r=[f"ps{bk}"], w=[f"attnT{i // 4}"])
                if stop_after != "M":
                    for b_ in range(p_next[0] + 4, p_next[0] + 8):
                        if b_ < NB:
                            p_load(b_)
                    for _ in range(4):
                        if p_next[0] < NB:
                            p_compute(p_next[0])
                            p_next[0] += 1
        P.barrier()
        A.off = mS

        mergedT = A.alloc(32768, BF16).rearrange("p (c t) -> p c t", t=S)
        sga = [A.alloc(2048, F32) for _ in range(2)]
        sgc = [A.alloc(2048, F32) for _ in range(2)]
        t1 = [A.alloc(2048, F32) for _ in range(2)]
        t2 = [A.alloc(2048, F32) for _ in range(2)]
        wao_v = w_attn_o.rearrange("(c p) n -> p c n", p=128)
        wco_v = w_conv_o.rearrange("(c p) n -> p c n", p=128)
        step = 0
        def merge_panels(cch):
            cs = slice(cch * 128, (cch + 1) * 128)
            return (load_panel(wao_v[:, :, cs]), load_panel(wco_v[:, :, cs]),
                    load_panel(w_in_v[:, :, 6144 + cch * 128:6144 + (cch + 1) * 128]),
                    load_panel(w_in_v[:, :, 7168 + cch * 128:7168 + (cch + 1) * 128]))

        mpan = {0: merge_panels(0)}
        for cch in range(8):
            if cch + 1 < 8:
                mpan[cch + 1] = merge_panels(cch + 1)
            (wao, kao), (wco, kco), (wga, kga), (wgc, kgc) = mpan.pop(cch)
            for tg in range(4):
                bset = (step % 2) * 4
                s = step % 2
                step += 1
                bA, bB, bC, bD = bset, bset + 1, bset + 2, bset + 3
                ts = slice(tg * 512, (tg + 1) * 512)
                for (bk, wp, wk_, src, sname) in ((bA, wao, kao, attnT, "attnT"), (bB, wco, kco, ycT, "ycT"),
                                                 (bC, wga, kga, nT, "nT"), (bD, wgc, kgc, nT, "nT")):
                    for c in range(8):
                        P.op("pe", lambda e, bk=bk, wp=wp, c=c, src=src, ts=ts: e.matmul(bank(bk), lhsT=wp[:, c, :], rhs=src[:, c, ts],
                                                                                     start=(c == 0), stop=(c == 7)),
                             r=[wk_, f"{sname}{tg}"], w=[f"ps{bk}"])
                P.op("act", lambda e, s=s, bC=bC: e.activation(sga[s], bank(bC), AF.Sigmoid), r=[f"ps{bC}"], w=[f"sga{s}"])
                P.op("act", lambda e, s=s, bD=bD: e.activation(sgc[s], bank(bD), AF.Sigmoid), r=[f"ps{bD}"], w=[f"sgc{s}"])
                P.op("dve", lambda e, s=s, bA=bA: e.tensor_tensor(t1[s], bank(bA), sga[s], ALU.mult), r=[f"ps{bA}", f"sga{s}"], w=[f"t1{s}"])
                P.op("dve", lambda e, s=s, bB=bB: e.tensor_tensor(t2[s], bank(bB), sgc[s], ALU.mult), r=[f"ps{bB}", f"sgc{s}"], w=[f"t2{s}"])
                P.op("dve", lambda e, s=s, cch=cch, ts=ts: e.tensor_tensor(mergedT[:, cch, ts], t1[s], t2[s], ALU.add),
                     r=[f"t1{s}", f"t2{s}"], w=[f"mg{tg}"])
        P.barrier()
        if debug:
            P.op("sp", lambda e: e.dma_start(out=dbg_nT, in_=nT.rearrange("p c t -> p (c t)")), w=["dbg1"], key="dbg1")
            P.op("sp", lambda e: e.dma_start(out=dbg_ycT, in_=ycT.rearrange("p c t -> p (c t)")), w=["dbg2"], key="dbg2")
            P.op("sp", lambda e: e.dma_start(out=dbg_attnT, in_=attnT.rearrange("p c t -> p (c t)")), w=["dbg3"], key="dbg3")
            P.op("sp", lambda e: e.dma_start(out=dbg_mg, in_=mergedT.rearrange("p c t -> p (c t)")), w=["dbg4"], key="dbg4")
            P.op("sp", lambda e: e.dma_start(out=dbg_small, in_=small), w=["dbg5"], key="dbg5")
            P.barrier()
        A.off = mP
        wout = A.alloc(16384, BF16).rearrange("p (c n) -> p c n", n=1024)
        xt2 = [A.alloc(4096, F32) for _ in range(2)]
        ht = [A.alloc(4096, F32) for _ in range(2)]
        assert A.off <= mP + 65536
        for c in range(8):
            P.op("pool", lambda e, c=c: e.dma_start(out=wout[:, c, :], in_=w_out[c * 128:(c + 1) * 128, :]), w=["wout"], key="wout")
        for tt in range(NT):
            s = tt % 2
            P.op("sp", lambda e, tt=tt, s=s: e.dma_start(out=xt2[s], in_=x[tt * 128:(tt + 1) * 128, :]), w=[f"xt2{s}"], key=f"xt2{s}")
            for half in range(2):
                bk = (tt * 2 + half) % 8
                hs_ = slice(half * 512, (half + 1) * 512)
                for c in range(8):
                    P.op("pe", lambda e, bk=bk, c=c, tt=tt, hs_=hs_: e.matmul(bank(bk), lhsT=mergedT[:, c, tt * 128:(tt + 1) * 128],
                                                                          rhs=wout[:, c, hs_], start=(c == 0), stop=(c == 7)),
                         r=["wout", f"mg{tt // 4}"], w=[f"ps{bk}"])
                P.op("dve", lambda e, bk=bk, s=s, hs_=hs_: e.tensor_tensor(ht[s][:, hs_], bank(bk), xt2[s][:, hs_], ALU.add),
                     r=[f"ps{bk}", f"xt2{s}"], w=[f"ht{s}"])
            P.op("sp", lambda e, tt=tt, s=s: e.dma_start(out=h_scr[tt * 128:(tt + 1) * 128, :], in_=ht[s]), r=[f"ht{s}"], w=["h_scr"], key=f"ht{s}")
        P.barrier()
        A.off = mP

        if stop_after != "M":
            HB = NB // 2
            GTh = [A.alloc(HB * TGE * 2, BF16).rearrange("p (i t) -> p i t", t=TGE) for _ in range(2)]
            xn2T = [A.alloc(8 * TGE * 2, BF16).rearrange("p (c t) -> p c t", t=TGE) for _ in range(2)]
            hsb = [[A.alloc(4096, F32) for _ in range(2)] for _ in range(2)]
            IJGT = [A.alloc(3 * TGE * 4, F32).rearrange("p (q t) -> p q t", t=TGE) for _ in range(2)]
            IJb = [A.alloc(2 * TGE * 2, BF16).rearrange("p (q t) -> p q t", t=TGE) for _ in range(2)]
            qTa = A.alloc(16 * TGE * 2, BF16).rearrange("p (g t) -> p g t", t=TGE)
            S2 = A.alloc(8192, F32).rearrange("p (g n) -> p g n", n=128)
            cand2 = S2.rearrange("p g n -> p (g n)").rearrange("p (h c) -> p h c", c=256)
            Eo = cand2.rearrange("p h (k a) -> p h k a", a=16)
            xnb = [A.alloc(2048, BF16) for _ in range(2)]
            tv = A.alloc(1024, F32).rearrange("p (g k) -> p g k", k=16)
            ti = A.alloc(1024, U32).rearrange("p (g k) -> p g k", k=16)
            tif = A.alloc(1024, F32).rearrange("p (g k) -> p g k", k=16)
            cand = A.alloc(8192, F32).rearrange("p (h c) -> p h c", c=256)
            cv = A.alloc(512, F32).rearrange("p (h k) -> p h k", k=16)
            ci = A.alloc(512, U32).rearrange("p (h k) -> p h k", k=16)
            cia = A.alloc(512, U32).rearrange("p (h k) -> p h k", k=16)
            cib = A.alloc(512, U32).rearrange("p (h k) -> p h k", k=16)
            af_ = A.alloc(512, F32).rearrange("p (h k) -> p h k", k=16)
            bf_ = A.alloc(512, F32).rearrange("p (h k) -> p h k", k=16)
            IJG = A.alloc(3 * 512, F32).rearrange("p (q h k) -> p q h k", q=3, k=16)
            dg = A.alloc(512, F32).rearrange("p (h k) -> p h k", k=16)
            eg = A.alloc(512, F32).rearrange("p (h k) -> p h k", k=16)
            zs = A.alloc(32, F32)
            rz = A.alloc(32, F32)
            CH = 8
            NOH = 3
            oh1 = [A.alloc(CH * 64 * 2, BF16).rearrange("p (t i) -> p t i", i=64) for _ in range(NOH)]
            oh2 = [A.alloc(CH * 128 * 2, BF16).rearrange("p (t i) -> p t i", i=128) for _ in range(NOH)]
            oh2g = [A.alloc(CH * 128 * 2, BF16).rearrange("p (t i) -> p t i", i=128) for _ in range(NOH)]
            NU = 6
            uTb = [A.alloc(2048, BF16).rearrange("p (c e) -> p c e", e=128) for _ in range(NU)]
            vb = [A.alloc(2048, BF16) for _ in range(NU)]
            geb = [A.alloc(TGE * 2, BF16) for _ in range(3)]
            hab = [A.alloc(TGE * 2, BF16) for _ in range(3)]
            NWQ = 2
            wqr = [A.alloc(2048, BF16).rearrange("p (c n) -> p c n", n=128) for _ in range(NWQ)]
            wq_v = wq_scr.rearrange("(c p) n -> p c n", p=128)
            iota16 = iota_f[:, 0:16]
            st = {"wq": 0, "oh": 0, "blk": 0, "pp": 0}

            class RPool:
                def __init__(self, items):
                    self.free = list(items)

                def acquire(self):
                    return self.free.pop(0) if self.free else None

                def release(self, x):
                    self.free.append(x)

            bank_pool = RPool([6, 7])
            oh_pool = RPool(list(range(NOH)))
            wq_pool = RPool(list(range(NWQ)))

            def prep_topk(g):
                gb = g % 2
                th = []

                def e1(_unused):
                    ssq2, ln2, rs2, pc = smp_slot()
                    hss = [hsb[gb][tl] for tl in range(2)]
                    hcs = [f"hsb{gb}{tl}" for tl in range(2)]
                    for tl in range(2):
                        tt = g * 2 + tl
                        P.op("sp", lambda e, tl=tl, tt=tt: e.dma_start(out=hss[tl], in_=h_scr[tt * 128:(tt + 1) * 128, :]),
                             r=["h_scr"], w=[hcs[tl]], key=hcs[tl])
                    yield
                    yield
                    for tl in range(2):
                        P.op("dve", lambda e, tl=tl: e.tensor_tensor_reduce(out=junk, in0=hss[tl], in1=hss[tl], scale=1.0, scalar=0.0,
                                                                            op0=ALU.mult, op1=ALU.add, accum_out=ssq2[:, tl:tl + 1]),
                             r=[hcs[tl]], w=[pc + f"s{tl}"])
                    yield
                    P.op("act", lambda e: e.activation(ln2, ssq2, AF.Ln, bias=EPS, scale=1.0 / D), r=[pc + "s0", pc + "s1"], w=[pc + "l"])
                    P.op("act", lambda e: e.activation(rs2, ln2, AF.Exp, scale=-0.5), r=[pc + "l"], w=[pc + "r"])
                    yield
                    yield
                    for tl in range(2):
                        P.op("dve", lambda e, tl=tl: e.scalar_tensor_tensor(out=xnb[tl], in0=hss[tl], scalar=rs2[:, tl:tl + 1], in1=g2b,
                                                                            op0=ALU.mult, op1=ALU.mult),
                             r=[hcs[tl], pc + "r", "g2b"], w=[f"xnb{tl}"])
                    yield
                    for tl in range(2):
                        while True:
                            pb = bank_pool.acquire()
                            if pb is not None:
                                break
                            yield
                        for c in range(8):
                            P.op("pe", lambda e, c=c, tl=tl, pb=pb: e.transpose(bank_bf(pb)[:, c * 128:(c + 1) * 128],
                                                                             xnb[tl][:, c * 128:(c + 1) * 128], ident_bf),
                                 r=[f"xnb{tl}"], w=[f"ps{pb}"])
                        yield
                        P.op("act", lambda e, tl=tl, pb=pb: e.activation(xn2T[gb][:, :, tl * 128:(tl + 1) * 128],
                                                                        bank_bf(pb).rearrange("p (c t) -> p c t", t=128), AF.Copy),
                             r=[f"ps{pb}"], w=[f"xn2T{gb}"])
                        bank_pool.release(pb)

                def e2(grp):
                    while True:
                        sl = wq_pool.acquire()
                        if sl is not None:
                            break
                        yield
                    P.op("sp", lambda e: e.dma_start(out=wqr[sl], in_=wq_v[:, :, grp * 128:(grp + 1) * 128]), w=[f"wq{sl}"], key=f"wq{sl}")
                    yield
                    yield
                    yield
                    while True:
                        bk = bank_pool.acquire()
                        if bk is not None:
                            break
                        yield
                    for c in range(8):
                        P.op("pe", lambda e, c=c: e.matmul(bank(bk)[:, 0:TGE], lhsT=wqr[sl][:, c, :], rhs=xn2T[gb][:, c, :],
                                                           start=(c == 0), stop=(c == 7)),
                             r=[f"wq{sl}", f"xn2T{gb}"], w=[f"ps{bk}"])
                    wq_pool.release(sl)
                    yield
                    P.op("act", lambda e: e.activation(qTa[:, grp, :], bank(bk)[:, 0:TGE], AF.Copy), r=[f"ps{bk}"], w=[f"qTa{grp}"])
                    bank_pool.release(bk)

                def e3(tl, quad):
                    while True:
                        bk = bank_pool.acquire()
                        if bk is not None:
                            break
                        yield
                    grps = [quad * 4 + q for q in range(4)]
                    scs = {grp: bank(bk)[:, (grp % 4) * 128:(grp % 4 + 1) * 128] for grp in grps}
                    for grp in grps:
                        P.op("pe", lambda e, grp=grp: e.matmul(scs[grp], lhsT=qTa[:, grp, tl * 128:(tl + 1) * 128], rhs=skT[:, grp, :],
                                                               start=True, stop=True),
                             r=[f"qTa{grp}", "skT"], w=[f"ps{bk}"])
                    yield
                    for grp in grps:
                        P.op("dve", lambda e, grp=grp: e.max(out=tv[:, grp, 0:8], in_=scs[grp]), r=[f"ps{bk}"], w=[f"tva{grp}"])
                    for grp in grps:
                        P.op("dve", lambda e, grp=grp: e.max_index(out=ti[:, grp, 0:8], in_max=tv[:, grp, 0:8], in_values=scs[grp]),
                             r=[f"ps{bk}", f"tva{grp}"], w=[f"tia{grp}"])
                    for grp in grps:
                        P.op("dve", lambda e, grp=grp: e.match_replace(out=S2[:, grp, :], in_to_replace=tv[:, grp, 0:8], in_values=scs[grp],
                                                                     imm_value=-1e30),
                             r=[f"ps{bk}", f"tva{grp}"], w=[f"S2{grp}"])
                    for grp in grps:
                        P.op("dve", lambda e, grp=grp: e.max(out=tv[:, grp, 8:16], in_=S2[:, grp, :]), r=[f"S2{grp}"], w=[f"tvb{grp}"])
                    for grp in grps:
                        P.op("dve", lambda e, grp=grp: e.max_index(out=ti[:, grp, 8:16], in_max=tv[:, grp, 8:16], in_values=S2[:, grp, :]),
                             r=[f"S2{grp}", f"tvb{grp}"], w=[f"tib{grp}"])
                    bank_pool.release(bk)

                TVC = [f"tva{g_}" for g_ in range(16)] + [f"tvb{g_}" for g_ in range(16)]
                TIC = [f"tia{g_}" for g_ in range(16)] + [f"tib{g_}" for g_ in range(16)]
                S2C = [f"S2{g_}" for g_ in range(16)]
                CVC = [f"cva{h_}" for h_ in range(8)] + [f"cvb{h_}" for h_ in range(8)]
                CIC = [f"cia{h_}" for h_ in range(8)] + [f"cib{h_}" for h_ in range(8)]
                C2C = [f"cand2_{h_}" for h_ in range(8)]
                tvv = tv.rearrange("p (h m) k -> p h m k", m=2)
                tifv = tif.rearrange("p (h m) k -> p h m k", m=2)
                candv = cand.rearrange("p h (a b) -> p h a b", b=16)

                def e4a(tl):
                    yield
                    P.op("dve", lambda e: e.tensor_copy(tif, ti), r=TIC, w=["tif"])
                    P.op("dve", lambda e: e.tensor_tensor(candv, tvv[:, :, 0, :].unsqueeze(3).broadcast_to([128, 8, 16, 16]),
                                                          tvv[:, :, 1, :].unsqueeze(2).broadcast_to([128, 8, 16, 16]), ALU.add),
                         r=TVC, w=[f"cand{h_}" for h_ in range(8)])
                    for h in range(8):
                        P.op("dve", lambda e, h=h: e.max(out=cv[:, h, 0:8], in_=cand[:, h, :]), r=[f"cand{h}"], w=[f"cva{h}"])
                    for h in range(8):
                        P.op("dve", lambda e, h=h: e.max_index(out=ci[:, h, 0:8], in_max=cv[:, h, 0:8], in_values=cand[:, h, :]),
                             r=[f"cand{h}", f"cva{h}"], w=[f"cia{h}"])
                    for h in range(8):
                        P.op("dve", lambda e, h=h: e.match_replace(out=cand2[:, h, :], in_to_replace=cv[:, h, 0:8], in_values=cand[:, h, :],
                                                                 imm_value=-1e30), r=[f"cand{h}", f"cva{h}"], w=[f"cand2_{h}"] + S2C[2 * h:2 * h + 2])

                def e4b(tl):
                    yield
                    for h in range(8):
                        P.op("dve", lambda e, h=h: e.max(out=cv[:, h, 8:16], in_=cand2[:, h, :]), r=[f"cand2_{h}"], w=[f"cvb{h}"])
                    for h in range(8):
                        P.op("dve", lambda e, h=h: e.max_index(out=ci[:, h, 8:16], in_max=cv[:, h, 8:16], in_values=cand2[:, h, :]),
                             r=[f"cand2_{h}", f"cvb{h}"], w=[f"cib{h}"])
                    P.op("dve", lambda e: e.tensor_single_scalar(cia, ci, 4, ALU.logical_shift_right), r=CIC, w=["cia"])
                    P.op("dve", lambda e: e.tensor_single_scalar(cib, ci, 15, ALU.bitwise_and), r=CIC, w=["cib"])
                    P.op("dve", lambda e: e.tensor_copy(af_, cia), r=["cia"], w=["af"])
                    P.op("dve", lambda e: e.tensor_copy(bf_, cib), r=["cib"], w=["bf"])

                def e4c(tl):
                    io4 = iota16.unsqueeze(1).unsqueeze(1).broadcast_to([128, 8, 16, 16])
                    for (q, src, mm) in ((0, af_, 0), (1, bf_, 1)):
                        P.op("dve", lambda e, src=src: e.tensor_tensor(Eo, src.unsqueeze(3).broadcast_to([128, 8, 16, 16]), io4, ALU.is_equal),
                             r=["af", "bf", "iota_f"], w=C2C + S2C)
                        P.op("dve", lambda e, mm=mm: e.tensor_tensor(Eo, Eo, tifv[:, :, mm, :].unsqueeze(2).broadcast_to([128, 8, 16, 16]), ALU.mult),
                             r=C2C + ["tif"], w=C2C + S2C)
                        P.op("dve", lambda e, q=q: e.tensor_reduce(out=IJG[:, q, :, :], in_=Eo, axis=AX.X, op=ALU.add), r=C2C + S2C, w=["IJG"])
                    P.op("dve", lambda e: e.tensor_tensor(dg, cv, cv[:, :, 0:1].broadcast_to([128, 8, 16]), ALU.subtract), r=CVC, w=["dg"])
                    yield
                    P.op("act", lambda e: e.activation(eg, dg, AF.Exp), r=["dg"], w=["eg"])
                    yield
                    P.op("dve", lambda e: e.tensor_reduce(out=zs, in_=eg, axis=AX.X, op=ALU.add), r=["eg"], w=["zs"])
                    P.op("dve", lambda e: e.reciprocal(rz, zs), r=["zs"], w=["rz"])
                    P.op("dve", lambda e: e.tensor_tensor(IJG[:, 2, :, :], eg, rz.unsqueeze(2).broadcast_to([128, 8, 16]), ALU.mult),
                         r=["eg", "rz", "IJG"], w=["IJG"])
                    yield
                    while True:
                        pb = bank_pool.acquire()
                        if pb is not None:
                            break
                        yield
                    for q in range(3):
                        P.op("pe", lambda e, q=q: e.transpose(bank(pb)[:, q * 128:(q + 1) * 128], IJG[:, q, :, :].rearrange("p h k -> p (h k)"), ident_f),
                             r=["IJG", "ident_f"], w=[f"ps{pb}"])
                    yield
                    P.op("act", lambda e: e.activation(IJGT[gb][:, :, tl * 128:(tl + 1) * 128],
                                                       bank(pb)[:, 0:384].rearrange("p (q t) -> p q t", t=128), AF.Copy),
                         r=[f"ps{pb}"], w=[f"IJGT{gb}"])
                    P.op("act", lambda e: e.activation(IJb[gb][:, :, tl * 128:(tl + 1) * 128],
                                                       bank(pb)[:, 0:256].rearrange("p (q t) -> p q t", t=128), AF.Copy),
                         r=[f"ps{pb}"], w=[f"IJb{gb}"])
                    bank_pool.release(pb)

                th.append(lambda: e1(0))
                th.append("FENCE")
                for grp in range(16):
                    th.append(lambda grp=grp: e2(grp))
                th.append("FENCE")
                for tl in range(2):
                    for quad in range(4):
                        th.append(lambda tl=tl, quad=quad: e3(tl, quad))
                    th.append("FENCE")
                    th.append(lambda tl=tl: e4a(tl))
                    th.append("FENCE")
                    th.append(lambda tl=tl: e4b(tl))
                    th.append("FENCE")
                    th.append(lambda tl=tl: e4c(tl))
                    th.append("FENCE")
                return th

            def prep_G(g, half):
                gb = g % 2
                th = []

                def chunk(ch):
                    while True:
                        s = oh_pool.acquire()
                        if s is not None:
                            break
                        yield
                    c0 = ch * CH
                    iob = iota_b.unsqueeze(1).broadcast_to([128, CH, 128])
                    iobh = iota_b[:, 64 * half:64 * half + 64].unsqueeze(1).broadcast_to([128, CH, 64])
                    P.op("dve", lambda e: e.tensor_tensor(oh1[s], iobh, IJb[gb][:, 0, c0:c0 + CH].unsqueeze(2).broadcast_to([128, CH, 64]), ALU.is_equal),
                         r=[f"IJb{gb}", "iota_b"], w=[f"oh1{s}"])
                    P.op("dve", lambda e: e.tensor_tensor(oh2[s], iob, IJb[gb][:, 1, c0:c0 + CH].unsqueeze(2).broadcast_to([128, CH, 128]), ALU.is_equal),
                         r=[f"IJb{gb}", "iota_b"], w=[f"oh2{s}"])
                    P.op("pool", lambda e: e.tensor_tensor(oh2g[s], oh2[s], IJGT[gb][:, 2, c0:c0 + CH].unsqueeze(2).broadcast_to([128, CH, 128]), ALU.mult),
                         r=[f"IJGT{gb}", f"oh2{s}"], w=[f"oh2g{s}"])
                    yield
                    yield
                    while True:
                        bk = bank_pool.acquire()
                        if bk is not None:
                            break
                        yield
                    for t in range(CH):
                        P.op("pe", lambda e, t=t: e.matmul(bank(bk)[:, t * 64:(t + 1) * 64], lhsT=oh2g[s][:, t, :], rhs=oh1[s][:, t, :],
                                                           start=True, stop=True),
                             r=[f"oh1{s}", f"oh2g{s}"], w=[f"ps{bk}"])
                    oh_pool.release(s)
                    yield
                    P.op("act", lambda e: e.activation(GTh[half][:, :, c0:c0 + CH].rearrange("p i t -> p t i"),
                                                       bank(bk).rearrange("p (t i) -> p t i", i=64), AF.Copy),
                         r=[f"ps{bk}"], w=[f"GT{half}"])
                    bank_pool.release(bk)

                for ch in range(TGE // CH):
                    th.append(lambda ch=ch: chunk(ch))
                return th

            def emit_U(g, i):
                gb = g % 2
                k = st["blk"]
                st["blk"] += 1
                s = k % NU
                pa = 4 + k % 2
                s2 = k % 3
                half = i // HB
                P.op("sp", lambda e: e.dma_start(out=uTb[s], in_=uT_scr[i].rearrange("p (c e) -> p c e", e=128)), w=[f"uTb{s}"], key=f"uTb{s}")
                P.op("sp", lambda e: e.dma_start(out=vb[s], in_=v_scr[i * 128:(i + 1) * 128, :]), w=[f"vb{s}"], key=f"vb{s}")
                for c in range(8):
                    P.op("pe", lambda e, c=c: e.matmul(bank(pa)[:, 0:TGE], lhsT=uTb[s][:, c, :], rhs=xn2T[gb][:, c, :],
                                                       start=(c == 0), stop=(c == 7)),
                         r=[f"uTb{s}", f"xn2T{gb}"], w=[f"ps{pa}"])
                P.op("act", lambda e: e.activation(geb[s2], bank(pa)[:, 0:TGE], AF.Gelu), r=[f"ps{pa}"], w=[f"geb{s2}"])
                P.op("dve", lambda e: e.tensor_tensor(hab[s2], geb[s2], GTh[half][:, i % HB, :], ALU.mult),
                     r=[f"geb{s2}", f"GT{half}"], w=[f"hab{s2}"])
                return (s, s2)

            def emit_V(i, ss):
                s, s2 = ss
                for tl in range(2):
                    for hf_ in range(2):
                        bk = tl * 2 + hf_
                        P.op("pe", lambda e, tl=tl, hf_=hf_, bk=bk: e.matmul(
                            bank(bk), lhsT=hab[s2][:, tl * 128:(tl + 1) * 128], rhs=vb[s][:, hf_ * 512:(hf_ + 1) * 512],
                            start=(i == 0), stop=(i == NB - 1)),
                             r=[f"hab{s2}", f"vb{s}"], w=[f"ps{bk}"])

            def group_end(g):
                gb = g % 2
                ssq2, ln2, rs2, pc = smp_slot()
                hss = [hsb[gb][tl] for tl in range(2)]
                hcs = [f"hsb{gb}{tl}" for tl in range(2)]
                for tl in range(2):
                    for hf_ in range(2):
                        bk = tl * 2 + hf_
                        hs_ = slice(hf_ * 512, (hf_ + 1) * 512)
                        P.op("dve", lambda e, bk=bk, hs_=hs_, tl=tl: e.tensor_tensor(hss[tl][:, hs_], bank(bk), hss[tl][:, hs_], ALU.add),
                             r=[f"ps{bk}", hcs[tl]], w=[hcs[tl]])
                    P.op("dve", lambda e, tl=tl: e.tensor_tensor_reduce(out=junk, in0=hss[tl], in1=hss[tl], scale=1.0, scalar=0.0,
                                                                        op0=ALU.mult, op1=ALU.add, accum_out=ssq2[:, tl:tl + 1]),
                         r=[hcs[tl]], w=[pc + f"s{tl}"])
                yield
                P.op("act", lambda e: e.activation(ln2, ssq2, AF.Ln, bias=EPS, scale=1.0 / D), r=[pc + "s0", pc + "s1"], w=[pc + "l"])
                P.op("act", lambda e: e.activation(rs2, ln2, AF.Exp, scale=-0.5), r=[pc + "l"], w=[pc + "r"])
                yield
                yield
                for tl in range(2):
                    tt = g * 2 + tl
                    P.op("dve", lambda e, tl=tl: e.scalar_tensor_tensor(out=hss[tl], in0=hss[tl], scalar=rs2[:, tl:tl + 1], in1=gfb,
                                                                        op0=ALU.mult, op1=ALU.mult),
                         r=[hcs[tl], pc + "r", "gfb"], w=[hcs[tl]])
                    P.op("sp", lambda e, tl=tl, tt=tt: e.dma_start(out=out[tt * 128:(tt + 1) * 128, :], in_=hss[tl]),
                         r=[hcs[tl]], w=["out"], key=hcs[tl])

            class Stream:
                def __init__(self, items):
                    self.items = list(items)
                    self.active = []

                def pending(self):
                    return sum(1 for x_ in self.items if x_ != "FENCE")

                def busy(self):
                    return bool(self.items or self.active)

                def step(self, nstart):
                    for gen in list(self.active):
                        try:
                            next(gen)
                        except StopIteration:
                            self.active.remove(gen)
                    started = 0
                    while self.items and started < nstart:
                        if self.items[0] == "FENCE":
                            if self.active:
                                break
                            self.items.pop(0)
                            continue
                        gen = self.items.pop(0)()
                        try:
                            next(gen)
                            self.active.append(gen)
                        except StopIteration:
                            pass
                        started += 1

                def drain(self, tag=""):
                    n0 = self.pending(); a0 = len(self.active); k = 0
                    while self.busy():
                        self.step(4); k += 1
                    if debug and (n0 or a0):
                        print(f"drain {tag}: pending={n0} active={a0} steps={k}")

            ge_streams = []
            Stream(prep_topk(0)).drain()
            Stream(prep_G(0, 0)).drain()
            for g in range(NGE):
                sB = Stream(prep_G(g, 1))
                sT = Stream(prep_topk(g + 1) if g + 1 < NGE else [])
                sA = Stream(prep_G(g + 1, 0) if g + 1 < NGE else [])
                sE = ge_streams.pop(0) if ge_streams else None
                pend = {0: emit_U(g, 0), 1: emit_U(g, 1)}
                for i in range(NB):
                    ii = i % HB
                    ep_busy = sE is not None and sE.busy()
                    if ep_busy:
                        sE.step(0)
                    if i < HB:
                        left = max(1, (HB - 14) - ii)
                        sB.step((sB.pending() + left - 1) // left if sB.pending() else 0)
                        if not ep_busy:
                            sT.step(2)
                    else:
                        if i == HB:
                            sT.drain(f'g{g} sT@HB')
                        left = max(1, (HB - 8) - ii)
                        sA.step((sA.pending() + left - 1) // left if sA.pending() else 0)
                    if i + 2 < NB:
                        if (i + 2) == HB:
                            sB.drain(f'g{g} sB@62')
                        pend[i + 2] = emit_U(g, i + 2)
                    emit_V(i, pend.pop(i))
                sB.drain(f'g{g} sB@end')
                sT.drain(f'g{g} sT@end')
                sA.drain(f'g{g} sA@end')
                if sE is not None:
                    sE.drain()
                sEn = Stream([lambda g=g: group_end(g)])
                sEn.step(1)
                ge_streams.append(sEn)
            ge_streams[0].drain()
        P.barrier()

        P.finalize()
        sems = {}
        for en in Prog.ENGS:
            sems[("eng", en)] = es.enter_context(nc.semaphore(f"sem_{en}"))
        for i, k in enumerate(sorted(P.dma_count.keys())):
            sems[("dma", k)] = es.enter_context(nc.semaphore(f"semd_{i}"))
        block = es.enter_context(nc.Block())

        @block.tensor
        def _(e):
            P.emit("pe", e, sems)

        @block.scalar
        def _(e):
            P.emit("act", e, sems)

        @block.vector
        def _(e):
            P.emit("dve", e, sems)

        @block.gpsimd
        def _(e):
            P.emit("pool", e, sems)

        @block.sync
        def _(e):
            P.emit("sp", e, sems)

    mybir.codegen_inst_isa_subclasses(nc)
    return nc


_NC_CACHE = {}


def _prep_inputs(inputs):
    f = lambda a: np.ascontiguousarray(np.asarray(a, dtype=np.float32))
    shared = {
        "norm1_g": f(inputs["norm1_g"]).reshape(D),
        "w_in": f(inputs["w_in"]).reshape(D, 8192),
        "lambda_qk": f(inputs["lambda_qk"]).reshape(256),
        "subln_g": f(inputs["subln_g"]).reshape(128),
        "conv_w": f(inputs["conv_w"]).reshape(3, D),
        "w_attn_o": f(inputs["w_attn_o"]).reshape(D, D),
        "w_conv_o": f(inputs["w_conv_o"]).reshape(D, D),
        "w_out": f(inputs["w_out"]).reshape(D, D),
        "norm2_g": f(inputs["norm2_g"]).reshape(D),
        "w_query": f(inputs["w_query"]).reshape(D, 2048),
        "sub_keys": f(inputs["sub_keys"]).reshape(8, 2, 128, 128),
        "expert_u": f(inputs["expert_u"]).reshape(16384, D),
        "expert_v": f(inputs["expert_v"]).reshape(16384, D),
        "final_g": f(inputs["final_g"]).reshape(D),
    }
    xs = f(inputs["x"])
    in_maps = []
    for b in range(8):
        m = dict(shared)
        m["x"] = np.ascontiguousarray(xs[b])
        in_maps.append(m)
    return in_maps


def kernel(**inputs):
    if "nc" not in _NC_CACHE:
        _NC_CACHE["nc"] = build_nc()
    nc = _NC_CACHE["nc"]
    in_maps = _prep_inputs(inputs)
    res = run_bass_kernel_spmd(nc, in_maps, core_ids=list(range(8)))
    outs = [np.asarray(r["out"], dtype=np.float32).reshape(S, D) for r in res.results]
    return np.stack(outs, axis=0)
```
